# Optimizing a Trainium2 kernel written in Bass

```python
import math
import jax, jax.numpy as jnp
from jax import lax
import numpy as np

D_MODEL = 1024
BATCH = 8
SEQ = 4096
DEPTH = 4

D_MIX = D_MODEL
HEAD_DIM = 64
N_Q_HEADS = 8
N_KV_HEADS = 2
GQA_GROUP = N_Q_HEADS // N_KV_HEADS
D_ATTN = N_Q_HEADS * HEAD_DIM
D_KV = N_KV_HEADS * HEAD_DIM
WINDOW = 128
ATT_BLOCK = 128
N_BUCKETS = 32
MAX_DISTANCE = 128
D_CONV = D_MIX // 4
CONV_WIDTH = 3
D_SGU = D_MIX // 4
SGU_GROUPS = 4
SGU_GROUP_DIM = D_SGU // SGU_GROUPS
SGU_CHUNK = 128
IN_SPLITS = (D_ATTN, D_KV, D_KV, D_CONV, D_CONV, D_CONV, D_SGU, D_SGU)
D_IN = D_ATTN + 2 * D_KV + 3 * D_CONV + 2 * D_SGU
PEER_HEADS = 8
PEER_TOPK = 16
N_KEYS = 128
N_EXPERTS = N_KEYS * N_KEYS
PEER_QDIM = 256
PEER_HALF = PEER_QDIM // 2
PEER_TOKEN_BLOCK = 128
EPS = 1e-6
NEG_INF = -1e30

kernel_name = "hybrid_conv_swa_sgu_peer_adaln"


def rms_norm(x, g):
    xf = x.astype(jnp.float32)
    y = xf * lax.rsqrt(jnp.mean(xf * xf, axis=-1, keepdims=True) + EPS)
    return (y * g.astype(jnp.float32)).astype(x.dtype)


def t5_causal_bucket(dist):
    max_exact = N_BUCKETS // 2
    d = jnp.maximum(dist, 0)
    log_ratio = jnp.log(jnp.maximum(d, 1).astype(jnp.float32) / max_exact) / math.log(MAX_DISTANCE / max_exact)
    large = max_exact + (log_ratio * (N_BUCKETS - max_exact)).astype(jnp.int32)
    large = jnp.minimum(large, N_BUCKETS - 1)
    return jnp.where(d < max_exact, d, large)


def sliding_window_attention(q, k, v, sink, rel_bias):
    B, S = q.shape[0], q.shape[1]
    nb = S // ATT_BLOCK
    qb = q.reshape(B, nb, ATT_BLOCK, N_KV_HEADS, GQA_GROUP, HEAD_DIM)
    kb = k.reshape(B, nb, ATT_BLOCK, N_KV_HEADS, HEAD_DIM)
    vb = v.reshape(B, nb, ATT_BLOCK, N_KV_HEADS, HEAD_DIM)

    def with_prev(t):
        prev = jnp.pad(t[:, :-1], ((0, 0), (1, 0), (0, 0), (0, 0), (0, 0)))
        return jnp.concatenate([prev, t], axis=2)

    kw, vw = with_prev(kb), with_prev(vb)
    logits = jnp.einsum('bnqkgd,bnskd->bnkgqs', qb, kw,
                        preferred_element_type=jnp.float32) * (HEAD_DIM ** -0.5)
    q_idx = jnp.arange(ATT_BLOCK)[:, None]
    s_idx = jnp.arange(2 * ATT_BLOCK)[None, :]
    dist = q_idx + ATT_BLOCK - s_idx
    in_window = (dist >= 0) & (dist < WINDOW)
    blk = jnp.arange(nb)[:, None, None]
    key_exists = (blk * ATT_BLOCK + s_idx[None] - ATT_BLOCK) >= 0
    mask = in_window[None] & key_exists
    bias = rel_bias.astype(jnp.float32)[t5_causal_bucket(dist)]
    bias = bias.transpose(2, 0, 1).reshape(N_KV_HEADS, GQA_GROUP, ATT_BLOCK, 2 * ATT_BLOCK)
    logits = jnp.where(mask[None, :, None, None], logits + bias, NEG_INF)
    sink_l = sink.astype(jnp.float32).reshape(1, 1, N_KV_HEADS, GQA_GROUP, 1, 1)
    m = jnp.maximum(jnp.max(logits, axis=-1, keepdims=True), sink_l)
    p = jnp.exp(logits - m)
    p = p / (jnp.sum(p, axis=-1, keepdims=True) + jnp.exp(sink_l - m))
    out = jnp.einsum('bnkgqs,bnskd->bnqkgd', p.astype(v.dtype), vw)
    return out.reshape(B, S, D_ATTN)


def short_conv_mixer(b_gate, c_gate, h, conv_w):
    S = h.shape[1]
    z = c_gate * h
    zp = jnp.pad(z, ((0, 0), (CONV_WIDTH - 1, 0), (0, 0)))
    conv = zp[:, 0:S] * conv_w[0]
    for j in range(1, CONV_WIDTH):
        conv = conv + zp[:, j:j + S] * conv_w[j]
    return b_gate * conv


def chunked_spatial_gating(u, v, w_s, b_s):
    B, S = u.shape[0], u.shape[1]
    nc = S // SGU_CHUNK
    vf = v.astype(jnp.float32).reshape(B, S, SGU_GROUPS, SGU_GROUP_DIM)
    mu = jnp.mean(vf, axis=-1, keepdims=True)
    var = jnp.mean(jnp.square(vf - mu), axis=-1, keepdims=True)
    vn = ((vf - mu) * lax.rsqrt(var + EPS)).astype(v.dtype)
    vn = vn.reshape(B, nc, SGU_CHUNK, SGU_GROUPS, SGU_GROUP_DIM)
    causal = jnp.tril(jnp.ones((SGU_CHUNK, SGU_CHUNK), dtype=w_s.dtype))
    w = w_s * causal[None]
    mixed = jnp.einsum('gts,bnsgc->bntgc', w, vn) + b_s.T[None, None, :, :, None]
    return u * mixed.reshape(B, S, D_SGU)


def peer_ffn(h, w_pq, sub_keys, expert_down, expert_up):
    B, S, D = h.shape
    tokens = h.reshape(-1, PEER_TOKEN_BLOCK, D)

    def block(t):
        T = t.shape[0]
        q = (t @ w_pq).reshape(T, PEER_HEADS, 2, PEER_HALF)
        s = jnp.einsum('thpk,hpnk->thpn', q, sub_keys, preferred_element_type=jnp.float32)
        top_s, top_i = lax.top_k(s, PEER_TOPK)
        cand_s = top_s[:, :, 0, :, None] + top_s[:, :, 1, None, :]
        cand_i = top_i[:, :, 0, :, None] * N_KEYS + top_i[:, :, 1, None, :]
        best_s, best_pos = lax.top_k(cand_s.reshape(T, PEER_HEADS, PEER_TOPK * PEER_TOPK), PEER_TOPK)
        experts = jnp.take_along_axis(cand_i.reshape(T, PEER_HEADS, PEER_TOPK * PEER_TOPK), best_pos, axis=-1)
        gate = jax.nn.softmax(best_s, axis=-1)
        u = expert_down[experts]
        act = jax.nn.gelu(jnp.einsum('td,thkd->thk', t, u, preferred_element_type=jnp.float32),
                          approximate=False)
        wgt = (gate * act).astype(t.dtype)
        vv = expert_up[experts]
        return jnp.einsum('thk,thkd->td', wgt, vv)

    return lax.map(block, tokens).reshape(B, S, D)


def hybrid_layer(x, c_act, rel_bias, w_ada, b_ada, norm1_g, norm2_g, w_in, q_norm_g, k_norm_g,
                 attn_sink, conv_w, sgu_w, sgu_b, out_norm_g, w_out, peer_wq, peer_sub_keys,
                 peer_down, peer_up):
    B, S = x.shape[0], x.shape[1]
    mod = (c_act @ w_ada + b_ada)[:, None, :]
    sh1, sc1, g1, sh2, sc2, g2 = jnp.split(mod, 6, axis=-1)
    h = rms_norm(x, norm1_g) * (1 + sc1) + sh1
    proj = h @ w_in
    split_pts = [int(i) for i in np.cumsum(IN_SPLITS)[:-1]]
    q, k, v, cb, cc, ch, su, sv = jnp.split(proj, split_pts, axis=-1)
    q = rms_norm(q.reshape(B, S, N_Q_HEADS, HEAD_DIM), q_norm_g)
    k = rms_norm(k.reshape(B, S, N_KV_HEADS, HEAD_DIM), k_norm_g)
    v = v.reshape(B, S, N_KV_HEADS, HEAD_DIM)
    y_attn = sliding_window_attention(q, k, v, attn_sink, rel_bias)
    y_conv = short_conv_mixer(cb, cc, ch, conv_w)
    y_sgu = chunked_spatial_gating(su, sv, sgu_w, sgu_b)
    ga, gc, gs = jnp.split(out_norm_g, [D_ATTN, D_ATTN + D_CONV])
    merged = jnp.concatenate([rms_norm(y_attn, ga), rms_norm(y_conv, gc), rms_norm(y_sgu, gs)], axis=-1)
    x = x + g1 * (merged @ w_out)
    h2 = rms_norm(x, norm2_g) * (1 + sc2) + sh2
    x = x + g2 * peer_ffn(h2, peer_wq, peer_sub_keys, peer_down, peer_up)
    return x


def setup_inputs(seed: int = 0) -> dict:
    key = jax.random.key(seed)
    ks = jax.random.split(key, 20)
    f32 = jnp.float32
    nrm = lambda k, shape, s: jax.random.normal(k, shape, f32) * s
    return {
        "x": nrm(ks[0], (BATCH, SEQ, D_MODEL), 1.0),
        "c": nrm(ks[1], (BATCH, D_MODEL), 1.0),
        "rel_bias": nrm(ks[2], (N_BUCKETS, N_Q_HEADS), 0.5),
        "w_ada": nrm(ks[3], (DEPTH, D_MODEL, 6 * D_MODEL), 0.5 * D_MODEL ** -0.5),
        "b_ada": nrm(ks[4], (DEPTH, 6 * D_MODEL), 0.02),
        "norm1_g": 1.0 + nrm(ks[5], (DEPTH, D_MODEL), 0.02),
        "norm2_g": 1.0 + nrm(ks[6], (DEPTH, D_MODEL), 0.02),
        "w_in": nrm(ks[7], (DEPTH, D_MODEL, D_IN), D_MODEL ** -0.5),
        "q_norm_g": 1.0 + nrm(ks[8], (DEPTH, HEAD_DIM), 0.02),
        "k_norm_g": 1.0 + nrm(ks[9], (DEPTH, HEAD_DIM), 0.02),
        "attn_sink": nrm(ks[10], (DEPTH, N_Q_HEADS), 0.5),
        "conv_w": nrm(ks[11], (DEPTH, CONV_WIDTH, D_CONV), CONV_WIDTH ** -0.5),
        "sgu_w": nrm(ks[12], (DEPTH, SGU_GROUPS, SGU_CHUNK, SGU_CHUNK), SGU_CHUNK ** -0.5),
        "sgu_b": 1.0 + nrm(ks[13], (DEPTH, SGU_GROUPS, SGU_CHUNK), 0.02),
        "out_norm_g": 1.0 + nrm(ks[14], (DEPTH, D_MIX), 0.02),
        "w_out": nrm(ks[15], (DEPTH, D_MIX, D_MODEL), D_MIX ** -0.5),
        "peer_wq": nrm(ks[16], (DEPTH, D_MODEL, PEER_HEADS * PEER_QDIM), D_MODEL ** -0.5),
        "peer_sub_keys": nrm(ks[17], (DEPTH, PEER_HEADS, 2, N_KEYS, PEER_HALF), PEER_HALF ** -0.5),
        "peer_down": nrm(ks[18], (DEPTH, N_EXPERTS, D_MODEL), D_MODEL ** -0.5),
        "peer_up": nrm(ks[19], (DEPTH, N_EXPERTS, D_MODEL), 0.5 * PEER_HEADS ** -0.5),
    }


def reference(x, c, rel_bias, w_ada, b_ada, norm1_g, norm2_g, w_in, q_norm_g, k_norm_g, attn_sink,
              conv_w, sgu_w, sgu_b, out_norm_g, w_out, peer_wq, peer_sub_keys, peer_down, peer_up):
    c_act = jax.nn.silu(c)
    for l in range(DEPTH):
        x = hybrid_layer(x, c_act, rel_bias, w_ada[l], b_ada[l], norm1_g[l], norm2_g[l], w_in[l],
                         q_norm_g[l], k_norm_g[l], attn_sink[l], conv_w[l], sgu_w[l], sgu_b[l],
                         out_norm_g[l], w_out[l], peer_wq[l], peer_sub_keys[l], peer_down[l], peer_up[l])
    return x
```

```python
import math
from contextlib import ExitStack
import numpy as np
import concourse.bass as bass
import concourse.mybir as mybir
from concourse.bass_utils import run_bass_kernel_spmd

F32 = mybir.dt.float32
BF16 = mybir.dt.bfloat16
U32 = mybir.dt.uint32
ALU = mybir.AluOpType
AF = mybir.ActivationFunctionType
AX = mybir.AxisListType

D = 1024
DEPTH = 4
SEQ = 4096
EPS = 1e-6
NSLOT = 8
SAME_ENGINE_SYNC = True
DEBUG_MAXOPS = None
DEBUG_SKIP = set()
DEBUG_SMALL_TABLES = False

C_ID = 0
C_S1 = 128
C_S2 = 256
C_S1P = 384
C_S2P = 512
C_TRIL = 640
C_MPREV = 768
C_MCUR = 896
C_IOTA = 1024
C_SEGL = 1040
NCST = 1048


class Prog:
    ENG = ("pe", "act", "dve", "pool", "sp")

    def __init__(self):
        self.ops = {e: [] for e in self.ENG}
        self.cnt = {e: 0 for e in self.ENG}
        self.known = {e: {} for e in self.ENG}
        self.tags = {}
        self.dcnt = {}

    def _dep(self, eng, tok, raw):
        k, v = tok
        if k == eng:
            if eng == "pe" or not SAME_ENGINE_SYNC:
                return
        if self.known[eng].get(k, 0) >= v:
            return
        self.known[eng][k] = v
        self.ops[eng].append(("w", k, v))

    def op(self, eng, fn, r=(), w=(), dma=None):
        self.nops = getattr(self, 'nops', 0) + 1
        if DEBUG_MAXOPS is not None and self.nops > DEBUG_MAXOPS:
            return
        if self.nops in DEBUG_SKIP:
            return
        for t in r:
            st = self.tags.get(t)
            if st and st["w"]:
                self._dep(eng, st["w"], True)
        for t in w:
            st = self.tags.get(t)
            if st:
                if st["w"]:
                    self._dep(eng, st["w"], False)
                for k, v in st["r"].items():
                    self._dep(eng, (k, v), False)
        if dma is None:
            self.cnt[eng] += 1
            tok = (eng, self.cnt[eng])
        else:
            key = ("dma", dma)
            self.dcnt[key] = self.dcnt.get(key, 0) + 16
            tok = (key, self.dcnt[key])
        self.ops[eng].append(("i", fn, tok))
        for t in w:
            self.tags[t] = {"w": tok, "r": {}}
        for t in r:
            st = self.tags.setdefault(t, {"w": None, "r": {}})
            st["r"][tok[0]] = max(st["r"].get(tok[0], 0), tok[1])

    def barrier(self):
        for e in self.ENG:
            for f in self.ENG:
                if f != e and self.cnt[f] > 0:
                    self._dep(e, (f, self.cnt[f]), True)
            for key, v in self.dcnt.items():
                self._dep(e, (key, v), True)


ARENA_LOG = []


class Arena:
    def __init__(self, ap_f32, nwords):
        self.ap = ap_f32
        self.n = nwords
        self.off = 0

    def reset(self):
        self.off = 0

    def alloc(self, free_shape, dtype):
        nel = int(np.prod(free_shape))
        esz = 2 if dtype == BF16 else 4
        nw = (nel * esz + 3) // 4
        nw = (nw + 7) // 8 * 8
        assert self.off + nw <= self.n, ("arena overflow", self.off, nw, self.n)
        v = self.ap[:, self.off:self.off + nw]
        ARENA_LOG.append((self.off, nel, dtype, tuple(free_shape)))
        self.off += nw
        if dtype != F32:
            v = v.bitcast(dtype)
        v = v[:, 0:nel]
        if len(free_shape) == 2:
            v = v.rearrange("p (a b) -> p a b", a=free_shape[0])
        elif len(free_shape) == 3:
            v = v.rearrange("p (a b c) -> p a b c", a=free_shape[0], b=free_shape[1])
        return v


def bc(ap, pos, n):
    u = ap.unsqueeze(pos)
    shp = list(u.shape)
    shp[pos] = n
    return u.broadcast_to(shp)


def build_program(nt, depth, only_a=False):
    S = nt * 128
    nc = bass.Bass("TRN2", target_bir_lowering=False)

    def din(name, shape, dt=F32):
        return nc.dram_tensor(name, list(shape), dt, kind="ExternalInput").ap()

    x_in = din("x", [S, D])
    cT_in = din("cT", [128, 8])
    cst_in = din("cst", [128, NCST])
    braw_in = din("braw", [128, 2 * 8 * 128])
    w_ada = din("w_ada", [depth, D, 6 * D])
    badaT = din("badaT", [depth, 128, 48])
    badaG = din("badaG", [depth, 128, 2 * D])
    n1T = din("n1T", [depth, 128, 8])
    n2T = din("n2T", [depth, 128, 8])
    ongT = din("ongT", [depth, 128, 8])
    w_in = din("w_in", [depth, D, 2048])
    w_out = din("w_out", [depth, D, D])
    wq = din("wq", [depth, D, 2048])
    qgB = din("qgB", [depth, 128, 64])
    kgB = din("kgB", [depth, 128, 64])
    sinkB = din("sinkB", [depth, 128, 8])
    convB = din("convB", [depth, 128, 768])
    sgu_w = din("sgu_w", [depth, 4, 128, 128])
    sgubT = din("sgubT", [depth, 128, 4])
    subk = din("subk", [depth, 16, 128, 128])
    NE = 128 if DEBUG_SMALL_TABLES else 16384
    pdown = [din("pdown%d" % i, [NE, D]) for i in range(depth)]
    pup = [din("pup%d" % i, [NE, D]) for i in range(depth)]
    y_out = nc.dram_tensor("y", [S, D], F32, kind="ExternalOutput").ap()
    pdn_bf = nc.dram_tensor("pdn_bf", [NE, D], BF16).ap()
    pup_bf = nc.dram_tensor("pup_bf", [NE, D], BF16).ap()

    P = Prog()
    es = ExitStack()
    with es:
        def sb(name, shape, dt=F32):
            return es.enter_context(nc.sbuf_tensor(name, list(shape), dt))

        def ps(name):
            return es.enter_context(nc.psum_tensor(name, [128, 512], F32))

        R1W = 12288
        r1 = sb("r1", [128, R1W])
        GW = NSLOT * 512
        gdn2 = sb("gdn", [128, NSLOT * D], BF16)
        gup2 = sb("gup", [128, NSLOT * D], BF16)
        gdn = gdn2[:].rearrange("p (s d) -> p s d", s=NSLOT)
        gup = gup2[:].rearrange("p (s d) -> p s d", s=NSLOT)
        stg = [gdn2[:].bitcast(F32)[:, 0:2048], gup2[:].bitcast(F32)[:, 0:2048]]
        cst = sb("cst_sb", [128, NCST])
        identb = sb("identb", [128, 128], BF16)
        biasT = sb("biasT", [128, 2, 8, 128])
        R3W = 16 * 1024
        r3 = sb("r3", [128, R3W])
        cact = sb("cact", [128, 8])
        cbc = sb("cbc", [128, 8, 128])
        modT = sb("modT", [128, 4, 8])
        bT = sb("bT", [128, 48])
        nT = sb("nT", [128, 3, 8])
        sT = sb("sT", [128, 2, 8])
        gB = sb("gB", [128, 2, D])
        qg8 = sb("qg8", [128, 64])
        kgb = sb("kgb", [128, 64])
        esink = sb("esink", [128, 8])
        cwb = sb("cwb", [128, 3, 256])
        swT = sb("swT", [128, 4, 128], BF16)
        sgb = sb("sgb", [128, 4])
        xbuf = [sb("xbuf0", [128, D]), sb("xbuf1", [128, D])]
        kTb = [sb("kT0", [64, 2, 128], BF16), sb("kT1", [64, 2, 128], BF16)]
        vaug = [sb("va0", [128, 2, 80], BF16), sb("va1", [128, 2, 80], BF16)]
        zb = [sb("z0", [128, 256]), sb("z1", [128, 256])]
        pb = [ps("pb%d" % i) for i in range(8)]
        pb0b = pb[0][:].bitcast(BF16)

        w_in_sb = r1[:, 0:8192].bitcast(BF16).rearrange("p (k n) -> p k n", k=8)
        w_out_sb = r1[:, 8192:12288].bitcast(BF16).rearrange("p (k n) -> p k n", k=8)
        wq_sb = w_in_sb
        skT_sb = r1[:, 8192:9216].bitcast(BF16).rearrange("p (j n) -> p j n", j=16)

        ar = Arena(r3[:], R3W)

        def dve(fn, r=(), w=()):
            P.op("dve", fn, r, w)

        def act(fn, r=(), w=()):
            P.op("act", fn, r, w)

        def pe(fn, r=(), w=()):
            P.op("pe", fn, r, w)

        def dma(eng, out, in_, r=(), w=(), key=None):
            P.op(eng, lambda e, out=out, in_=in_: e.dma_start(out=out, in_=in_), r, w, dma=key)

        def rstd_from_ss(ss, out, scale, n, tagss, tagout, tmp, tagtmp):
            dve(lambda e: e.tensor_scalar(out=tmp, in0=ss, scalar1=scale, scalar2=EPS, op0=ALU.mult, op1=ALU.add),
                r=[tagss], w=[tagtmp])
            act(lambda e: e.activation(out=tmp, in_=tmp, func=AF.Sqrt), r=[tagtmp], w=[tagtmp])
            dve(lambda e: e.reciprocal(out=out, in_=tmp), r=[tagtmp], w=[tagout])

        dma("sp", cst[:], cst_in, w=["cst"], key="c0")
        dma("sp", biasT[:].rearrange("p a h q -> p (a h q)"), braw_in, w=["biasT"], key="c1")
        dma("sp", cact[:], cT_in, w=["cact"], key="c2")
        dve(lambda e: e.tensor_copy(out=identb[:], in_=cst[:, C_ID:C_ID + 128]), r=["cst"], w=["identb"])
        for a, cm in ((0, C_MPREV), (1, C_MCUR)):
            dve(lambda e, a=a, cm=cm: e.tensor_tensor(out=biasT[:, a], in0=biasT[:, a],
                                                       in1=bc(cst[:, cm:cm + 128], 1, 8), op=ALU.add),
                r=["cst", "biasT"], w=["biasT"])
        act(lambda e: e.activation(out=cact[:], in_=cact[:], func=AF.Silu), r=["cact"], w=["cact"])
        dve(lambda e: e.tensor_copy(out=cbc[:], in_=bc(cact[:], 2, 128)), r=["cact"], w=["cbc"])
        for par in range(2):
            dve(lambda e, par=par: e.memset(vaug[par][:], 1.0), w=[("va", par)])

        ident32 = cst[:, C_ID:C_ID + 128]

        def load_convert(dst_fn, src_rows_fn, nrows_chunks, ncols, scale_fn=None, wtag=None):
            i = 0
            for k in range(nrows_chunks):
                for c0 in range(0, ncols, 2048):
                    c1 = min(ncols, c0 + 2048)
                    s = stg[i % 2]
                    tg = ("stg", i % 2)
                    dma("sp", s[:, 0:c1 - c0], src_rows_fn(k)[:, c0:c1], w=[tg], key=("stg", i % 2))
                    eng = "dve" if i % 2 == 0 else "act"
                    if scale_fn is None:
                        if eng == "dve":
                            dve(lambda e, s=s, k=k, c0=c0, c1=c1: e.tensor_copy(out=dst_fn(k, c0, c1), in_=s[:, 0:c1 - c0]),
                                r=[tg], w=[wtag])
                        else:
                            act(lambda e, s=s, k=k, c0=c0, c1=c1: e.activation(out=dst_fn(k, c0, c1), in_=s[:, 0:c1 - c0], func=AF.Copy),
                                r=[tg], w=[wtag])
                    else:
                        dve(lambda e, s=s, k=k, c0=c0, c1=c1: e.tensor_scalar(out=dst_fn(k, c0, c1), in0=s[:, 0:c1 - c0],
                                                                              scalar1=scale_fn(k), scalar2=None, op0=ALU.mult),
                            r=[tg, "nT"], w=[wtag])
                    i += 1

        def layer_params(l):
            dma("sp", bT[:], badaT[l], w=["bT"], key="p0")
            dma("sp", nT[:, 0, :], n1T[l], w=["nT"], key="p1")
            dma("sp", nT[:, 1, :], n2T[l], w=["nT"], key="p1")
            dma("sp", nT[:, 2, :], ongT[l], w=["nT"], key="p1")
            dma("sp", gB[:].rearrange("p a d -> p (a d)"), badaG[l], w=["gB"], key="p2")
            dma("sp", qg8[:], qgB[l], w=["qg8"], key="p3")
            dma("sp", kgb[:], kgB[l], w=["kgb"], key="p4")
            dma("sp", esink[:], sinkB[l], w=["esink"], key="p5")
            dma("sp", cwb[:].rearrange("p a c -> p (a c)"), convB[l], w=["cwb"], key="p6")
            dma("sp", sgb[:], sgubT[l], w=["sgb"], key="p7")
            dve(lambda e: e.tensor_scalar(out=qg8[:], in0=qg8[:], scalar1=0.125, scalar2=None, op0=ALU.mult),
                r=["qg8"], w=["qg8"])
            act(lambda e: e.activation(out=esink[:], in_=esink[:], func=AF.Exp), r=["esink"], w=["esink"])
            fm_slot = {0: 0, 1: 1, 3: 2, 4: 3}
            for ci in range(24):
                seg = ci // 4
                s = stg[ci % 2]
                tg = ("stg", ci % 2)
                sv = s.rearrange("p (k n) -> p k n", k=8)
                dma("sp", sv, w_ada[l, :, ci * 256:(ci + 1) * 256].rearrange("(k p) n -> p k n", p=128),
                    w=[tg], key=("stg", ci % 2))
                if seg in fm_slot:
                    for half in range(2):
                        col = (ci % 4) * 2 + half
                        for k in range(8):
                            pe(lambda e, sv=sv, k=k, half=half: e.matmul(
                                pb[7][:, 0:1], lhsT=sv[:, k, half * 128:(half + 1) * 128], rhs=cact[:, k:k + 1],
                                start=(k == 0), stop=(k == 7)), r=[tg, "cact"], w=["pb7"])
                        dve(lambda e, seg=seg, col=col: e.tensor_tensor(
                            out=modT[:, fm_slot[seg], col:col + 1], in0=pb[7][:, 0:1],
                            in1=bT[:, seg * 8 + col:seg * 8 + col + 1], op=ALU.add),
                            r=["pb7", "bT"], w=["modT"])
                else:
                    gi = 0 if seg == 2 else 1
                    cc0 = (ci % 4) * 256
                    for k in range(8):
                        pe(lambda e, sv=sv, k=k: e.matmul(pb[6][:, 0:256], lhsT=cbc[:, k, :], rhs=sv[:, k, :],
                                                          start=(k == 0), stop=(k == 7)), r=[tg, "cbc"], w=["pb6"])
                    dve(lambda e, gi=gi, cc0=cc0: e.tensor_tensor(out=gB[:, gi, cc0:cc0 + 256], in0=pb[6][:, 0:256],
                                                                   in1=gB[:, gi, cc0:cc0 + 256], op=ALU.add),
                        r=["pb6", "gB"], w=["gB"])
            for j, slot in ((0, 1), (1, 3)):
                dve(lambda e, j=j, slot=slot: e.scalar_tensor_tensor(out=sT[:, j, :], in0=modT[:, slot, :], scalar=1.0,
                                                                     in1=nT[:, j, :], op0=ALU.add, op1=ALU.mult),
                    r=["modT", "nT"], w=["sT"])

        def phaseA_weights(l):
            load_convert(lambda k, c0, c1: w_in_sb[:, k, c0:c1],
                         lambda k: w_in[l, k * 128:(k + 1) * 128, :], 8, 2048, wtag="w_in")
            load_convert(lambda k, c0, c1: w_out_sb[:, k, c0:c1],
                         lambda k: w_out[l, k * 128:(k + 1) * 128, :], 8, 1024,
                         scale_fn=lambda k: nT[:, 2, k:k + 1], wtag="w_out")
            s = stg[0]
            sv = s[:, 0:512].rearrange("p (g s) -> p g s", g=4)
            dma("sp", sv, sgu_w[l].rearrange("g t s -> t g s"), w=[("stg", 0)], key=("stg", 0))
            dve(lambda e: e.tensor_tensor(out=sv, in0=sv, in1=bc(cst[:, C_TRIL:C_TRIL + 128], 1, 4), op=ALU.mult),
                r=[("stg", 0), "cst"], w=[("stg", 0)])
            for g in range(4):
                pe(lambda e, g=g: e.transpose(out=pb[7][:, g * 128:(g + 1) * 128], in_=sv[:, g, :], identity=ident32),
                   r=[("stg", 0), "cst"], w=["pb7"])
            dve(lambda e: e.tensor_copy(out=swT[:].rearrange("p g t -> p (g t)"), in_=pb[7][:, 0:512]),
                r=["pb7"], w=["swT"])

        def phaseB_weights(l):
            load_convert(lambda k, c0, c1: wq_sb[:, k, c0:c1],
                         lambda k: wq[l, k * 128:(k + 1) * 128, :], 8, 2048, wtag="wq")
            for jb in range(4):
                s = stg[jb % 2]
                tg = ("stg", jb % 2)
                sv = s[:, 0:512].rearrange("p (j k) -> p j k", j=4)
                dma("sp", sv, subk[l, jb * 4:(jb + 1) * 4].rearrange("j n k -> n j k"), w=[tg], key=("stg", jb % 2))
                for jj in range(4):
                    pe(lambda e, sv=sv, jj=jj: e.transpose(out=pb[7][:, jj * 128:(jj + 1) * 128], in_=sv[:, jj, :],
                                                           identity=ident32), r=[tg, "cst"], w=["pb7"])
                dve(lambda e, jb=jb: e.tensor_copy(out=skT_sb[:, jb * 4:(jb + 1) * 4, :].rearrange("p j n -> p (j n)"),
                                                   in_=pb[7][:, 0:512]), r=["pb7"], w=["skT"])

        def convert_tables(l):
            cbuf = [gdn2[:, 4096:6144], gup2[:, 4096:6144]]
            jobs = []
            for src_t, dst_t in ((pdown[l], pdn_bf), (pup[l], pup_bf)):
                for c in range(NE // 256):
                    jobs.append((src_t[c * 256:(c + 1) * 256, :].rearrange("(p r) d -> p (r d)", r=2),
                                 dst_t[c * 256:(c + 1) * 256, :].rearrange("(p r) d -> p (r d)", r=2)))

            def load(i):
                dma("sp", stg[i % 2], jobs[i][0], w=[("stg", i % 2)], key=("stg", i % 2))

            load(0)
            if len(jobs) > 1:
                load(1)
            for i in range(len(jobs)):
                b = i % 2
                if i % 2 == 0:
                    act(lambda e, b=b: e.activation(out=cbuf[b], in_=stg[b], func=AF.Copy), r=[("stg", b)], w=[("cb", b)])
                else:
                    dve(lambda e, b=b: e.tensor_copy(out=cbuf[b], in_=stg[b]), r=[("stg", b)], w=[("cb", b)])
                dma("sp", jobs[i][1], cbuf[b], r=[("cb", b)], w=[("tabrow", i)], key=("cbo", b))
                if i + 2 < len(jobs):
                    load(i + 2)

        def tile_front(n, src, which, A):
            xt = xbuf[n % 2][:]
            xtag = ("x", n % 2)
            dma("sp", xt, src[n * 128:(n + 1) * 128, :], w=[xtag], key=("x", n % 2))
            act(lambda e: e.activation(out=A["junkb"], in_=xt, func=AF.Square, accum_out=A["ss"][:, 0:1]),
                r=[xtag], w=["junkb", "ss"])
            rstd_from_ss(A["ss"][:, 0:1], A["rs"][:, 0:1], 1.0 / D, 1, "ss", "rs", A["ss"][:, 1:2], "ss1")
            act(lambda e: e.activation(out=A["xn"], in_=xt, func=AF.Copy, scale=A["rs"][:, 0:1]),
                r=[xtag, "rs"], w=["xn"])
            for k in range(8):
                pe(lambda e, k=k: e.transpose(out=pb0b[:, k * 128:(k + 1) * 128], in_=A["xn"][:, k * 128:(k + 1) * 128],
                                              identity=identb[:]), r=["xn", "identb"], w=["pb0"])
            sh_slot = 0 if which == 0 else 2
            for k in range(8):
                dve(lambda e, k=k: e.tensor_scalar(out=A["hT"][:, k, :], in0=pb0b[:, k * 128:(k + 1) * 128],
                                                   scalar1=sT[:, which, k:k + 1], scalar2=modT[:, sh_slot, k:k + 1],
                                                   op0=ALU.mult, op1=ALU.add),
                    r=["pb0", "sT", "modT"], w=["hT"])
            return xt, xtag

        def phaseA(l):
            ar.reset()
            A = {}
            A["junkb"] = ar.alloc([D], BF16)
            A["ss"] = ar.alloc([4], F32)
            A["rs"] = ar.alloc([4], F32)
            A["xn"] = ar.alloc([D], BF16)
            A["hT"] = ar.alloc([8, 128], BF16)
            prsb = ar.alloc([1280], F32)
            sq = ar.alloc([640], F32)
            ssq = ar.alloc([16], F32)
            rq = ar.alloc([16], F32)
            tq = ar.alloc([16], F32)
            qn = ar.alloc([640], BF16)
            qtmp = ar.alloc([640], F32)
            qT = ar.alloc([8, 128], BF16)
            lg = ar.alloc([512], F32)
            PT = ar.alloc([2, 8, 128], BF16)
            mf = ar.alloc([D], F32)
            mb = ar.alloc([D], BF16)
            mT = ar.alloc([8, 128], BF16)
            den = ar.alloc([8], F32)
            c1 = ar.alloc([256], F32)
            c2 = ar.alloc([256], F32)
            bst = ar.alloc([4, 6], F32)
            mv = ar.alloc([4, 2], F32)
            vr = ar.alloc([4], F32)
            vt = ar.alloc([4], F32)
            vc = ar.alloc([4, 64], F32)
            vn = ar.alloc([4, 64], BF16)
            s3 = ar.alloc([4], F32)
            r3s = ar.alloc([4], F32)
            t3 = ar.alloc([4], F32)
            og = ar.alloc([512], F32)
            src = x_in if l == 0 else y_out
            for n in range(nt):
                par = n % 2
                xt, xtag = tile_front(n, src, 0, A)
                for cc in range(4):
                    for k in range(8):
                        pe(lambda e, cc=cc, k=k: e.matmul(pb[1 + cc][:, :], lhsT=A["hT"][:, k, :],
                                                          rhs=w_in_sb[:, k, cc * 512:(cc + 1) * 512],
                                                          start=(k == 0), stop=(k == 7)),
                           r=["hT", "w_in"], w=["pb%d" % (1 + cc)])
                act(lambda e: e.activation(out=prsb[:, 0:256], in_=pb[2][:, 256:512], func=AF.Copy), r=["pb2"], w=["prsb0"])
                act(lambda e: e.activation(out=prsb[:, 256:768], in_=pb[3][:, :], func=AF.Copy), r=["pb3"], w=["prsb1"])
                act(lambda e: e.activation(out=prsb[:, 768:1280], in_=pb[4][:, :], func=AF.Copy), r=["pb4"], w=["prsb2"])
                for g in range(2):
                    act(lambda e, par=par, g=g: e.activation(out=vaug[par][:, g, 0:64],
                                                             in_=pb[2][:, 128 + g * 64:128 + (g + 1) * 64], func=AF.Copy),
                        r=["pb2"], w=[("va", par)])
                act(lambda e: e.activation(out=sq[:, 0:512], in_=pb[1][:, :], func=AF.Square), r=["pb1"], w=["sq"])
                act(lambda e: e.activation(out=sq[:, 512:640], in_=pb[2][:, 0:128], func=AF.Square), r=["pb2"], w=["sq"])
                dve(lambda e: e.tensor_reduce(out=ssq[:, 0:10], in_=sq[:, 0:640].rearrange("p (h d) -> p h d", d=64),
                                              axis=AX.X, op=ALU.add), r=["sq"], w=["ssq"])
                rstd_from_ss(ssq[:, 0:10], rq[:, 0:10], 1.0 / 64, 10, "ssq", "rq", tq[:, 0:10], "tq")
                dve(lambda e: e.tensor_tensor(out=qtmp[:, 0:512].rearrange("p (h d) -> p h d", d=64),
                                              in0=pb[1][:, :].rearrange("p (h d) -> p h d", d=64),
                                              in1=bc(rq[:, 0:8], 2, 64), op=ALU.mult), r=["pb1", "rq"], w=["qtmp"])
                dve(lambda e: e.tensor_tensor(out=qtmp[:, 512:640].rearrange("p (h d) -> p h d", d=64),
                                              in0=pb[2][:, 0:128].rearrange("p (h d) -> p h d", d=64),
                                              in1=bc(rq[:, 8:10], 2, 64), op=ALU.mult), r=["pb2", "rq"], w=["qtmp"])
                dve(lambda e: e.tensor_tensor(out=qn[:, 0:512].rearrange("p (h d) -> p h d", d=64),
                                              in0=qtmp[:, 0:512].rearrange("p (h d) -> p h d", d=64),
                                              in1=bc(qg8[:], 1, 8), op=ALU.mult), r=["qtmp", "qg8"], w=["qn"])
                dve(lambda e: e.tensor_tensor(out=qn[:, 512:640].rearrange("p (h d) -> p h d", d=64),
                                              in0=qtmp[:, 512:640].rearrange("p (h d) -> p h d", d=64),
                                              in1=bc(kgb[:], 1, 2), op=ALU.mult), r=["qtmp", "kgb"], w=["qn"])
                for h in range(8):
                    pe(lambda e, h=h: e.transpose(out=pb0b[0:64, h * 128:(h + 1) * 128], in_=qn[:, h * 64:(h + 1) * 64],
                                                  identity=identb[:]), r=["qn", "identb"], w=["pb0"])
                act(lambda e: e.activation(out=qT[0:64].rearrange("p h t -> p (h t)"), in_=pb0b[0:64, :], func=AF.Copy),
                    r=["pb0"], w=["qT"])
                for g in range(2):
                    pe(lambda e, g=g: e.transpose(out=pb0b[0:64, g * 128:(g + 1) * 128],
                                                  in_=qn[:, 512 + g * 64:512 + (g + 1) * 64], identity=identb[:]),
                       r=["qn", "identb"], w=["pb0"])
                act(lambda e, par=par: e.activation(out=kTb[par][:].rearrange("p g t -> p (g t)"), in_=pb0b[0:64, 0:256],
                                                    func=AF.Copy), r=["pb0"], w=[("kT", par)])
                whichs = (0, 1) if n > 0 else (1,)
                for g in range(2):
                    for a in whichs:
                        kp = par if a == 1 else 1 - par
                        bank = 1 + g * 2 + a
                        pe(lambda e, g=g, kp=kp, bank=bank: e.matmul(
                            pb[bank][:, :], lhsT=kTb[kp][:, g, :],
                            rhs=qT[0:64, 4 * g:4 * g + 4, :].rearrange("p h t -> p (h t)"), start=True, stop=True),
                            r=[("kT", kp), "qT"], w=["pb%d" % bank])
                        dve(lambda e, g=g, a=a, bank=bank: e.tensor_tensor(
                            out=lg[:, :], in0=pb[bank][:, :],
                            in1=biasT[:, a, 4 * g:4 * g + 4, :].rearrange("p h q -> p (h q)"), op=ALU.add),
                            r=["pb%d" % bank, "biasT"], w=["lg"])
                        act(lambda e, g=g, a=a: e.activation(
                            out=PT[:, a, 4 * g:4 * g + 4, :].rearrange("p h q -> p (h q)"), in_=lg[:, :], func=AF.Exp),
                            r=["lg"], w=[("PT", a, g)])
                for h in range(8):
                    g = h // 4
                    bank = 5 + g
                    o0 = (h % 4) * 65
                    for a in whichs:
                        kp = par if a == 1 else 1 - par
                        pe(lambda e, h=h, g=g, a=a, kp=kp, bank=bank, o0=o0, st=(a == whichs[0]): e.matmul(
                            pb[bank][:, o0:o0 + 65], lhsT=PT[:, a, h, :], rhs=vaug[kp][:, g, 0:65],
                            start=st, stop=(a == 1)),
                            r=[("PT", a, g), ("va", kp)], w=["pb%d" % bank])
                for g in range(2):
                    bank = 5 + g
                    pv = pb[bank][:, 0:260].rearrange("p (h c) -> p h c", c=65)
                    dve(lambda e, g=g, pv=pv: e.tensor_tensor(out=den[:, 4 * g:4 * g + 4], in0=pv[:, :, 64],
                                                              in1=esink[:, 4 * g:4 * g + 4], op=ALU.add),
                        r=["pb%d" % bank, "esink"], w=["den"])
                    dve(lambda e, g=g: e.reciprocal(out=den[:, 4 * g:4 * g + 4], in_=den[:, 4 * g:4 * g + 4]),
                        r=["den"], w=["den"])
                    dve(lambda e, g=g, pv=pv: e.tensor_tensor(
                        out=mf[:, g * 256:(g + 1) * 256].rearrange("p (h d) -> p h d", d=64), in0=pv[:, :, 0:64],
                        in1=bc(den[:, 4 * g:4 * g + 4], 2, 64), op=ALU.mult),
                        r=["pb%d" % bank, "den"], w=["mf0"])
                z = zb[par][:]
                zp = zb[1 - par][:]
                dve(lambda e, z=z: e.tensor_tensor(out=z, in0=prsb[:, 256:512], in1=prsb[:, 512:768], op=ALU.mult),
                    r=["prsb1"], w=[("z", par)])
                for j, (cs, cp) in enumerate(((C_S1, C_S1P), (C_S2, C_S2P))):
                    pe(lambda e, j=j, cs=cs, z=z, st=(n == 0): e.matmul(pb[7][:, j * 256:(j + 1) * 256], lhsT=cst[:, cs:cs + 128], rhs=z,
                                                           start=True, stop=st),
                       r=["cst", ("z", par)], w=["pb7"])
                    if n > 0:
                        pe(lambda e, j=j, cp=cp, zp=zp: e.matmul(pb[7][:, j * 256:(j + 1) * 256], lhsT=cst[:, cp:cp + 128],
                                                                 rhs=zp, start=False, stop=True),
                           r=["cst", ("z", 1 - par)], w=["pb7"])
                dve(lambda e: e.tensor_tensor(out=c1[:, :], in0=pb[7][:, 256:512], in1=cwb[:, 0, :], op=ALU.mult),
                    r=["pb7", "cwb"], w=["c1"])
                dve(lambda e: e.tensor_tensor(out=c2[:, :], in0=pb[7][:, 0:256], in1=cwb[:, 1, :], op=ALU.mult),
                    r=["pb7", "cwb"], w=["c2"])
                dve(lambda e: e.tensor_tensor(out=c1[:, :], in0=c1[:, :], in1=c2[:, :], op=ALU.add), r=["c1", "c2"], w=["c1"])
                dve(lambda e, z=z: e.tensor_tensor(out=c2[:, :], in0=z, in1=cwb[:, 2, :], op=ALU.mult),
                    r=[("z", par), "cwb", "c1"], w=["c2"])
                dve(lambda e: e.tensor_tensor(out=c1[:, :], in0=c1[:, :], in1=c2[:, :], op=ALU.add), r=["c1", "c2"], w=["c1"])
                dve(lambda e: e.tensor_tensor(out=mf[:, 512:768], in0=c1[:, :], in1=prsb[:, 0:256], op=ALU.mult),
                    r=["c1", "prsb0"], w=["mf1"])
                for g in range(4):
                    dve(lambda e, g=g: e.bn_stats(out=bst[:, g, :], in_=prsb[:, 1024 + g * 64:1024 + (g + 1) * 64]),
                        r=["prsb2"], w=["bst"])
                for g in range(4):
                    dve(lambda e, g=g: e.bn_aggr(out=mv[:, g, :], in_=bst[:, g, :]), r=["bst"], w=["mv"])
                dve(lambda e: e.tensor_copy(out=vr[:, :], in_=mv[:, :, 1]), r=["mv"], w=["vr"])
                rstd_from_ss(vr[:, 0:4], vr[:, 0:4], 1.0, 4, "vr", "vr", vt[:, 0:4], "vt")
                dve(lambda e: e.tensor_tensor(out=vc[:, :, :], in0=prsb[:, 1024:1280].rearrange("p (g d) -> p g d", g=4),
                                              in1=bc(mv[:, :, 0], 2, 64), op=ALU.subtract), r=["prsb2", "mv"], w=["vc"])
                dve(lambda e: e.tensor_tensor(out=vn[:, :, :], in0=vc[:, :, :], in1=bc(vr[:, 0:4], 2, 64), op=ALU.mult),
                    r=["vc", "vr"], w=["vn"])
                for g in range(4):
                    pe(lambda e, g=g: e.matmul(pb[7][:, g * 64:(g + 1) * 64], lhsT=swT[:, g, :], rhs=vn[:, g, :],
                                               start=True, stop=True), r=["swT", "vn"], w=["pb7"])
                dve(lambda e: e.tensor_tensor(out=vc[:, :, :], in0=pb[7][:, 0:256].rearrange("p (g d) -> p g d", g=4),
                                              in1=bc(sgb[:], 2, 64), op=ALU.add), r=["pb7", "sgb"], w=["vc"])
                dve(lambda e: e.tensor_tensor(out=mf[:, 768:1024], in0=vc[:, :, :].rearrange("p g d -> p (g d)"),
                                              in1=prsb[:, 768:1024], op=ALU.mult), r=["vc", "prsb2"], w=["mf2"])
                for j, (a0, a1) in enumerate(((0, 512), (512, 768), (768, 1024))):
                    act(lambda e, j=j, a0=a0, a1=a1: e.activation(out=A["junkb"][:, a0:a1], in_=mf[:, a0:a1], func=AF.Square,
                                                                  accum_out=s3[:, j:j + 1]),
                        r=["mf%d" % j], w=["junkb", "s3"])
                dve(lambda e: e.tensor_tensor(out=s3[:, 0:3], in0=s3[:, 0:3], in1=cst[:, C_SEGL:C_SEGL + 3], op=ALU.mult),
                    r=["s3", "cst"], w=["s3"])
                rstd_from_ss(s3[:, 0:3], r3s[:, 0:3], 1.0, 3, "s3", "r3s", t3[:, 0:3], "t3")
                for j, (a0, a1) in enumerate(((0, 512), (512, 768), (768, 1024))):
                    act(lambda e, j=j, a0=a0, a1=a1: e.activation(out=mb[:, a0:a1], in_=mf[:, a0:a1], func=AF.Copy,
                                                                  scale=r3s[:, j:j + 1]),
                        r=["mf%d" % j, "r3s"], w=["mb"])
                for k in range(8):
                    pe(lambda e, k=k: e.transpose(out=pb0b[:, k * 128:(k + 1) * 128], in_=mb[:, k * 128:(k + 1) * 128],
                                                  identity=identb[:]), r=["mb", "identb"], w=["pb0"])
                act(lambda e: e.activation(out=mT[:].rearrange("p k t -> p (k t)"), in_=pb0b[:, :], func=AF.Copy),
                    r=["pb0"], w=["mT"])
                for cc in range(2):
                    bank = 1 + cc
                    for k in range(8):
                        pe(lambda e, cc=cc, k=k, bank=bank: e.matmul(pb[bank][:, :], lhsT=mT[:, k, :],
                                                                     rhs=w_out_sb[:, k, cc * 512:(cc + 1) * 512],
                                                                     start=(k == 0), stop=(k == 7)),
                           r=["mT", "w_out"], w=["pb%d" % bank])
                    dve(lambda e, cc=cc, bank=bank: e.tensor_tensor(out=og[:, :], in0=pb[bank][:, :],
                                                                    in1=gB[:, 0, cc * 512:(cc + 1) * 512], op=ALU.mult),
                        r=["pb%d" % bank, "gB"], w=["og"])
                    dve(lambda e, cc=cc, xt=xt: e.tensor_tensor(out=xt[:, cc * 512:(cc + 1) * 512],
                                                                in0=xt[:, cc * 512:(cc + 1) * 512], in1=og[:, :], op=ALU.add),
                        r=["og", xtag], w=[xtag])
                dma("sp", y_out[n * 128:(n + 1) * 128, :], xt, r=[xtag], w=[("yrow", n)], key=("xo", n % 2))

        def phaseB(l):
            ar.reset()
            A = {}
            A["junkb"] = ar.alloc([D], BF16)
            A["ss"] = ar.alloc([4], F32)
            A["rs"] = ar.alloc([4], F32)
            A["xn"] = ar.alloc([D], BF16)
            A["hT"] = ar.alloc([8, 128], BF16)
            h2 = ar.alloc([D], BF16)
            qTp = ar.alloc([16, 128], BF16)
            ssb = ar.alloc([16, 128], F32)
            s2 = ar.alloc([256], F32)
            tops = ar.alloc([16, 16], F32)
            topi = ar.alloc([16, 16], U32)
            topf = ar.alloc([16, 16], F32)
            cand = ar.alloc([8, 256], F32)
            best = ar.alloc([8, 16], F32)
            bpos = ar.alloc([8, 16], U32)
            ab_u = ar.alloc([2, 128], U32)
            ab_f = ar.alloc([2, 128], F32)
            eq = ar.alloc([8, 16, 16], F32)
            ij = ar.alloc([2, 128], F32)
            Ef = ar.alloc([128], F32)
            Eu = ar.alloc([128], U32)
            gd = ar.alloc([8, 16], F32)
            gs = ar.alloc([8], F32)
            gate = ar.alloc([8, 16], F32)
            actv = ar.alloc([128], F32)
            wgt = ar.alloc([128], F32)
            yacc = ar.alloc([D], F32)
            yacc1 = ar.alloc([D], F32)
            junk1 = ar.alloc([D], BF16)
            og = ar.alloc([512], F32)
            for n in range(nt):
                xt, xtag = tile_front(n, y_out, 1, A)
                for k in range(8):
                    pe(lambda e, k=k: e.transpose(out=pb0b[:, k * 128:(k + 1) * 128], in_=A["hT"][:, k, :],
                                                  identity=identb[:]), r=["hT", "identb"], w=["pb0"])
                act(lambda e: e.activation(out=h2[:, :], in_=pb0b[:, :], func=AF.Copy), r=["pb0"], w=["h2"])
                for j in range(16):
                    bank = 1 + j // 4
                    for k in range(8):
                        pe(lambda e, j=j, k=k, bank=bank: e.matmul(pb[bank][:, (j % 4) * 128:(j % 4 + 1) * 128],
                                                                   lhsT=wq_sb[:, k, j * 128:(j + 1) * 128], rhs=A["hT"][:, k, :],
                                                                   start=(k == 0), stop=(k == 7)),
                           r=["wq", "hT"], w=["pb%d" % bank])
                for b4 in range(4):
                    act(lambda e, b4=b4: e.activation(out=qTp[:, b4 * 4:(b4 + 1) * 4, :].rearrange("p j t -> p (j t)"),
                                                      in_=pb[1 + b4][:, :], func=AF.Copy),
                        r=["pb%d" % (1 + b4)], w=[("qTp", b4)])
                for j in range(16):
                    bank = 1 + j // 4
                    pe(lambda e, j=j, bank=bank: e.matmul(pb[bank][:, (j % 4) * 128:(j % 4 + 1) * 128], lhsT=qTp[:, j, :],
                                                          rhs=skT_sb[:, j, :], start=True, stop=True),
                       r=[("qTp", j // 4), "skT"], w=["pb%d" % bank])
                for b4 in range(4):
                    act(lambda e, b4=b4: e.activation(out=ssb[:, b4 * 4:(b4 + 1) * 4, :].rearrange("p j t -> p (j t)"),
                                                      in_=pb[1 + b4][:, :], func=AF.Copy),
                        r=["pb%d" % (1 + b4)], w=[("ssb", b4)])

                def top16(src_ap, srctag, tmp_ap, vals, idxs, vtag, itag):
                    dve(lambda e: e.max(out=vals[:, 0:8], in_=src_ap), r=[srctag], w=[vtag])
                    dve(lambda e: e.max_index(out=idxs[:, 0:8], in_max=vals[:, 0:8], in_values=src_ap),
                        r=[srctag, vtag], w=[itag])
                    dve(lambda e: e.match_replace(out=tmp_ap, in_to_replace=vals[:, 0:8], in_values=src_ap, imm_value=-1e30),
                        r=[srctag, vtag], w=["s2"])
                    dve(lambda e: e.max(out=vals[:, 8:16], in_=tmp_ap), r=["s2"], w=[vtag + "b"])
                    dve(lambda e: e.max_index(out=idxs[:, 8:16], in_max=vals[:, 8:16], in_values=tmp_ap),
                        r=["s2", vtag + "b"], w=[itag + "b"])

                for j in range(16):
                    top16(ssb[:, j, :], ("ssb", j // 4), s2[:, 0:128], tops[:, j, :], topi[:, j, :], "tops", "topi")
                dve(lambda e: e.tensor_copy(out=topf[:, :, :], in_=topi[:, :, :]), r=["topi", "topib"], w=["topf"])
                tv = tops[:, :, :].rearrange("p (h c) k -> p h c k", c=2)
                tf = topf[:, :, :].rearrange("p (h c) k -> p h c k", c=2)
                dve(lambda e: e.tensor_tensor(out=cand[:, :, :].rearrange("p h (a b) -> p h a b", a=16),
                                              in0=bc(tv[:, :, 0, :], 3, 16), in1=bc(tv[:, :, 1, :], 2, 16), op=ALU.add),
                    r=["tops", "topsb"], w=["cand"])
                for h in range(8):
                    top16(cand[:, h, :], "cand", s2[:, 0:256], best[:, h, :], bpos[:, h, :], "best", "bpos")
                bp = bpos[:, :, :].rearrange("p h k -> p (h k)")
                dve(lambda e: e.tensor_single_scalar(out=ab_u[:, 0, :], in_=bp, scalar=4, op=ALU.logical_shift_right),
                    r=["bpos", "bposb"], w=["ab_u"])
                dve(lambda e: e.tensor_single_scalar(out=ab_u[:, 1, :], in_=bp, scalar=15, op=ALU.bitwise_and),
                    r=["bpos", "bposb"], w=["ab_u"])
                dve(lambda e: e.tensor_copy(out=ab_f[:, :, :], in_=ab_u[:, :, :]), r=["ab_u"], w=["ab_f"])
                iota = cst[:, C_IOTA:C_IOTA + 16]
                for c in range(2):
                    abv = ab_f[:, c, :].rearrange("p (h k) -> p h k", h=8)
                    dve(lambda e, abv=abv: e.tensor_tensor(out=eq[:, :, :, :], in0=bc(abv, 3, 16),
                                                           in1=bc(bc(iota, 1, 16), 1, 8), op=ALU.is_equal),
                        r=["ab_f", "cst", "ij"], w=["eq"])
                    dve(lambda e, c=c: e.tensor_tensor(out=eq[:, :, :, :], in0=eq[:, :, :, :], in1=bc(tf[:, :, c, :], 2, 16),
                                                       op=ALU.mult), r=["eq", "topf"], w=["eq"])
                    dve(lambda e, c=c: e.tensor_reduce(out=ij[:, c, :], in_=eq[:, :, :, :].rearrange("p h k a -> p (h k) a"),
                                                       axis=AX.X, op=ALU.add), r=["eq"], w=["ij"])
                dve(lambda e: e.scalar_tensor_tensor(out=Ef[:, :], in0=ij[:, 0, :], scalar=128.0, in1=ij[:, 1, :],
                                                     op0=ALU.mult, op1=ALU.add), r=["ij"], w=["Ef"])
                dve(lambda e: e.tensor_copy(out=Eu[:, :], in_=Ef[:, :]), r=["Ef"], w=["Eu"])
                dve(lambda e: e.tensor_tensor(out=gd[:, :, :], in0=best[:, :, :], in1=bc(best[:, :, 0], 2, 16), op=ALU.subtract),
                    r=["best", "bestb"], w=["gd"])
                act(lambda e: e.activation(out=gd[:, :, :], in_=gd[:, :, :], func=AF.Exp), r=["gd"], w=["gd"])
                dve(lambda e: e.tensor_reduce(out=gs[:, :], in_=gd[:, :, :], axis=AX.X, op=ALU.add), r=["gd"], w=["gs"])
                dve(lambda e: e.reciprocal(out=gs[:, :], in_=gs[:, :]), r=["gs"], w=["gs"])
                dve(lambda e: e.tensor_tensor(out=gate[:, :, :], in0=gd[:, :, :], in1=bc(gs[:, :], 2, 16), op=ALU.mult),
                    r=["gd", "gs"], w=["gate"])
                for m in range(128):
                    sl = m % NSLOT
                    P.op("pool", lambda e, m=m, sl=sl: e.indirect_dma_start(
                        out=gdn[:, sl, :], out_offset=None, in_=pdn_bf,
                        in_offset=bass.IndirectOffsetOnAxis(ap=Eu[:, m:m + 1], axis=0)),
                        r=["Eu"], w=[("gdn", sl)], dma=("gdn", sl))
                    jk = A["junkb"] if m % 2 == 0 else junk1
                    dve(lambda e, m=m, sl=sl, jk=jk: e.scalar_tensor_tensor(out=jk, in0=gdn[:, sl, :], scalar=1.0, in1=h2[:, :],
                                                                            op0=ALU.mult, op1=ALU.mult, accum_out=actv[:, m:m + 1]),
                        r=[("gdn", sl), "h2"], w=["junkb" if m % 2 == 0 else "junk1", ("actv", m % 2)])
                act(lambda e: e.activation(out=wgt[:, :], in_=actv[:, :], func=AF.Gelu), r=[("actv", 0), ("actv", 1)], w=["wgt"])
                dve(lambda e: e.tensor_tensor(out=wgt[:, :], in0=wgt[:, :], in1=gate[:, :, :].rearrange("p h k -> p (h k)"),
                                              op=ALU.mult), r=["wgt", "gate"], w=["wgt"])
                for m in range(128):
                    sl = m % NSLOT
                    P.op("pool", lambda e, m=m, sl=sl: e.indirect_dma_start(
                        out=gup[:, sl, :], out_offset=None, in_=pup_bf,
                        in_offset=bass.IndirectOffsetOnAxis(ap=Eu[:, m:m + 1], axis=0)),
                        r=["Eu"], w=[("gup", sl)], dma=("gup", sl))
                    ya_ = yacc if m % 2 == 0 else yacc1
                    yt_ = "yacc" if m % 2 == 0 else "yacc1"
                    if m < 2:
                        dve(lambda e, m=m, sl=sl, ya_=ya_: e.tensor_scalar(out=ya_[:, :], in0=gup[:, sl, :], scalar1=wgt[:, m:m + 1],
                                                                          scalar2=None, op0=ALU.mult),
                            r=[("gup", sl), "wgt"], w=[yt_])
                    else:
                        dve(lambda e, m=m, sl=sl, ya_=ya_: e.scalar_tensor_tensor(out=ya_[:, :], in0=gup[:, sl, :], scalar=wgt[:, m:m + 1],
                                                                                 in1=ya_[:, :], op0=ALU.mult, op1=ALU.add),
                            r=[("gup", sl), "wgt", yt_], w=[yt_])
                dve(lambda e: e.tensor_tensor(out=yacc[:, :], in0=yacc[:, :], in1=yacc1[:, :], op=ALU.add),
                    r=["yacc", "yacc1"], w=["yacc"])
                for cc in range(2):
                    dve(lambda e, cc=cc: e.tensor_tensor(out=og[:, :], in0=yacc[:, cc * 512:(cc + 1) * 512],
                                                         in1=gB[:, 1, cc * 512:(cc + 1) * 512], op=ALU.mult),
                        r=["yacc", "gB"], w=["og"])
                    dve(lambda e, cc=cc, xt=xt: e.tensor_tensor(out=xt[:, cc * 512:(cc + 1) * 512],
                                                                in0=xt[:, cc * 512:(cc + 1) * 512], in1=og[:, :], op=ALU.add),
                        r=["og", xtag], w=[xtag])
                dma("sp", y_out[n * 128:(n + 1) * 128, :], xt, r=[xtag], w=[("yrow", n)], key=("xo", n % 2))

        for l in range(depth):
            P.barrier()
            layer_params(l)
            phaseA_weights(l)
            P.barrier()
            phaseA(l)
            P.barrier()
            if only_a:
                continue
            phaseB_weights(l)
            P.barrier()
            convert_tables(l)
            P.barrier()
            phaseB(l)
        P.barrier()

        sems = {}
        for e in Prog.ENG:
            sems[e] = es.enter_context(nc.semaphore("s_" + e))
        for i, key in enumerate(P.dcnt):
            sems[key] = es.enter_context(nc.semaphore("d%d" % i))
        block = es.enter_context(nc.Block())

        def make_body(ename):
            def body(eng):
                for o in P.ops[ename]:
                    if o[0] == "w":
                        eng.wait_ge(sems[o[1]], o[2])
                    else:
                        ins = o[1](eng)
                        k = o[2][0]
                        ins.then_inc(sems[k], 16 if isinstance(k, tuple) else 1)
            return body

        block.tensor(make_body("pe"))
        block.scalar(make_body("act"))
        block.vector(make_body("dve"))
        block.gpsimd(make_body("pool"))
        block.sync(make_body("sp"))
    return nc


def _bucket(dist):
    n_buckets, max_distance = 32, 128
    max_exact = n_buckets // 2
    d = np.maximum(dist, 0)
    lr = np.log(np.maximum(d, 1).astype(np.float32) / np.float32(max_exact)) / np.float32(math.log(max_distance / max_exact))
    large = max_exact + (lr.astype(np.float32) * np.float32(n_buckets - max_exact)).astype(np.int32)
    large = np.minimum(large, n_buckets - 1)
    return np.where(d < max_exact, d, large)


def _consts():
    c = np.zeros((128, NCST), np.float32)
    i = np.arange(128)
    c[:, C_ID:C_ID + 128] = np.eye(128, dtype=np.float32)
    tp, t = np.meshgrid(i, i, indexing="ij")
    c[:, C_S1:C_S1 + 128] = (tp == t - 1)
    c[:, C_S2:C_S2 + 128] = (tp == t - 2)
    c[:, C_S1P:C_S1P + 128] = (tp == t + 127)
    c[:, C_S2P:C_S2P + 128] = (tp == t + 126)
    tt, ss = np.meshgrid(i, i, indexing="ij")
    c[:, C_TRIL:C_TRIL + 128] = (ss <= tt)
    s, q = np.meshgrid(i, i, indexing="ij")
    c[:, C_MPREV:C_MPREV + 128] = np.where(s > q, 0.0, -1e30)
    c[:, C_MCUR:C_MCUR + 128] = np.where(s <= q, 0.0, -1e30)
    c[:, C_IOTA:C_IOTA + 16] = np.arange(16, dtype=np.float32)[None, :]
    c[:, C_SEGL:C_SEGL + 3] = np.array([1 / 512., 1 / 256., 1 / 256.], np.float32)[None, :]
    return c


def _bias_index():
    i = np.arange(128)
    s, q = np.meshgrid(i, i, indexing="ij")
    dprev = q + 128 - s
    dcur = q - s
    bp = np.where(s > q, _bucket(dprev), 0)
    bcur = np.where(s <= q, _bucket(dcur), 0)
    return np.stack([bp, bcur], axis=1)


_CACHE = {}


def _prep_shared(inp, depth):
    f = lambda a: np.ascontiguousarray(np.asarray(a, dtype=np.float32))
    rb = f(inp["rel_bias"])
    bidx = _bias_index()
    braw = rb[bidx]
    braw = np.ascontiguousarray(braw.transpose(0, 1, 3, 2)).reshape(128, 2 * 8 * 128)
    b_ada = f(inp["b_ada"])[:depth]
    sh = {
        "cst": _consts(),
        "braw": braw,
        "w_ada": f(inp["w_ada"])[:depth],
        "badaT": np.ascontiguousarray(b_ada.reshape(depth, 48, 128).transpose(0, 2, 1)),
        "badaG": np.ascontiguousarray(np.broadcast_to(
            np.concatenate([b_ada[:, 2048:3072], b_ada[:, 5120:6144]], axis=1)[:, None, :], (depth, 128, 2048))),
        "n1T": np.ascontiguousarray(f(inp["norm1_g"])[:depth].reshape(depth, 8, 128).transpose(0, 2, 1)),
        "n2T": np.ascontiguousarray(f(inp["norm2_g"])[:depth].reshape(depth, 8, 128).transpose(0, 2, 1)),
        "ongT": np.ascontiguousarray(f(inp["out_norm_g"])[:depth].reshape(depth, 8, 128).transpose(0, 2, 1)),
        "w_in": f(inp["w_in"])[:depth],
        "w_out": f(inp["w_out"])[:depth],
        "wq": f(inp["peer_wq"])[:depth],
        "qgB": np.ascontiguousarray(np.broadcast_to(f(inp["q_norm_g"])[:depth, None, :], (depth, 128, 64))),
        "kgB": np.ascontiguousarray(np.broadcast_to(f(inp["k_norm_g"])[:depth, None, :], (depth, 128, 64))),
        "sinkB": np.ascontiguousarray(np.broadcast_to(f(inp["attn_sink"])[:depth, None, :], (depth, 128, 8))),
        "convB": np.ascontiguousarray(np.broadcast_to(f(inp["conv_w"])[:depth].reshape(depth, 1, 768), (depth, 128, 768))),
        "sgu_w": f(inp["sgu_w"])[:depth],
        "sgubT": np.ascontiguousarray(f(inp["sgu_b"])[:depth].transpose(0, 2, 1)),
        "subk": f(inp["peer_sub_keys"])[:depth].reshape(depth, 16, 128, 128),
    }
    ne = 128 if DEBUG_SMALL_TABLES else 16384
    for i in range(depth):
        sh["pdown%d" % i] = np.ascontiguousarray(np.asarray(inp["peer_down"][i], dtype=np.float32)[:ne])
        sh["pup%d" % i] = np.ascontiguousarray(np.asarray(inp["peer_up"][i], dtype=np.float32)[:ne])
    return sh


def run(inp, n_cores=8, nt=SEQ // 128, depth=DEPTH, trace=False, only_a=False):
    key = (nt, depth, only_a)
    if key not in _CACHE:
        _CACHE[key] = build_program(nt, depth, only_a=only_a)
    nc = _CACHE[key]
    sh = _prep_shared(inp, depth)
    x = np.asarray(inp["x"], dtype=np.float32)
    c = np.asarray(inp["c"], dtype=np.float32)
    in_maps = []
    for b in range(n_cores):
        m = dict(sh)
        m["x"] = np.ascontiguousarray(x[b, :nt * 128])
        m["cT"] = np.ascontiguousarray(c[b].reshape(8, 128).T)
        in_maps.append(m)
    res = run_bass_kernel_spmd(nc, in_maps, core_ids=list(range(n_cores)), **({"trace": True} if trace else {}))
    return np.stack([r["y"] for r in res.results], axis=0), res


def kernel(**inputs):
    out, _ = run(inputs)
    return out.astype(np.float32)
```

```python
import math
from contextlib import ExitStack
import numpy as np
import concourse.bass as bass
import concourse.mybir as mybir
from concourse.bass_utils import run_bass_kernel_spmd

F32 = mybir.dt.float32
BF16 = mybir.dt.bfloat16
U32 = mybir.dt.uint32
ALU = mybir.AluOpType
AF = mybir.ActivationFunctionType
AX = mybir.AxisListType

D = 1024
DEPTH = 4
SEQ = 4096
EPS = 1e-6
NSLOT = 12
SAME_ENGINE_SYNC = True
DEBUG_MAXOPS = None
DEBUG_SKIP = set()
DEBUG_SMALL_TABLES = False

C_ID = 0
C_S1 = 128
C_S2 = 256
C_S1P = 384
C_S2P = 512
C_TRIL = 640
C_MPREV = 768
C_MCUR = 896
C_IOTA = 1024
C_SEGL = 1040
NCST = 1048


class Prog:
    ENG = ("pe", "act", "dve", "pool", "sp")

    def __init__(self):
        self.ops = {e: [] for e in self.ENG}
        self.cnt = {e: 0 for e in self.ENG}
        self.known = {e: {} for e in self.ENG}
        self.tags = {}
        self.dcnt = {}

    def _dep(self, eng, tok, raw):
        k, v = tok
        if k == eng:
            if eng == "pe" or not SAME_ENGINE_SYNC:
                return
        if self.known[eng].get(k, 0) >= v:
            return
        self.known[eng][k] = v
        self.ops[eng].append(("w", k, v))

    def op(self, eng, fn, r=(), w=(), dma=None):
        self.nops = getattr(self, 'nops', 0) + 1
        if DEBUG_MAXOPS is not None and self.nops > DEBUG_MAXOPS:
            return
        if self.nops in DEBUG_SKIP:
            return
        for t in r:
            st = self.tags.get(t)
            if st and st["w"]:
                self._dep(eng, st["w"], True)
        for t in w:
            st = self.tags.get(t)
            if st:
                if st["w"]:
                    self._dep(eng, st["w"], False)
                for k, v in st["r"].items():
                    self._dep(eng, (k, v), False)
        if dma is None:
            self.cnt[eng] += 1
            tok = (eng, self.cnt[eng])
        else:
            key = ("dma", dma)
            self.dcnt[key] = self.dcnt.get(key, 0) + 16
            tok = (key, self.dcnt[key])
        self.ops[eng].append(("i", fn, tok))
        for t in w:
            self.tags[t] = {"w": tok, "r": {}}
        for t in r:
            st = self.tags.setdefault(t, {"w": None, "r": {}})
            st["r"][tok[0]] = max(st["r"].get(tok[0], 0), tok[1])

    def barrier(self):
        for e in self.ENG:
            for f in self.ENG:
                if f != e and self.cnt[f] > 0:
                    self._dep(e, (f, self.cnt[f]), True)
            for key, v in self.dcnt.items():
                self._dep(e, (key, v), True)


ARENA_LOG = []


class Arena:
    def __init__(self, ap_f32, nwords):
        self.ap = ap_f32
        self.n = nwords
        self.off = 0

    def reset(self):
        self.off = 0

    def alloc(self, free_shape, dtype):
        nel = int(np.prod(free_shape))
        esz = 2 if dtype == BF16 else 4
        nw = (nel * esz + 3) // 4
        nw = (nw + 7) // 8 * 8
        assert self.off + nw <= self.n, ("arena overflow", self.off, nw, self.n)
        v = self.ap[:, self.off:self.off + nw]
        ARENA_LOG.append((self.off, nel, dtype, tuple(free_shape)))
        self.off += nw
        if dtype != F32:
            v = v.bitcast(dtype)
        v = v[:, 0:nel]
        if len(free_shape) == 2:
            v = v.rearrange("p (a b) -> p a b", a=free_shape[0])
        elif len(free_shape) == 3:
            v = v.rearrange("p (a b c) -> p a b c", a=free_shape[0], b=free_shape[1])
        return v


def bc(ap, pos, n):
    u = ap.unsqueeze(pos)
    shp = list(u.shape)
    shp[pos] = n
    return u.broadcast_to(shp)


def build_program(nt, depth, only_a=False):
    S = nt * 128
    nc = bass.Bass("TRN2", target_bir_lowering=False)

    def din(name, shape, dt=F32):
        return nc.dram_tensor(name, list(shape), dt, kind="ExternalInput").ap()

    x_in = din("x", [S, D])
    cT_in = din("cT", [128, 8])
    cst_in = din("cst", [128, NCST])
    braw_in = din("braw", [128, 2 * 8 * 128])
    w_ada = din("w_ada", [depth, D, 6 * D])
    badaT = din("badaT", [depth, 128, 48])
    badaG = din("badaG", [depth, 128, 2 * D])
    n1T = din("n1T", [depth, 128, 8])
    n2T = din("n2T", [depth, 128, 8])
    ongT = din("ongT", [depth, 128, 8])
    w_in = din("w_in", [depth, D, 2048])
    w_out = din("w_out", [depth, D, D])
    wq = din("wq", [depth, D, 2048])
    qgB = din("qgB", [depth, 128, 64])
    kgB = din("kgB", [depth, 128, 64])
    sinkB = din("sinkB", [depth, 128, 8])
    convB = din("convB", [depth, 128, 768])
    sgu_w = din("sgu_w", [depth, 4, 128, 128])
    sgubT = din("sgubT", [depth, 128, 4])
    subk = din("subk", [depth, 16, 128, 128])
    NE = 128 if DEBUG_SMALL_TABLES else 16384
    pdown = [din("pdown%d" % i, [NE, D]) for i in range(depth)]
    pup = [din("pup%d" % i, [NE, D]) for i in range(depth)]
    y_out = nc.dram_tensor("y", [S, D], F32, kind="ExternalOutput").ap()
    ptab = nc.dram_tensor("ptab_bf", [NE, 2 * D], BF16).ap()

    P = Prog()
    es = ExitStack()
    with es:
        def sb(name, shape, dt=F32):
            return es.enter_context(nc.sbuf_tensor(name, list(shape), dt))

        def ps(name):
            return es.enter_context(nc.psum_tensor(name, [128, 512], F32))

        R1W = 12288
        r1 = sb("r1", [128, R1W])
        GW = NSLOT * 512
        gsl2 = sb("gsl", [128, NSLOT * 2 * D], BF16)
        gsl = gsl2[:].rearrange("p (s c d) -> p s c d", s=NSLOT, c=2)
        gslf = gsl2[:].rearrange("p (s d) -> p s d", s=NSLOT)
        stg = [gsl2[:].bitcast(F32)[:, 0:2048], gsl2[:].bitcast(F32)[:, 2048:4096]]
        cst = sb("cst_sb", [128, NCST])
        identb = sb("identb", [128, 128], BF16)
        biasT = sb("biasT", [128, 2, 8, 128])
        R3W = 16 * 1024
        r3 = sb("r3", [128, R3W])
        cact = sb("cact", [128, 8])
        cbc = sb("cbc", [128, 8, 128])
        modT = sb("modT", [128, 4, 8])
        bT = sb("bT", [128, 48])
        nT = sb("nT", [128, 3, 8])
        sT = sb("sT", [128, 2, 8])
        gB = sb("gB", [128, 2, D])
        qg8 = sb("qg8", [128, 64])
        kgb = sb("kgb", [128, 64])
        esink = sb("esink", [128, 8])
        cwb = sb("cwb", [128, 3, 256])
        swT = sb("swT", [128, 4, 128], BF16)
        sgb = sb("sgb", [128, 4])
        xbuf = [sb("xbuf0", [128, D]), sb("xbuf1", [128, D])]
        kTb = [sb("kT0", [64, 2, 128], BF16), sb("kT1", [64, 2, 128], BF16)]
        vaug = [sb("va0", [128, 2, 80], BF16), sb("va1", [128, 2, 80], BF16)]
        zb = [sb("z0", [128, 256]), sb("z1", [128, 256])]
        pb = [ps("pb%d" % i) for i in range(8)]
        pb0b = pb[0][:].bitcast(BF16)

        w_in_sb = r1[:, 0:8192].bitcast(BF16).rearrange("p (k n) -> p k n", k=8)
        w_out_sb = r1[:, 8192:12288].bitcast(BF16).rearrange("p (k n) -> p k n", k=8)
        wq_sb = w_in_sb
        skT_sb = r1[:, 8192:9216].bitcast(BF16).rearrange("p (j n) -> p j n", j=16)

        ar = Arena(r3[:], R3W)

        def dve(fn, r=(), w=()):
            P.op("dve", fn, r, w)

        def act(fn, r=(), w=()):
            P.op("act", fn, r, w)

        def pe(fn, r=(), w=()):
            P.op("pe", fn, r, w)

        def dma(eng, out, in_, r=(), w=(), key=None):
            P.op(eng, lambda e, out=out, in_=in_: e.dma_start(out=out, in_=in_), r, w, dma=key)

        def rstd_from_ss(ss, out, scale, n, tagss, tagout, tmp, tagtmp):
            dve(lambda e: e.tensor_scalar(out=tmp, in0=ss, scalar1=scale, scalar2=EPS, op0=ALU.mult, op1=ALU.add),
                r=[tagss], w=[tagtmp])
            act(lambda e: e.activation(out=tmp, in_=tmp, func=AF.Sqrt), r=[tagtmp], w=[tagtmp])
            dve(lambda e: e.reciprocal(out=out, in_=tmp), r=[tagtmp], w=[tagout])

        dma("sp", cst[:], cst_in, w=["cst"], key="c0")
        dma("sp", biasT[:].rearrange("p a h q -> p (a h q)"), braw_in, w=["biasT"], key="c1")
        dma("sp", cact[:], cT_in, w=["cact"], key="c2")
        dve(lambda e: e.tensor_copy(out=identb[:], in_=cst[:, C_ID:C_ID + 128]), r=["cst"], w=["identb"])
        for a, cm in ((0, C_MPREV), (1, C_MCUR)):
            dve(lambda e, a=a, cm=cm: e.tensor_tensor(out=biasT[:, a], in0=biasT[:, a],
                                                       in1=bc(cst[:, cm:cm + 128], 1, 8), op=ALU.add),
                r=["cst", "biasT"], w=["biasT"])
        act(lambda e: e.activation(out=cact[:], in_=cact[:], func=AF.Silu), r=["cact"], w=["cact"])
        dve(lambda e: e.tensor_copy(out=cbc[:], in_=bc(cact[:], 2, 128)), r=["cact"], w=["cbc"])
        for par in range(2):
            dve(lambda e, par=par: e.memset(vaug[par][:], 1.0), w=[("va", par)])

        ident32 = cst[:, C_ID:C_ID + 128]

        def load_convert(dst_fn, src_rows_fn, nrows_chunks, ncols, scale_fn=None, wtag=None):
            i = 0
            for k in range(nrows_chunks):
                for c0 in range(0, ncols, 2048):
                    c1 = min(ncols, c0 + 2048)
                    s = stg[i % 2]
                    tg = ("stg", i % 2)
                    dma("sp", s[:, 0:c1 - c0], src_rows_fn(k)[:, c0:c1], w=[tg], key=("stg", i % 2))
                    eng = "dve" if i % 2 == 0 else "act"
                    if scale_fn is None:
                        if eng == "dve":
                            dve(lambda e, s=s, k=k, c0=c0, c1=c1: e.tensor_copy(out=dst_fn(k, c0, c1), in_=s[:, 0:c1 - c0]),
                                r=[tg], w=[wtag])
                        else:
                            act(lambda e, s=s, k=k, c0=c0, c1=c1: e.activation(out=dst_fn(k, c0, c1), in_=s[:, 0:c1 - c0], func=AF.Copy),
                                r=[tg], w=[wtag])
                    else:
                        dve(lambda e, s=s, k=k, c0=c0, c1=c1: e.tensor_scalar(out=dst_fn(k, c0, c1), in0=s[:, 0:c1 - c0],
                                                                              scalar1=scale_fn(k), scalar2=None, op0=ALU.mult),
                            r=[tg, "nT"], w=[wtag])
                    i += 1

        def layer_params(l):
            dma("sp", bT[:], badaT[l], w=["bT"], key="p0")
            dma("sp", nT[:, 0, :], n1T[l], w=["nT"], key="p1")
            dma("sp", nT[:, 1, :], n2T[l], w=["nT"], key="p1")
            dma("sp", nT[:, 2, :], ongT[l], w=["nT"], key="p1")
            dma("sp", gB[:].rearrange("p a d -> p (a d)"), badaG[l], w=["gB"], key="p2")
            dma("sp", qg8[:], qgB[l], w=["qg8"], key="p3")
            dma("sp", kgb[:], kgB[l], w=["kgb"], key="p4")
            dma("sp", esink[:], sinkB[l], w=["esink"], key="p5")
            dma("sp", cwb[:].rearrange("p a c -> p (a c)"), convB[l], w=["cwb"], key="p6")
            dma("sp", sgb[:], sgubT[l], w=["sgb"], key="p7")
            dve(lambda e: e.tensor_scalar(out=qg8[:], in0=qg8[:], scalar1=0.125, scalar2=None, op0=ALU.mult),
                r=["qg8"], w=["qg8"])
            act(lambda e: e.activation(out=esink[:], in_=esink[:], func=AF.Exp), r=["esink"], w=["esink"])
            fm_slot = {0: 0, 1: 1, 3: 2, 4: 3}
            for ci in range(24):
                seg = ci // 4
                s = stg[ci % 2]
                tg = ("stg", ci % 2)
                sv = s.rearrange("p (k n) -> p k n", k=8)
                dma("sp", sv, w_ada[l, :, ci * 256:(ci + 1) * 256].rearrange("(k p) n -> p k n", p=128),
                    w=[tg], key=("stg", ci % 2))
                if seg in fm_slot:
                    for half in range(2):
                        col = (ci % 4) * 2 + half
                        for k in range(8):
                            pe(lambda e, sv=sv, k=k, half=half: e.matmul(
                                pb[7][:, 0:1], lhsT=sv[:, k, half * 128:(half + 1) * 128], rhs=cact[:, k:k + 1],
                                start=(k == 0), stop=(k == 7)), r=[tg, "cact"], w=["pb7"])
                        dve(lambda e, seg=seg, col=col: e.tensor_tensor(
                            out=modT[:, fm_slot[seg], col:col + 1], in0=pb[7][:, 0:1],
                            in1=bT[:, seg * 8 + col:seg * 8 + col + 1], op=ALU.add),
                            r=["pb7", "bT"], w=["modT"])
                else:
                    gi = 0 if seg == 2 else 1
                    cc0 = (ci % 4) * 256
                    for k in range(8):
                        pe(lambda e, sv=sv, k=k: e.matmul(pb[6][:, 0:256], lhsT=cbc[:, k, :], rhs=sv[:, k, :],
                                                          start=(k == 0), stop=(k == 7)), r=[tg, "cbc"], w=["pb6"])
                    dve(lambda e, gi=gi, cc0=cc0: e.tensor_tensor(out=gB[:, gi, cc0:cc0 + 256], in0=pb[6][:, 0:256],
                                                                   in1=gB[:, gi, cc0:cc0 + 256], op=ALU.add),
                        r=["pb6", "gB"], w=["gB"])
            for j, slot in ((0, 1), (1, 3)):
                dve(lambda e, j=j, slot=slot: e.scalar_tensor_tensor(out=sT[:, j, :], in0=modT[:, slot, :], scalar=1.0,
                                                                     in1=nT[:, j, :], op0=ALU.add, op1=ALU.mult),
                    r=["modT", "nT"], w=["sT"])

        def phaseA_weights(l):
            load_convert(lambda k, c0, c1: w_in_sb[:, k, c0:c1],
                         lambda k: w_in[l, k * 128:(k + 1) * 128, :], 8, 2048, wtag="w_in")
            load_convert(lambda k, c0, c1: w_out_sb[:, k, c0:c1],
                         lambda k: w_out[l, k * 128:(k + 1) * 128, :], 8, 1024,
                         scale_fn=lambda k: nT[:, 2, k:k + 1], wtag="w_out")
            s = stg[0]
            sv = s[:, 0:512].rearrange("p (g s) -> p g s", g=4)
            dma("sp", sv, sgu_w[l].rearrange("g t s -> t g s"), w=[("stg", 0)], key=("stg", 0))
            dve(lambda e: e.tensor_tensor(out=sv, in0=sv, in1=bc(cst[:, C_TRIL:C_TRIL + 128], 1, 4), op=ALU.mult),
                r=[("stg", 0), "cst"], w=[("stg", 0)])
            for g in range(4):
                pe(lambda e, g=g: e.transpose(out=pb[7][:, g * 128:(g + 1) * 128], in_=sv[:, g, :], identity=ident32),
                   r=[("stg", 0), "cst"], w=["pb7"])
            dve(lambda e: e.tensor_copy(out=swT[:].rearrange("p g t -> p (g t)"), in_=pb[7][:, 0:512]),
                r=["pb7"], w=["swT"])

        def phaseB_weights(l):
            load_convert(lambda k, c0, c1: wq_sb[:, k, c0:c1],
                         lambda k: wq[l, k * 128:(k + 1) * 128, :], 8, 2048, wtag="wq")
            for jb in range(4):
                s = stg[jb % 2]
                tg = ("stg", jb % 2)
                sv = s[:, 0:512].rearrange("p (j k) -> p j k", j=4)
                dma("sp", sv, subk[l, jb * 4:(jb + 1) * 4].rearrange("j n k -> n j k"), w=[tg], key=("stg", jb % 2))
                for jj in range(4):
                    pe(lambda e, sv=sv, jj=jj: e.transpose(out=pb[7][:, jj * 128:(jj + 1) * 128], in_=sv[:, jj, :],
                                                           identity=ident32), r=[tg, "cst"], w=["pb7"])
                dve(lambda e, jb=jb: e.tensor_copy(out=skT_sb[:, jb * 4:(jb + 1) * 4, :].rearrange("p j n -> p (j n)"),
                                                   in_=pb[7][:, 0:512]), r=["pb7"], w=["skT"])

        def convert_tables(l):
            cbuf = [gsl2[:, 8192:10240], gsl2[:, 10240:12288]]
            jobs = []
            for src_t, c0 in ((pdown[l], 0), (pup[l], D)):
                for c in range(NE // 256):
                    jobs.append((src_t[c * 256:(c + 1) * 256, :].rearrange("(p r) d -> p (r d)", r=2),
                                 ptab[c * 256:(c + 1) * 256, c0:c0 + D].rearrange("(p r) d -> p r d", r=2)))

            def load(i):
                dma("sp", stg[i % 2], jobs[i][0], w=[("stg", i % 2)], key=("stg", i % 2))

            load(0)
            if len(jobs) > 1:
                load(1)
            for i in range(len(jobs)):
                b = i % 2
                if i % 2 == 0:
                    act(lambda e, b=b: e.activation(out=cbuf[b], in_=stg[b], func=AF.Copy), r=[("stg", b)], w=[("cb", b)])
                else:
                    dve(lambda e, b=b: e.tensor_copy(out=cbuf[b], in_=stg[b]), r=[("stg", b)], w=[("cb", b)])
                dma("sp", jobs[i][1], cbuf[b].rearrange("p (r d) -> p r d", r=2), r=[("cb", b)], w=[("tabrow", i)], key=("cbo", b))
                if i + 2 < len(jobs):
                    load(i + 2)

        def tile_front(n, src, which, A):
            xt = xbuf[n % 2][:]
            xtag = ("x", n % 2)
            dma("sp", xt, src[n * 128:(n + 1) * 128, :], w=[xtag], key=("x", n % 2))
            act(lambda e: e.activation(out=A["junkb"], in_=xt, func=AF.Square, accum_out=A["ss"][:, 0:1]),
                r=[xtag], w=["junkb", "ss"])
            rstd_from_ss(A["ss"][:, 0:1], A["rs"][:, 0:1], 1.0 / D, 1, "ss", "rs", A["ss"][:, 1:2], "ss1")
            act(lambda e: e.activation(out=A["xn"], in_=xt, func=AF.Copy, scale=A["rs"][:, 0:1]),
                r=[xtag, "rs"], w=["xn"])
            for k in range(8):
                pe(lambda e, k=k: e.transpose(out=pb0b[:, k * 128:(k + 1) * 128], in_=A["xn"][:, k * 128:(k + 1) * 128],
                                              identity=identb[:]), r=["xn", "identb"], w=["pb0"])
            sh_slot = 0 if which == 0 else 2
            for k in range(8):
                dve(lambda e, k=k: e.tensor_scalar(out=A["hT"][:, k, :], in0=pb0b[:, k * 128:(k + 1) * 128],
                                                   scalar1=sT[:, which, k:k + 1], scalar2=modT[:, sh_slot, k:k + 1],
                                                   op0=ALU.mult, op1=ALU.add),
                    r=["pb0", "sT", "modT"], w=["hT"])
            return xt, xtag

        def phaseA(l):
            ar.reset()
            A = {}
            A["junkb"] = ar.alloc([D], BF16)
            A["ss"] = ar.alloc([4], F32)
            A["rs"] = ar.alloc([4], F32)
            A["xn"] = ar.alloc([D], BF16)
            A["hT"] = ar.alloc([8, 128], BF16)
            prsb = ar.alloc([1280], F32)
            sq = ar.alloc([640], F32)
            ssq = ar.alloc([16], F32)
            rq = ar.alloc([16], F32)
            tq = ar.alloc([16], F32)
            qn = ar.alloc([640], BF16)
            qtmp = ar.alloc([640], F32)
            qT = ar.alloc([8, 128], BF16)
            lg = ar.alloc([512], F32)
            PT = ar.alloc([2, 8, 128], BF16)
            mf = ar.alloc([D], F32)
            mb = ar.alloc([D], BF16)
            mT = ar.alloc([8, 128], BF16)
            den = ar.alloc([8], F32)
            c1 = ar.alloc([256], F32)
            c2 = ar.alloc([256], F32)
            bst = ar.alloc([4, 6], F32)
            mv = ar.alloc([4, 2], F32)
            vr = ar.alloc([4], F32)
            vt = ar.alloc([4], F32)
            vc = ar.alloc([4, 64], F32)
            vn = ar.alloc([4, 64], BF16)
            s3 = ar.alloc([4], F32)
            r3s = ar.alloc([4], F32)
            t3 = ar.alloc([4], F32)
            og = ar.alloc([512], F32)
            src = x_in if l == 0 else y_out
            for n in range(nt):
                par = n % 2
                xt, xtag = tile_front(n, src, 0, A)
                for cc in range(4):
                    for k in range(8):
                        pe(lambda e, cc=cc, k=k: e.matmul(pb[1 + cc][:, :], lhsT=A["hT"][:, k, :],
                                                          rhs=w_in_sb[:, k, cc * 512:(cc + 1) * 512],
                                                          start=(k == 0), stop=(k == 7)),
                           r=["hT", "w_in"], w=["pb%d" % (1 + cc)])
                act(lambda e: e.activation(out=prsb[:, 0:256], in_=pb[2][:, 256:512], func=AF.Copy), r=["pb2"], w=["prsb0"])
                act(lambda e: e.activation(out=prsb[:, 256:768], in_=pb[3][:, :], func=AF.Copy), r=["pb3"], w=["prsb1"])
                act(lambda e: e.activation(out=prsb[:, 768:1280], in_=pb[4][:, :], func=AF.Copy), r=["pb4"], w=["prsb2"])
                for g in range(2):
                    act(lambda e, par=par, g=g: e.activation(out=vaug[par][:, g, 0:64],
                                                             in_=pb[2][:, 128 + g * 64:128 + (g + 1) * 64], func=AF.Copy),
                        r=["pb2"], w=[("va", par)])
                act(lambda e: e.activation(out=sq[:, 0:512], in_=pb[1][:, :], func=AF.Square), r=["pb1"], w=["sq"])
                act(lambda e: e.activation(out=sq[:, 512:640], in_=pb[2][:, 0:128], func=AF.Square), r=["pb2"], w=["sq"])
                dve(lambda e: e.tensor_reduce(out=ssq[:, 0:10], in_=sq[:, 0:640].rearrange("p (h d) -> p h d", d=64),
                                              axis=AX.X, op=ALU.add), r=["sq"], w=["ssq"])
                rstd_from_ss(ssq[:, 0:10], rq[:, 0:10], 1.0 / 64, 10, "ssq", "rq", tq[:, 0:10], "tq")
                dve(lambda e: e.tensor_tensor(out=qtmp[:, 0:512].rearrange("p (h d) -> p h d", d=64),
                                              in0=pb[1][:, :].rearrange("p (h d) -> p h d", d=64),
                                              in1=bc(rq[:, 0:8], 2, 64), op=ALU.mult), r=["pb1", "rq"], w=["qtmp"])
                dve(lambda e: e.tensor_tensor(out=qtmp[:, 512:640].rearrange("p (h d) -> p h d", d=64),
                                              in0=pb[2][:, 0:128].rearrange("p (h d) -> p h d", d=64),
                                              in1=bc(rq[:, 8:10], 2, 64), op=ALU.mult), r=["pb2", "rq"], w=["qtmp"])
                dve(lambda e: e.tensor_tensor(out=qn[:, 0:512].rearrange("p (h d) -> p h d", d=64),
                                              in0=qtmp[:, 0:512].rearrange("p (h d) -> p h d", d=64),
                                              in1=bc(qg8[:], 1, 8), op=ALU.mult), r=["qtmp", "qg8"], w=["qn"])
                dve(lambda e: e.tensor_tensor(out=qn[:, 512:640].rearrange("p (h d) -> p h d", d=64),
                                              in0=qtmp[:, 512:640].rearrange("p (h d) -> p h d", d=64),
                                              in1=bc(kgb[:], 1, 2), op=ALU.mult), r=["qtmp", "kgb"], w=["qn"])
                for h in range(8):
                    pe(lambda e, h=h: e.transpose(out=pb0b[0:64, h * 128:(h + 1) * 128], in_=qn[:, h * 64:(h + 1) * 64],
                                                  identity=identb[:]), r=["qn", "identb"], w=["pb0"])
                act(lambda e: e.activation(out=qT[0:64].rearrange("p h t -> p (h t)"), in_=pb0b[0:64, :], func=AF.Copy),
                    r=["pb0"], w=["qT"])
                for g in range(2):
                    pe(lambda e, g=g: e.transpose(out=pb0b[0:64, g * 128:(g + 1) * 128],
                                                  in_=qn[:, 512 + g * 64:512 + (g + 1) * 64], identity=identb[:]),
                       r=["qn", "identb"], w=["pb0"])
                act(lambda e, par=par: e.activation(out=kTb[par][:].rearrange("p g t -> p (g t)"), in_=pb0b[0:64, 0:256],
                                                    func=AF.Copy), r=["pb0"], w=[("kT", par)])
                whichs = (0, 1) if n > 0 else (1,)
                for g in range(2):
                    for a in whichs:
                        kp = par if a == 1 else 1 - par
                        bank = 1 + g * 2 + a
                        pe(lambda e, g=g, kp=kp, bank=bank: e.matmul(
                            pb[bank][:, :], lhsT=kTb[kp][:, g, :],
                            rhs=qT[0:64, 4 * g:4 * g + 4, :].rearrange("p h t -> p (h t)"), start=True, stop=True),
                            r=[("kT", kp), "qT"], w=["pb%d" % bank])
                        dve(lambda e, g=g, a=a, bank=bank: e.tensor_tensor(
                            out=lg[:, :], in0=pb[bank][:, :],
                            in1=biasT[:, a, 4 * g:4 * g + 4, :].rearrange("p h q -> p (h q)"), op=ALU.add),
                            r=["pb%d" % bank, "biasT"], w=["lg"])
                        act(lambda e, g=g, a=a: e.activation(
                            out=PT[:, a, 4 * g:4 * g + 4, :].rearrange("p h q -> p (h q)"), in_=lg[:, :], func=AF.Exp),
                            r=["lg"], w=[("PT", a, g)])
                for h in range(8):
                    g = h // 4
                    bank = 5 + g
                    o0 = (h % 4) * 65
                    for a in whichs:
                        kp = par if a == 1 else 1 - par
                        pe(lambda e, h=h, g=g, a=a, kp=kp, bank=bank, o0=o0, st=(a == whichs[0]): e.matmul(
                            pb[bank][:, o0:o0 + 65], lhsT=PT[:, a, h, :], rhs=vaug[kp][:, g, 0:65],
                            start=st, stop=(a == 1)),
                            r=[("PT", a, g), ("va", kp)], w=["pb%d" % bank])
                for g in range(2):
                    bank = 5 + g
                    pv = pb[bank][:, 0:260].rearrange("p (h c) -> p h c", c=65)
                    dve(lambda e, g=g, pv=pv: e.tensor_tensor(out=den[:, 4 * g:4 * g + 4], in0=pv[:, :, 64],
                                                              in1=esink[:, 4 * g:4 * g + 4], op=ALU.add),
                        r=["pb%d" % bank, "esink"], w=["den"])
                    dve(lambda e, g=g: e.reciprocal(out=den[:, 4 * g:4 * g + 4], in_=den[:, 4 * g:4 * g + 4]),
                        r=["den"], w=["den"])
                    dve(lambda e, g=g, pv=pv: e.tensor_tensor(
                        out=mf[:, g * 256:(g + 1) * 256].rearrange("p (h d) -> p h d", d=64), in0=pv[:, :, 0:64],
                        in1=bc(den[:, 4 * g:4 * g + 4], 2, 64), op=ALU.mult),
                        r=["pb%d" % bank, "den"], w=["mf0"])
                z = zb[par][:]
                zp = zb[1 - par][:]
                dve(lambda e, z=z: e.tensor_tensor(out=z, in0=prsb[:, 256:512], in1=prsb[:, 512:768], op=ALU.mult),
                    r=["prsb1"], w=[("z", par)])
                for j, (cs, cp) in enumerate(((C_S1, C_S1P), (C_S2, C_S2P))):
                    pe(lambda e, j=j, cs=cs, z=z, st=(n == 0): e.matmul(pb[7][:, j * 256:(j + 1) * 256], lhsT=cst[:, cs:cs + 128], rhs=z,
                                                           start=True, stop=st),
                       r=["cst", ("z", par)], w=["pb7"])
                    if n > 0:
                        pe(lambda e, j=j, cp=cp, zp=zp: e.matmul(pb[7][:, j * 256:(j + 1) * 256], lhsT=cst[:, cp:cp + 128],
                                                                 rhs=zp, start=False, stop=True),
                           r=["cst", ("z", 1 - par)], w=["pb7"])
                dve(lambda e: e.tensor_tensor(out=c1[:, :], in0=pb[7][:, 256:512], in1=cwb[:, 0, :], op=ALU.mult),
                    r=["pb7", "cwb"], w=["c1"])
                dve(lambda e: e.tensor_tensor(out=c2[:, :], in0=pb[7][:, 0:256], in1=cwb[:, 1, :], op=ALU.mult),
                    r=["pb7", "cwb"], w=["c2"])
                dve(lambda e: e.tensor_tensor(out=c1[:, :], in0=c1[:, :], in1=c2[:, :], op=ALU.add), r=["c1", "c2"], w=["c1"])
                dve(lambda e, z=z: e.tensor_tensor(out=c2[:, :], in0=z, in1=cwb[:, 2, :], op=ALU.mult),
                    r=[("z", par), "cwb", "c1"], w=["c2"])
                dve(lambda e: e.tensor_tensor(out=c1[:, :], in0=c1[:, :], in1=c2[:, :], op=ALU.add), r=["c1", "c2"], w=["c1"])
                dve(lambda e: e.tensor_tensor(out=mf[:, 512:768], in0=c1[:, :], in1=prsb[:, 0:256], op=ALU.mult),
                    r=["c1", "prsb0"], w=["mf1"])
                for g in range(4):
                    dve(lambda e, g=g: e.bn_stats(out=bst[:, g, :], in_=prsb[:, 1024 + g * 64:1024 + (g + 1) * 64]),
                        r=["prsb2"], w=["bst"])
                for g in range(4):
                    dve(lambda e, g=g: e.bn_aggr(out=mv[:, g, :], in_=bst[:, g, :]), r=["bst"], w=["mv"])
                dve(lambda e: e.tensor_copy(out=vr[:, :], in_=mv[:, :, 1]), r=["mv"], w=["vr"])
                rstd_from_ss(vr[:, 0:4], vr[:, 0:4], 1.0, 4, "vr", "vr", vt[:, 0:4], "vt")
                dve(lambda e: e.tensor_tensor(out=vc[:, :, :], in0=prsb[:, 1024:1280].rearrange("p (g d) -> p g d", g=4),
                                              in1=bc(mv[:, :, 0], 2, 64), op=ALU.subtract), r=["prsb2", "mv"], w=["vc"])
                dve(lambda e: e.tensor_tensor(out=vn[:, :, :], in0=vc[:, :, :], in1=bc(vr[:, 0:4], 2, 64), op=ALU.mult),
                    r=["vc", "vr"], w=["vn"])
                for g in range(4):
                    pe(lambda e, g=g: e.matmul(pb[7][:, g * 64:(g + 1) * 64], lhsT=swT[:, g, :], rhs=vn[:, g, :],
                                               start=True, stop=True), r=["swT", "vn"], w=["pb7"])
                dve(lambda e: e.tensor_tensor(out=vc[:, :, :], in0=pb[7][:, 0:256].rearrange("p (g d) -> p g d", g=4),
                                              in1=bc(sgb[:], 2, 64), op=ALU.add), r=["pb7", "sgb"], w=["vc"])
                dve(lambda e: e.tensor_tensor(out=mf[:, 768:1024], in0=vc[:, :, :].rearrange("p g d -> p (g d)"),
                                              in1=prsb[:, 768:1024], op=ALU.mult), r=["vc", "prsb2"], w=["mf2"])
                for j, (a0, a1) in enumerate(((0, 512), (512, 768), (768, 1024))):
                    act(lambda e, j=j, a0=a0, a1=a1: e.activation(out=A["junkb"][:, a0:a1], in_=mf[:, a0:a1], func=AF.Square,
                                                                  accum_out=s3[:, j:j + 1]),
                        r=["mf%d" % j], w=["junkb", "s3"])
                dve(lambda e: e.tensor_tensor(out=s3[:, 0:3], in0=s3[:, 0:3], in1=cst[:, C_SEGL:C_SEGL + 3], op=ALU.mult),
                    r=["s3", "cst"], w=["s3"])
                rstd_from_ss(s3[:, 0:3], r3s[:, 0:3], 1.0, 3, "s3", "r3s", t3[:, 0:3], "t3")
                for j, (a0, a1) in enumerate(((0, 512), (512, 768), (768, 1024))):
                    act(lambda e, j=j, a0=a0, a1=a1: e.activation(out=mb[:, a0:a1], in_=mf[:, a0:a1], func=AF.Copy,
                                                                  scale=r3s[:, j:j + 1]),
                        r=["mf%d" % j, "r3s"], w=["mb"])
                for k in range(8):
                    pe(lambda e, k=k: e.transpose(out=pb0b[:, k * 128:(k + 1) * 128], in_=mb[:, k * 128:(k + 1) * 128],
                                                  identity=identb[:]), r=["mb", "identb"], w=["pb0"])
                act(lambda e: e.activation(out=mT[:].rearrange("p k t -> p (k t)"), in_=pb0b[:, :], func=AF.Copy),
                    r=["pb0"], w=["mT"])
                for cc in range(2):
                    bank = 1 + cc
                    for k in range(8):
                        pe(lambda e, cc=cc, k=k, bank=bank: e.matmul(pb[bank][:, :], lhsT=mT[:, k, :],
                                                                     rhs=w_out_sb[:, k, cc * 512:(cc + 1) * 512],
                                                                     start=(k == 0), stop=(k == 7)),
                           r=["mT", "w_out"], w=["pb%d" % bank])
                    dve(lambda e, cc=cc, bank=bank: e.tensor_tensor(out=og[:, :], in0=pb[bank][:, :],
                                                                    in1=gB[:, 0, cc * 512:(cc + 1) * 512], op=ALU.mult),
                        r=["pb%d" % bank, "gB"], w=["og"])
                    dve(lambda e, cc=cc, xt=xt: e.tensor_tensor(out=xt[:, cc * 512:(cc + 1) * 512],
                                                                in0=xt[:, cc * 512:(cc + 1) * 512], in1=og[:, :], op=ALU.add),
                        r=["og", xtag], w=[xtag])
                dma("sp", y_out[n * 128:(n + 1) * 128, :], xt, r=[xtag], w=[("yrow", n)], key=("xo", n % 2))

        def phaseB(l):
            ar.reset()
            A = {}
            A["junkb"] = ar.alloc([D], BF16)
            A["ss"] = ar.alloc([4], F32)
            A["rs"] = ar.alloc([4], F32)
            A["xn"] = ar.alloc([D], BF16)
            A["hT"] = ar.alloc([8, 128], BF16)
            h2 = ar.alloc([D], BF16)
            qTp = ar.alloc([16, 128], BF16)
            ssb = ar.alloc([16, 128], F32)
            s2 = ar.alloc([256], F32)
            tops = ar.alloc([16, 16], F32)
            topi = ar.alloc([16, 16], U32)
            topf = ar.alloc([16, 16], F32)
            cand = ar.alloc([8, 256], F32)
            best = ar.alloc([8, 16], F32)
            bpos = ar.alloc([8, 16], U32)
            ab_u = ar.alloc([2, 128], U32)
            ab_f = ar.alloc([2, 128], F32)
            eq = ar.alloc([8, 16, 16], F32)
            ij = ar.alloc([2, 128], F32)
            Ef = ar.alloc([128], F32)
            Eu = ar.alloc([128], U32)
            gd = ar.alloc([8, 16], F32)
            gs = ar.alloc([8], F32)
            gate = ar.alloc([8, 16], F32)
            actv = ar.alloc([128], F32)
            wgt = ar.alloc([128], F32)
            yacc = ar.alloc([D], F32)
            yacc1 = ar.alloc([D], F32)
            junk1 = ar.alloc([D], BF16)
            og = ar.alloc([512], F32)
            for n in range(nt):
                xt, xtag = tile_front(n, y_out, 1, A)
                for k in range(8):
                    pe(lambda e, k=k: e.transpose(out=pb0b[:, k * 128:(k + 1) * 128], in_=A["hT"][:, k, :],
                                                  identity=identb[:]), r=["hT", "identb"], w=["pb0"])
                act(lambda e: e.activation(out=h2[:, :], in_=pb0b[:, :], func=AF.Copy), r=["pb0"], w=["h2"])
                for j in range(16):
                    bank = 1 + j // 4
                    for k in range(8):
                        pe(lambda e, j=j, k=k, bank=bank: e.matmul(pb[bank][:, (j % 4) * 128:(j % 4 + 1) * 128],
                                                                   lhsT=wq_sb[:, k, j * 128:(j + 1) * 128], rhs=A["hT"][:, k, :],
                                                                   start=(k == 0), stop=(k == 7)),
                           r=["wq", "hT"], w=["pb%d" % bank])
                for b4 in range(4):
                    act(lambda e, b4=b4: e.activation(out=qTp[:, b4 * 4:(b4 + 1) * 4, :].rearrange("p j t -> p (j t)"),
                                                      in_=pb[1 + b4][:, :], func=AF.Copy),
                        r=["pb%d" % (1 + b4)], w=[("qTp", b4)])
                for j in range(16):
                    bank = 1 + j // 4
                    pe(lambda e, j=j, bank=bank: e.matmul(pb[bank][:, (j % 4) * 128:(j % 4 + 1) * 128], lhsT=qTp[:, j, :],
                                                          rhs=skT_sb[:, j, :], start=True, stop=True),
                       r=[("qTp", j // 4), "skT"], w=["pb%d" % bank])
                for b4 in range(4):
                    act(lambda e, b4=b4: e.activation(out=ssb[:, b4 * 4:(b4 + 1) * 4, :].rearrange("p j t -> p (j t)"),
                                                      in_=pb[1 + b4][:, :], func=AF.Copy),
                        r=["pb%d" % (1 + b4)], w=[("ssb", b4)])

                def top16(src_ap, srctag, tmp_ap, vals, idxs, vtag, itag):
                    dve(lambda e: e.max(out=vals[:, 0:8], in_=src_ap), r=[srctag], w=[vtag])
                    dve(lambda e: e.max_index(out=idxs[:, 0:8], in_max=vals[:, 0:8], in_values=src_ap),
                        r=[srctag, vtag], w=[itag])
                    dve(lambda e: e.match_replace(out=tmp_ap, in_to_replace=vals[:, 0:8], in_values=src_ap, imm_value=-1e30),
                        r=[srctag, vtag], w=["s2"])
                    dve(lambda e: e.max(out=vals[:, 8:16], in_=tmp_ap), r=["s2"], w=[vtag + "b"])
                    dve(lambda e: e.max_index(out=idxs[:, 8:16], in_max=vals[:, 8:16], in_values=tmp_ap),
                        r=["s2", vtag + "b"], w=[itag + "b"])

                for j in range(16):
                    top16(ssb[:, j, :], ("ssb", j // 4), s2[:, 0:128], tops[:, j, :], topi[:, j, :], "tops", "topi")
                dve(lambda e: e.tensor_copy(out=topf[:, :, :], in_=topi[:, :, :]), r=["topi", "topib"], w=["topf"])
                tv = tops[:, :, :].rearrange("p (h c) k -> p h c k", c=2)
                tf = topf[:, :, :].rearrange("p (h c) k -> p h c k", c=2)
                dve(lambda e: e.tensor_tensor(out=cand[:, :, :].rearrange("p h (a b) -> p h a b", a=16),
                                              in0=bc(tv[:, :, 0, :], 3, 16), in1=bc(tv[:, :, 1, :], 2, 16), op=ALU.add),
                    r=["tops", "topsb"], w=["cand"])
                for h in range(8):
                    top16(cand[:, h, :], "cand", s2[:, 0:256], best[:, h, :], bpos[:, h, :], "best", "bpos")
                bp = bpos[:, :, :].rearrange("p h k -> p (h k)")
                dve(lambda e: e.tensor_single_scalar(out=ab_u[:, 0, :], in_=bp, scalar=4, op=ALU.logical_shift_right),
                    r=["bpos", "bposb"], w=["ab_u"])
                dve(lambda e: e.tensor_single_scalar(out=ab_u[:, 1, :], in_=bp, scalar=15, op=ALU.bitwise_and),
                    r=["bpos", "bposb"], w=["ab_u"])
                dve(lambda e: e.tensor_copy(out=ab_f[:, :, :], in_=ab_u[:, :, :]), r=["ab_u"], w=["ab_f"])
                iota = cst[:, C_IOTA:C_IOTA + 16]
                for c in range(2):
                    abv = ab_f[:, c, :].rearrange("p (h k) -> p h k", h=8)
                    dve(lambda e, abv=abv: e.tensor_tensor(out=eq[:, :, :, :], in0=bc(abv, 3, 16),
                                                           in1=bc(bc(iota, 1, 16), 1, 8), op=ALU.is_equal),
                        r=["ab_f", "cst", "ij"], w=["eq"])
                    dve(lambda e, c=c: e.tensor_tensor(out=eq[:, :, :, :], in0=eq[:, :, :, :], in1=bc(tf[:, :, c, :], 2, 16),
                                                       op=ALU.mult), r=["eq", "topf"], w=["eq"])
                    dve(lambda e, c=c: e.tensor_reduce(out=ij[:, c, :], in_=eq[:, :, :, :].rearrange("p h k a -> p (h k) a"),
                                                       axis=AX.X, op=ALU.add), r=["eq"], w=["ij"])
                dve(lambda e: e.scalar_tensor_tensor(out=Ef[:, :], in0=ij[:, 0, :], scalar=128.0, in1=ij[:, 1, :],
                                                     op0=ALU.mult, op1=ALU.add), r=["ij"], w=["Ef"])
                dve(lambda e: e.tensor_copy(out=Eu[:, :], in_=Ef[:, :]), r=["Ef"], w=["Eu"])
                dve(lambda e: e.tensor_tensor(out=gd[:, :, :], in0=best[:, :, :], in1=bc(best[:, :, 0], 2, 16), op=ALU.subtract),
                    r=["best", "bestb"], w=["gd"])
                act(lambda e: e.activation(out=gd[:, :, :], in_=gd[:, :, :], func=AF.Exp), r=["gd"], w=["gd"])
                dve(lambda e: e.tensor_reduce(out=gs[:, :], in_=gd[:, :, :], axis=AX.X, op=ALU.add), r=["gd"], w=["gs"])
                dve(lambda e: e.reciprocal(out=gs[:, :], in_=gs[:, :]), r=["gs"], w=["gs"])
                dve(lambda e: e.tensor_tensor(out=gate[:, :, :], in0=gd[:, :, :], in1=bc(gs[:, :], 2, 16), op=ALU.mult),
                    r=["gd", "gs"], w=["gate"])
                gflat = gate[:, :, :].rearrange("p h k -> p (h k)")

                def emit_accum(g):
                    sl4 = slice(4 * g, 4 * g + 4)
                    tg_a = ("actvg", g % 4)
                    tg_w = ("wgtg", g % 4)
                    act(lambda e, sl4=sl4: e.activation(out=wgt[:, sl4], in_=actv[:, sl4], func=AF.Gelu), r=[tg_a], w=[tg_w])
                    dve(lambda e, sl4=sl4: e.tensor_tensor(out=wgt[:, sl4], in0=wgt[:, sl4], in1=gflat[:, sl4], op=ALU.mult),
                        r=[tg_w, "gate"], w=[tg_w])
                    for m in range(4 * g, 4 * g + 4):
                        sl = m % NSLOT
                        ya_ = yacc if m % 2 == 0 else yacc1
                        yt_ = "yacc" if m % 2 == 0 else "yacc1"
                        if m < 2:
                            dve(lambda e, m=m, sl=sl, ya_=ya_: e.tensor_scalar(out=ya_[:, :], in0=gsl[:, sl, 1, :], scalar1=wgt[:, m:m + 1],
                                                                              scalar2=None, op0=ALU.mult),
                                r=[("gs", sl), tg_w], w=[yt_])
                        else:
                            dve(lambda e, m=m, sl=sl, ya_=ya_: e.scalar_tensor_tensor(out=ya_[:, :], in0=gsl[:, sl, 1, :],
                                                                                     scalar=wgt[:, m:m + 1], in1=ya_[:, :],
                                                                                     op0=ALU.mult, op1=ALU.add),
                                r=[("gs", sl), tg_w, yt_], w=[yt_])

                for g in range(32):
                    for m in range(4 * g, 4 * g + 4):
                        sl = m % NSLOT
                        P.op("pool", lambda e, m=m, sl=sl: e.indirect_dma_start(
                            out=gslf[:, sl, :], out_offset=None, in_=ptab,
                            in_offset=bass.IndirectOffsetOnAxis(ap=Eu[:, m:m + 1], axis=0)),
                            r=["Eu"], w=[("gs", sl)], dma=("gs", sl))
                    for m in range(4 * g, 4 * g + 4):
                        sl = m % NSLOT
                        jk = A["junkb"] if m % 2 == 0 else junk1
                        dve(lambda e, m=m, sl=sl, jk=jk: e.scalar_tensor_tensor(out=jk, in0=gsl[:, sl, 0, :], scalar=1.0, in1=h2[:, :],
                                                                                op0=ALU.mult, op1=ALU.mult, accum_out=actv[:, m:m + 1]),
                            r=[("gs", sl), "h2"], w=["junkb" if m % 2 == 0 else "junk1", ("actvg", g % 4)])
                    if g >= 1:
                        emit_accum(g - 1)
                emit_accum(31)
                dve(lambda e: e.tensor_tensor(out=yacc[:, :], in0=yacc[:, :], in1=yacc1[:, :], op=ALU.add),
                    r=["yacc", "yacc1"], w=["yacc"])
                for cc in range(2):
                    dve(lambda e, cc=cc: e.tensor_tensor(out=og[:, :], in0=yacc[:, cc * 512:(cc + 1) * 512],
                                                         in1=gB[:, 1, cc * 512:(cc + 1) * 512], op=ALU.mult),
                        r=["yacc", "gB"], w=["og"])
                    dve(lambda e, cc=cc, xt=xt: e.tensor_tensor(out=xt[:, cc * 512:(cc + 1) * 512],
                                                                in0=xt[:, cc * 512:(cc + 1) * 512], in1=og[:, :], op=ALU.add),
                        r=["og", xtag], w=[xtag])
                dma("sp", y_out[n * 128:(n + 1) * 128, :], xt, r=[xtag], w=[("yrow", n)], key=("xo", n % 2))

        for l in range(depth):
            P.barrier()
            layer_params(l)
            phaseA_weights(l)
            P.barrier()
            phaseA(l)
            P.barrier()
            if only_a:
                continue
            phaseB_weights(l)
            P.barrier()
            convert_tables(l)
            P.barrier()
            phaseB(l)
        P.barrier()

        sems = {}
        for e in Prog.ENG:
            sems[e] = es.enter_context(nc.semaphore("s_" + e))
        for i, key in enumerate(P.dcnt):
            sems[key] = es.enter_context(nc.semaphore("d%d" % i))
        block = es.enter_context(nc.Block())

        def make_body(ename):
            def body(eng):
                for o in P.ops[ename]:
                    if o[0] == "w":
                        eng.wait_ge(sems[o[1]], o[2])
                    else:
                        ins = o[1](eng)
                        k = o[2][0]
                        ins.then_inc(sems[k], 16 if isinstance(k, tuple) else 1)
            return body

        block.tensor(make_body("pe"))
        block.scalar(make_body("act"))
        block.vector(make_body("dve"))
        block.gpsimd(make_body("pool"))
        block.sync(make_body("sp"))
    return nc


def _bucket(dist):
    n_buckets, max_distance = 32, 128
    max_exact = n_buckets // 2
    d = np.maximum(dist, 0)
    lr = np.log(np.maximum(d, 1).astype(np.float32) / np.float32(max_exact)) / np.float32(math.log(max_distance / max_exact))
    large = max_exact + (lr.astype(np.float32) * np.float32(n_buckets - max_exact)).astype(np.int32)
    large = np.minimum(large, n_buckets - 1)
    return np.where(d < max_exact, d, large)


def _consts():
    c = np.zeros((128, NCST), np.float32)
    i = np.arange(128)
    c[:, C_ID:C_ID + 128] = np.eye(128, dtype=np.float32)
    tp, t = np.meshgrid(i, i, indexing="ij")
    c[:, C_S1:C_S1 + 128] = (tp == t - 1)
    c[:, C_S2:C_S2 + 128] = (tp == t - 2)
    c[:, C_S1P:C_S1P + 128] = (tp == t + 127)
    c[:, C_S2P:C_S2P + 128] = (tp == t + 126)
    tt, ss = np.meshgrid(i, i, indexing="ij")
    c[:, C_TRIL:C_TRIL + 128] = (ss <= tt)
    s, q = np.meshgrid(i, i, indexing="ij")
    c[:, C_MPREV:C_MPREV + 128] = np.where(s > q, 0.0, -1e30)
    c[:, C_MCUR:C_MCUR + 128] = np.where(s <= q, 0.0, -1e30)
    c[:, C_IOTA:C_IOTA + 16] = np.arange(16, dtype=np.float32)[None, :]
    c[:, C_SEGL:C_SEGL + 3] = np.array([1 / 512., 1 / 256., 1 / 256.], np.float32)[None, :]
    return c


def _bias_index():
    i = np.arange(128)
    s, q = np.meshgrid(i, i, indexing="ij")
    dprev = q + 128 - s
    dcur = q - s
    bp = np.where(s > q, _bucket(dprev), 0)
    bcur = np.where(s <= q, _bucket(dcur), 0)
    return np.stack([bp, bcur], axis=1)


_CACHE = {}


def _prep_shared(inp, depth):
    f = lambda a: np.ascontiguousarray(np.asarray(a, dtype=np.float32))
    rb = f(inp["rel_bias"])
    bidx = _bias_index()
    braw = rb[bidx]
    braw = np.ascontiguousarray(braw.transpose(0, 1, 3, 2)).reshape(128, 2 * 8 * 128)
    b_ada = f(inp["b_ada"])[:depth]
    sh = {
        "cst": _consts(),
        "braw": braw,
        "w_ada": f(inp["w_ada"])[:depth],
        "badaT": np.ascontiguousarray(b_ada.reshape(depth, 48, 128).transpose(0, 2, 1)),
        "badaG": np.ascontiguousarray(np.broadcast_to(
            np.concatenate([b_ada[:, 2048:3072], b_ada[:, 5120:6144]], axis=1)[:, None, :], (depth, 128, 2048))),
        "n1T": np.ascontiguousarray(f(inp["norm1_g"])[:depth].reshape(depth, 8, 128).transpose(0, 2, 1)),
        "n2T": np.ascontiguousarray(f(inp["norm2_g"])[:depth].reshape(depth, 8, 128).transpose(0, 2, 1)),
        "ongT": np.ascontiguousarray(f(inp["out_norm_g"])[:depth].reshape(depth, 8, 128).transpose(0, 2, 1)),
        "w_in": f(inp["w_in"])[:depth],
        "w_out": f(inp["w_out"])[:depth],
        "wq": f(inp["peer_wq"])[:depth],
        "qgB": np.ascontiguousarray(np.broadcast_to(f(inp["q_norm_g"])[:depth, None, :], (depth, 128, 64))),
        "kgB": np.ascontiguousarray(np.broadcast_to(f(inp["k_norm_g"])[:depth, None, :], (depth, 128, 64))),
        "sinkB": np.ascontiguousarray(np.broadcast_to(f(inp["attn_sink"])[:depth, None, :], (depth, 128, 8))),
        "convB": np.ascontiguousarray(np.broadcast_to(f(inp["conv_w"])[:depth].reshape(depth, 1, 768), (depth, 128, 768))),
        "sgu_w": f(inp["sgu_w"])[:depth],
        "sgubT": np.ascontiguousarray(f(inp["sgu_b"])[:depth].transpose(0, 2, 1)),
        "subk": f(inp["peer_sub_keys"])[:depth].reshape(depth, 16, 128, 128),
    }
    ne = 128 if DEBUG_SMALL_TABLES else 16384
    for i in range(depth):
        sh["pdown%d" % i] = np.ascontiguousarray(np.asarray(inp["peer_down"][i], dtype=np.float32)[:ne])
        sh["pup%d" % i] = np.ascontiguousarray(np.asarray(inp["peer_up"][i], dtype=np.float32)[:ne])
    return sh


def run(inp, n_cores=8, nt=SEQ // 128, depth=DEPTH, trace=False, only_a=False):
    key = (nt, depth, only_a)
    if key not in _CACHE:
        _CACHE[key] = build_program(nt, depth, only_a=only_a)
    nc = _CACHE[key]
    sh = _prep_shared(inp, depth)
    x = np.asarray(inp["x"], dtype=np.float32)
    c = np.asarray(inp["c"], dtype=np.float32)
    in_maps = []
    for b in range(n_cores):
        m = dict(sh)
        m["x"] = np.ascontiguousarray(x[b, :nt * 128])
        m["cT"] = np.ascontiguousarray(c[b].reshape(8, 128).T)
        in_maps.append(m)
    res = run_bass_kernel_spmd(nc, in_maps, core_ids=list(range(n_cores)), **({"trace": True} if trace else {}))
    return np.stack([r["y"] for r in res.results], axis=0), res


def kernel(**inputs):
    out, _ = run(inputs)
    return out.astype(np.float32)
```

```python
import math
from contextlib import ExitStack
import numpy as np
import concourse.bass as bass
import concourse.mybir as mybir
from concourse.bass_utils import run_bass_kernel_spmd

F32 = mybir.dt.float32
BF16 = mybir.dt.bfloat16
U32 = mybir.dt.uint32
ALU = mybir.AluOpType
AF = mybir.ActivationFunctionType
AX = mybir.AxisListType

D = 1024
DEPTH = 4
SEQ = 4096
EPS = 1e-6
NSLOT = 12
SAME_ENGINE_SYNC = True
DEBUG_MAXOPS = None
DEBUG_SKIP = set()
DEBUG_SMALL_TABLES = False

C_ID = 0
C_S1 = 128
C_S2 = 256
C_S1P = 384
C_S2P = 512
C_TRIL = 640
C_MPREV = 768
C_MCUR = 896
C_IOTA = 1024
C_SEGL = 1040
NCST = 1048


class Prog:
    ENG = ("pe", "act", "dve", "pool", "sp")

    def __init__(self):
        self.ops = {e: [] for e in self.ENG}
        self.cnt = {e: 0 for e in self.ENG}
        self.known = {e: {} for e in self.ENG}
        self.tags = {}
        self.dcnt = {}

    def _dep(self, eng, tok, raw):
        k, v = tok
        if k == eng:
            if eng == "pe" or not SAME_ENGINE_SYNC:
                return
        if self.known[eng].get(k, 0) >= v:
            return
        self.known[eng][k] = v
        self.ops[eng].append(("w", k, v))

    def op(self, eng, fn, r=(), w=(), dma=None):
        self.nops = getattr(self, 'nops', 0) + 1
        if DEBUG_MAXOPS is not None and self.nops > DEBUG_MAXOPS:
            return
        if self.nops in DEBUG_SKIP:
            return
        for t in r:
            st = self.tags.get(t)
            if st and st["w"]:
                self._dep(eng, st["w"], True)
        for t in w:
            st = self.tags.get(t)
            if st:
                if st["w"]:
                    self._dep(eng, st["w"], False)
                for k, v in st["r"].items():
                    self._dep(eng, (k, v), False)
        if dma is None:
            self.cnt[eng] += 1
            tok = (eng, self.cnt[eng])
        else:
            key = ("dma", dma)
            self.dcnt[key] = self.dcnt.get(key, 0) + 16
            tok = (key, self.dcnt[key])
        self.ops[eng].append(("i", fn, tok))
        for t in w:
            self.tags[t] = {"w": tok, "r": {}}
        for t in r:
            st = self.tags.setdefault(t, {"w": None, "r": {}})
            st["r"][tok[0]] = max(st["r"].get(tok[0], 0), tok[1])

    def barrier(self):
        for e in self.ENG:
            for f in self.ENG:
                if f != e and self.cnt[f] > 0:
                    self._dep(e, (f, self.cnt[f]), True)
            for key, v in self.dcnt.items():
                self._dep(e, (key, v), True)


ARENA_LOG = []


class Arena:
    def __init__(self, ap_f32, nwords):
        self.ap = ap_f32
        self.n = nwords
        self.off = 0

    def reset(self):
        self.off = 0

    def alloc(self, free_shape, dtype):
        nel = int(np.prod(free_shape))
        esz = 2 if dtype == BF16 else 4
        nw = (nel * esz + 3) // 4
        nw = (nw + 7) // 8 * 8
        assert self.off + nw <= self.n, ("arena overflow", self.off, nw, self.n)
        v = self.ap[:, self.off:self.off + nw]
        ARENA_LOG.append((self.off, nel, dtype, tuple(free_shape)))
        self.off += nw
        if dtype != F32:
            v = v.bitcast(dtype)
        v = v[:, 0:nel]
        if len(free_shape) == 2:
            v = v.rearrange("p (a b) -> p a b", a=free_shape[0])
        elif len(free_shape) == 3:
            v = v.rearrange("p (a b c) -> p a b c", a=free_shape[0], b=free_shape[1])
        return v


def bc(ap, pos, n):
    u = ap.unsqueeze(pos)
    shp = list(u.shape)
    shp[pos] = n
    return u.broadcast_to(shp)


def build_program(nt, depth, only_a=False):
    S = nt * 128
    nc = bass.Bass("TRN2", target_bir_lowering=False)

    def din(name, shape, dt=F32):
        return nc.dram_tensor(name, list(shape), dt, kind="ExternalInput").ap()

    x_in = din("x", [S, D])
    cT_in = din("cT", [128, 8])
    cst_in = din("cst", [128, NCST])
    braw_in = din("braw", [128, 2 * 8 * 128])
    w_ada = din("w_ada", [depth, D, 6 * D])
    badaT = din("badaT", [depth, 128, 48])
    badaG = din("badaG", [depth, 128, 2 * D])
    n1T = din("n1T", [depth, 128, 8])
    n2T = din("n2T", [depth, 128, 8])
    ongT = din("ongT", [depth, 128, 8])
    w_in = din("w_in", [depth, D, 2048])
    w_out = din("w_out", [depth, D, D])
    wq = din("wq", [depth, D, 2048])
    qgB = din("qgB", [depth, 128, 64])
    kgB = din("kgB", [depth, 128, 64])
    sinkB = din("sinkB", [depth, 128, 8])
    convB = din("convB", [depth, 128, 768])
    sgu_w = din("sgu_w", [depth, 4, 128, 128])
    sgubT = din("sgubT", [depth, 128, 4])
    subk = din("subk", [depth, 16, 128, 128])
    NE = 128 if DEBUG_SMALL_TABLES else 16384
    pdown = [din("pdown%d" % i, [NE, D]) for i in range(depth)]
    pup = [din("pup%d" % i, [NE, D]) for i in range(depth)]
    y_out = nc.dram_tensor("y", [S, D], F32, kind="ExternalOutput").ap()
    ptab = nc.dram_tensor("ptab_bf", [NE, 2 * D], BF16).ap()

    P = Prog()
    es = ExitStack()
    with es:
        def sb(name, shape, dt=F32):
            return es.enter_context(nc.sbuf_tensor(name, list(shape), dt))

        def ps(name):
            return es.enter_context(nc.psum_tensor(name, [128, 512], F32))

        R1W = 12288
        r1 = sb("r1", [128, R1W])
        GW = NSLOT * 512
        gsl2 = sb("gsl", [128, NSLOT * 2 * D], BF16)
        gsl = gsl2[:].rearrange("p (s c d) -> p s c d", s=NSLOT, c=2)
        gslf = gsl2[:].rearrange("p (s d) -> p s d", s=NSLOT)
        stg = [gsl2[:].bitcast(F32)[:, 0:2048], gsl2[:].bitcast(F32)[:, 2048:4096]]
        cst = sb("cst_sb", [128, NCST])
        identb = sb("identb", [128, 128], BF16)
        biasT = sb("biasT", [128, 2, 8, 128])
        R3W = 16 * 1024
        r3 = sb("r3", [128, R3W])
        cact = sb("cact", [128, 8])
        cbc = sb("cbc", [128, 8, 128])
        modT = sb("modT", [128, 4, 8])
        bT = sb("bT", [128, 48])
        nT = sb("nT", [128, 3, 8])
        sT = sb("sT", [128, 2, 8])
        gB = sb("gB", [128, 2, D])
        qg8 = sb("qg8", [128, 64])
        kgb = sb("kgb", [128, 64])
        esink = sb("esink", [128, 8])
        cwb = sb("cwb", [128, 3, 256])
        swT = sb("swT", [128, 4, 128], BF16)
        sgb = sb("sgb", [128, 4])
        xbuf = [sb("xbuf0", [128, D]), sb("xbuf1", [128, D])]
        kTb = [sb("kT0", [64, 2, 128], BF16), sb("kT1", [64, 2, 128], BF16)]
        vaug = [sb("va0", [128, 2, 80], BF16), sb("va1", [128, 2, 80], BF16)]
        zb = [sb("z0", [128, 256]), sb("z1", [128, 256])]
        pb = [ps("pb%d" % i) for i in range(8)]
        pb0b = pb[0][:].bitcast(BF16)

        w_in_sb = r1[:, 0:8192].bitcast(BF16).rearrange("p (k n) -> p k n", k=8)
        w_out_sb = r1[:, 8192:12288].bitcast(BF16).rearrange("p (k n) -> p k n", k=8)
        wq_sb = w_in_sb
        skT_sb = r1[:, 8192:9216].bitcast(BF16).rearrange("p (j n) -> p j n", j=16)

        ar = Arena(r3[:], R3W)

        def dve(fn, r=(), w=()):
            P.op("dve", fn, r, w)

        def act(fn, r=(), w=()):
            P.op("act", fn, r, w)

        def pe(fn, r=(), w=()):
            P.op("pe", fn, r, w)

        def dma(eng, out, in_, r=(), w=(), key=None):
            P.op(eng, lambda e, out=out, in_=in_: e.dma_start(out=out, in_=in_), r, w, dma=key)

        def rstd_from_ss(ss, out, scale, n, tagss, tagout, tmp, tagtmp):
            dve(lambda e: e.tensor_scalar(out=tmp, in0=ss, scalar1=scale, scalar2=EPS, op0=ALU.mult, op1=ALU.add),
                r=[tagss], w=[tagtmp])
            act(lambda e: e.activation(out=tmp, in_=tmp, func=AF.Sqrt), r=[tagtmp], w=[tagtmp])
            dve(lambda e: e.reciprocal(out=out, in_=tmp), r=[tagtmp], w=[tagout])

        dma("sp", cst[:], cst_in, w=["cst"], key="c0")
        dma("sp", biasT[:].rearrange("p a h q -> p (a h q)"), braw_in, w=["biasT"], key="c1")
        dma("sp", cact[:], cT_in, w=["cact"], key="c2")
        dve(lambda e: e.tensor_copy(out=identb[:], in_=cst[:, C_ID:C_ID + 128]), r=["cst"], w=["identb"])
        for a, cm in ((0, C_MPREV), (1, C_MCUR)):
            dve(lambda e, a=a, cm=cm: e.tensor_tensor(out=biasT[:, a], in0=biasT[:, a],
                                                       in1=bc(cst[:, cm:cm + 128], 1, 8), op=ALU.add),
                r=["cst", "biasT"], w=["biasT"])
        act(lambda e: e.activation(out=cact[:], in_=cact[:], func=AF.Silu), r=["cact"], w=["cact"])
        dve(lambda e: e.tensor_copy(out=cbc[:], in_=bc(cact[:], 2, 128)), r=["cact"], w=["cbc"])
        for par in range(2):
            dve(lambda e, par=par: e.memset(vaug[par][:], 1.0), w=[("va", par)])

        ident32 = cst[:, C_ID:C_ID + 128]

        def load_convert(dst_fn, src_rows_fn, nrows_chunks, ncols, scale_fn=None, wtag=None):
            i = 0
            for k in range(nrows_chunks):
                for c0 in range(0, ncols, 2048):
                    c1 = min(ncols, c0 + 2048)
                    s = stg[i % 2]
                    tg = ("stg", i % 2)
                    dma("sp", s[:, 0:c1 - c0], src_rows_fn(k)[:, c0:c1], w=[tg], key=("stg", i % 2))
                    eng = "dve" if i % 2 == 0 else "act"
                    if scale_fn is None:
                        if eng == "dve":
                            dve(lambda e, s=s, k=k, c0=c0, c1=c1: e.tensor_copy(out=dst_fn(k, c0, c1), in_=s[:, 0:c1 - c0]),
                                r=[tg], w=[wtag])
                        else:
                            act(lambda e, s=s, k=k, c0=c0, c1=c1: e.activation(out=dst_fn(k, c0, c1), in_=s[:, 0:c1 - c0], func=AF.Copy),
                                r=[tg], w=[wtag])
                    else:
                        dve(lambda e, s=s, k=k, c0=c0, c1=c1: e.tensor_scalar(out=dst_fn(k, c0, c1), in0=s[:, 0:c1 - c0],
                                                                              scalar1=scale_fn(k), scalar2=None, op0=ALU.mult),
                            r=[tg, "nT"], w=[wtag])
                    i += 1

        def layer_params(l):
            dma("sp", bT[:], badaT[l], w=["bT"], key="p0")
            dma("sp", nT[:, 0, :], n1T[l], w=["nT"], key="p1")
            dma("sp", nT[:, 1, :], n2T[l], w=["nT"], key="p1")
            dma("sp", nT[:, 2, :], ongT[l], w=["nT"], key="p1")
            dma("sp", gB[:].rearrange("p a d -> p (a d)"), badaG[l], w=["gB"], key="p2")
            dma("sp", qg8[:], qgB[l], w=["qg8"], key="p3")
            dma("sp", kgb[:], kgB[l], w=["kgb"], key="p4")
            dma("sp", esink[:], sinkB[l], w=["esink"], key="p5")
            dma("sp", cwb[:].rearrange("p a c -> p (a c)"), convB[l], w=["cwb"], key="p6")
            dma("sp", sgb[:], sgubT[l], w=["sgb"], key="p7")
            dve(lambda e: e.tensor_scalar(out=qg8[:], in0=qg8[:], scalar1=0.125, scalar2=None, op0=ALU.mult),
                r=["qg8"], w=["qg8"])
            act(lambda e: e.activation(out=esink[:], in_=esink[:], func=AF.Exp), r=["esink"], w=["esink"])
            fm_slot = {0: 0, 1: 1, 3: 2, 4: 3}
            for ci in range(24):
                seg = ci // 4
                s = stg[ci % 2]
                tg = ("stg", ci % 2)
                sv = s.rearrange("p (k n) -> p k n", k=8)
                dma("sp", sv, w_ada[l, :, ci * 256:(ci + 1) * 256].rearrange("(k p) n -> p k n", p=128),
                    w=[tg], key=("stg", ci % 2))
                if seg in fm_slot:
                    for half in range(2):
                        col = (ci % 4) * 2 + half
                        for k in range(8):
                            pe(lambda e, sv=sv, k=k, half=half: e.matmul(
                                pb[7][:, 0:1], lhsT=sv[:, k, half * 128:(half + 1) * 128], rhs=cact[:, k:k + 1],
                                start=(k == 0), stop=(k == 7)), r=[tg, "cact"], w=["pb7"])
                        dve(lambda e, seg=seg, col=col: e.tensor_tensor(
                            out=modT[:, fm_slot[seg], col:col + 1], in0=pb[7][:, 0:1],
                            in1=bT[:, seg * 8 + col:seg * 8 + col + 1], op=ALU.add),
                            r=["pb7", "bT"], w=["modT"])
                else:
                    gi = 0 if seg == 2 else 1
                    cc0 = (ci % 4) * 256
                    for k in range(8):
                        pe(lambda e, sv=sv, k=k: e.matmul(pb[6][:, 0:256], lhsT=cbc[:, k, :], rhs=sv[:, k, :],
                                                          start=(k == 0), stop=(k == 7)), r=[tg, "cbc"], w=["pb6"])
                    dve(lambda e, gi=gi, cc0=cc0: e.tensor_tensor(out=gB[:, gi, cc0:cc0 + 256], in0=pb[6][:, 0:256],
                                                                   in1=gB[:, gi, cc0:cc0 + 256], op=ALU.add),
                        r=["pb6", "gB"], w=["gB"])
            for j, slot in ((0, 1), (1, 3)):
                dve(lambda e, j=j, slot=slot: e.scalar_tensor_tensor(out=sT[:, j, :], in0=modT[:, slot, :], scalar=1.0,
                                                                     in1=nT[:, j, :], op0=ALU.add, op1=ALU.mult),
                    r=["modT", "nT"], w=["sT"])

        def phaseA_weights(l):
            load_convert(lambda k, c0, c1: w_in_sb[:, k, c0:c1],
                         lambda k: w_in[l, k * 128:(k + 1) * 128, :], 8, 2048, wtag="w_in")
            load_convert(lambda k, c0, c1: w_out_sb[:, k, c0:c1],
                         lambda k: w_out[l, k * 128:(k + 1) * 128, :], 8, 1024,
                         scale_fn=lambda k: nT[:, 2, k:k + 1], wtag="w_out")
            s = stg[0]
            sv = s[:, 0:512].rearrange("p (g s) -> p g s", g=4)
            dma("sp", sv, sgu_w[l].rearrange("g t s -> t g s"), w=[("stg", 0)], key=("stg", 0))
            dve(lambda e: e.tensor_tensor(out=sv, in0=sv, in1=bc(cst[:, C_TRIL:C_TRIL + 128], 1, 4), op=ALU.mult),
                r=[("stg", 0), "cst"], w=[("stg", 0)])
            for g in range(4):
                pe(lambda e, g=g: e.transpose(out=pb[7][:, g * 128:(g + 1) * 128], in_=sv[:, g, :], identity=ident32),
                   r=[("stg", 0), "cst"], w=["pb7"])
            dve(lambda e: e.tensor_copy(out=swT[:].rearrange("p g t -> p (g t)"), in_=pb[7][:, 0:512]),
                r=["pb7"], w=["swT"])

        def phaseB_weights(l):
            load_convert(lambda k, c0, c1: wq_sb[:, k, c0:c1],
                         lambda k: wq[l, k * 128:(k + 1) * 128, :], 8, 2048, wtag="wq")
            for jb in range(4):
                s = stg[jb % 2]
                tg = ("stg", jb % 2)
                sv = s[:, 0:512].rearrange("p (j k) -> p j k", j=4)
                dma("sp", sv, subk[l, jb * 4:(jb + 1) * 4].rearrange("j n k -> n j k"), w=[tg], key=("stg", jb % 2))
                for jj in range(4):
                    pe(lambda e, sv=sv, jj=jj: e.transpose(out=pb[7][:, jj * 128:(jj + 1) * 128], in_=sv[:, jj, :],
                                                           identity=ident32), r=[tg, "cst"], w=["pb7"])
                dve(lambda e, jb=jb: e.tensor_copy(out=skT_sb[:, jb * 4:(jb + 1) * 4, :].rearrange("p j n -> p (j n)"),
                                                   in_=pb[7][:, 0:512]), r=["pb7"], w=["skT"])

        def convert_tables(l):
            cbuf = [gsl2[:, 8192:10240], gsl2[:, 10240:12288]]
            jobs = []
            for src_t, c0 in ((pdown[l], 0), (pup[l], D)):
                for c in range(NE // 256):
                    jobs.append((src_t[c * 256:(c + 1) * 256, :].rearrange("(p r) d -> p (r d)", r=2),
                                 ptab[c * 256:(c + 1) * 256, c0:c0 + D].rearrange("(p r) d -> p r d", r=2)))

            def load(i):
                dma("sp", stg[i % 2], jobs[i][0], w=[("stg", i % 2)], key=("stg", i % 2))

            load(0)
            if len(jobs) > 1:
                load(1)
            for i in range(len(jobs)):
                b = i % 2
                if i % 2 == 0:
                    act(lambda e, b=b: e.activation(out=cbuf[b], in_=stg[b], func=AF.Copy), r=[("stg", b)], w=[("cb", b)])
                else:
                    dve(lambda e, b=b: e.tensor_copy(out=cbuf[b], in_=stg[b]), r=[("stg", b)], w=[("cb", b)])
                dma("sp", jobs[i][1], cbuf[b].rearrange("p (r d) -> p r d", r=2), r=[("cb", b)], w=[("tabrow", i)], key=("cbo", b))
                if i + 2 < len(jobs):
                    load(i + 2)

        def tile_front(n, src, which, A):
            xt = xbuf[n % 2][:]
            xtag = ("x", n % 2)
            dma("sp", xt, src[n * 128:(n + 1) * 128, :], w=[xtag], key=("x", n % 2))
            act(lambda e: e.activation(out=A["junkb"], in_=xt, func=AF.Square, accum_out=A["ss"][:, 0:1]),
                r=[xtag], w=["junkb", "ss"])
            rstd_from_ss(A["ss"][:, 0:1], A["rs"][:, 0:1], 1.0 / D, 1, "ss", "rs", A["ss"][:, 1:2], "ss1")
            act(lambda e: e.activation(out=A["xn"], in_=xt, func=AF.Copy, scale=A["rs"][:, 0:1]),
                r=[xtag, "rs"], w=["xn"])
            for k in range(8):
                pe(lambda e, k=k: e.transpose(out=pb0b[:, k * 128:(k + 1) * 128], in_=A["xn"][:, k * 128:(k + 1) * 128],
                                              identity=identb[:]), r=["xn", "identb"], w=["pb0"])
            sh_slot = 0 if which == 0 else 2
            for k in range(8):
                dve(lambda e, k=k: e.tensor_scalar(out=A["hT"][:, k, :], in0=pb0b[:, k * 128:(k + 1) * 128],
                                                   scalar1=sT[:, which, k:k + 1], scalar2=modT[:, sh_slot, k:k + 1],
                                                   op0=ALU.mult, op1=ALU.add),
                    r=["pb0", "sT", "modT"], w=["hT"])
            return xt, xtag

        def phaseA(l):
            ar.reset()
            A = {}
            A["junkb"] = ar.alloc([D], BF16)
            A["ss"] = ar.alloc([4], F32)
            A["rs"] = ar.alloc([4], F32)
            A["xn"] = ar.alloc([D], BF16)
            A["hT"] = ar.alloc([8, 128], BF16)
            prsb = ar.alloc([1280], F32)
            sq = ar.alloc([640], F32)
            ssq = ar.alloc([16], F32)
            rq = ar.alloc([16], F32)
            tq = ar.alloc([16], F32)
            qn = ar.alloc([640], BF16)
            qtmp = ar.alloc([640], F32)
            qT = ar.alloc([8, 128], BF16)
            lg = ar.alloc([512], F32)
            PT = ar.alloc([2, 8, 128], BF16)
            mf = ar.alloc([D], F32)
            mb = ar.alloc([D], BF16)
            mT = ar.alloc([8, 128], BF16)
            den = ar.alloc([8], F32)
            c1 = ar.alloc([256], F32)
            c2 = ar.alloc([256], F32)
            bst = ar.alloc([4, 6], F32)
            mv = ar.alloc([4, 2], F32)
            vr = ar.alloc([4], F32)
            vt = ar.alloc([4], F32)
            vc = ar.alloc([4, 64], F32)
            vn = ar.alloc([4, 64], BF16)
            s3 = ar.alloc([4], F32)
            r3s = ar.alloc([4], F32)
            t3 = ar.alloc([4], F32)
            og = ar.alloc([512], F32)
            src = x_in if l == 0 else y_out
            for n in range(nt):
                par = n % 2
                xt, xtag = tile_front(n, src, 0, A)
                for cc in range(4):
                    for k in range(8):
                        pe(lambda e, cc=cc, k=k: e.matmul(pb[1 + cc][:, :], lhsT=A["hT"][:, k, :],
                                                          rhs=w_in_sb[:, k, cc * 512:(cc + 1) * 512],
                                                          start=(k == 0), stop=(k == 7)),
                           r=["hT", "w_in"], w=["pb%d" % (1 + cc)])
                act(lambda e: e.activation(out=prsb[:, 0:256], in_=pb[2][:, 256:512], func=AF.Copy), r=["pb2"], w=["prsb0"])
                act(lambda e: e.activation(out=prsb[:, 256:768], in_=pb[3][:, :], func=AF.Copy), r=["pb3"], w=["prsb1"])
                act(lambda e: e.activation(out=prsb[:, 768:1280], in_=pb[4][:, :], func=AF.Copy), r=["pb4"], w=["prsb2"])
                for g in range(2):
                    act(lambda e, par=par, g=g: e.activation(out=vaug[par][:, g, 0:64],
                                                             in_=pb[2][:, 128 + g * 64:128 + (g + 1) * 64], func=AF.Copy),
                        r=["pb2"], w=[("va", par)])
                act(lambda e: e.activation(out=sq[:, 0:512], in_=pb[1][:, :], func=AF.Square), r=["pb1"], w=["sq"])
                act(lambda e: e.activation(out=sq[:, 512:640], in_=pb[2][:, 0:128], func=AF.Square), r=["pb2"], w=["sq"])
                dve(lambda e: e.tensor_reduce(out=ssq[:, 0:10], in_=sq[:, 0:640].rearrange("p (h d) -> p h d", d=64),
                                              axis=AX.X, op=ALU.add), r=["sq"], w=["ssq"])
                rstd_from_ss(ssq[:, 0:10], rq[:, 0:10], 1.0 / 64, 10, "ssq", "rq", tq[:, 0:10], "tq")
                dve(lambda e: e.tensor_tensor(out=qtmp[:, 0:512].rearrange("p (h d) -> p h d", d=64),
                                              in0=pb[1][:, :].rearrange("p (h d) -> p h d", d=64),
                                              in1=bc(rq[:, 0:8], 2, 64), op=ALU.mult), r=["pb1", "rq"], w=["qtmp"])
                dve(lambda e: e.tensor_tensor(out=qtmp[:, 512:640].rearrange("p (h d) -> p h d", d=64),
                                              in0=pb[2][:, 0:128].rearrange("p (h d) -> p h d", d=64),
                                              in1=bc(rq[:, 8:10], 2, 64), op=ALU.mult), r=["pb2", "rq"], w=["qtmp"])
                dve(lambda e: e.tensor_tensor(out=qn[:, 0:512].rearrange("p (h d) -> p h d", d=64),
                                              in0=qtmp[:, 0:512].rearrange("p (h d) -> p h d", d=64),
                                              in1=bc(qg8[:], 1, 8), op=ALU.mult), r=["qtmp", "qg8"], w=["qn"])
                dve(lambda e: e.tensor_tensor(out=qn[:, 512:640].rearrange("p (h d) -> p h d", d=64),
                                              in0=qtmp[:, 512:640].rearrange("p (h d) -> p h d", d=64),
                                              in1=bc(kgb[:], 1, 2), op=ALU.mult), r=["qtmp", "kgb"], w=["qn"])
                for h in range(8):
                    pe(lambda e, h=h: e.transpose(out=pb0b[0:64, h * 128:(h + 1) * 128], in_=qn[:, h * 64:(h + 1) * 64],
                                                  identity=identb[:]), r=["qn", "identb"], w=["pb0"])
                act(lambda e: e.activation(out=qT[0:64].rearrange("p h t -> p (h t)"), in_=pb0b[0:64, :], func=AF.Copy),
                    r=["pb0"], w=["qT"])
                for g in range(2):
                    pe(lambda e, g=g: e.transpose(out=pb0b[0:64, g * 128:(g + 1) * 128],
                                                  in_=qn[:, 512 + g * 64:512 + (g + 1) * 64], identity=identb[:]),
                       r=["qn", "identb"], w=["pb0"])
                act(lambda e, par=par: e.activation(out=kTb[par][:].rearrange("p g t -> p (g t)"), in_=pb0b[0:64, 0:256],
                                                    func=AF.Copy), r=["pb0"], w=[("kT", par)])
                whichs = (0, 1) if n > 0 else (1,)
                for g in range(2):
                    for a in whichs:
                        kp = par if a == 1 else 1 - par
                        bank = 1 + g * 2 + a
                        pe(lambda e, g=g, kp=kp, bank=bank: e.matmul(
                            pb[bank][:, :], lhsT=kTb[kp][:, g, :],
                            rhs=qT[0:64, 4 * g:4 * g + 4, :].rearrange("p h t -> p (h t)"), start=True, stop=True),
                            r=[("kT", kp), "qT"], w=["pb%d" % bank])
                        dve(lambda e, g=g, a=a, bank=bank: e.tensor_tensor(
                            out=lg[:, :], in0=pb[bank][:, :],
                            in1=biasT[:, a, 4 * g:4 * g + 4, :].rearrange("p h q -> p (h q)"), op=ALU.add),
                            r=["pb%d" % bank, "biasT"], w=["lg"])
                        act(lambda e, g=g, a=a: e.activation(
                            out=PT[:, a, 4 * g:4 * g + 4, :].rearrange("p h q -> p (h q)"), in_=lg[:, :], func=AF.Exp),
                            r=["lg"], w=[("PT", a, g)])
                for h in range(8):
                    g = h // 4
                    bank = 5 + g
                    o0 = (h % 4) * 65
                    for a in whichs:
                        kp = par if a == 1 else 1 - par
                        pe(lambda e, h=h, g=g, a=a, kp=kp, bank=bank, o0=o0, st=(a == whichs[0]): e.matmul(
                            pb[bank][:, o0:o0 + 65], lhsT=PT[:, a, h, :], rhs=vaug[kp][:, g, 0:65],
                            start=st, stop=(a == 1)),
                            r=[("PT", a, g), ("va", kp)], w=["pb%d" % bank])
                for g in range(2):
                    bank = 5 + g
                    pv = pb[bank][:, 0:260].rearrange("p (h c) -> p h c", c=65)
                    dve(lambda e, g=g, pv=pv: e.tensor_tensor(out=den[:, 4 * g:4 * g + 4], in0=pv[:, :, 64],
                                                              in1=esink[:, 4 * g:4 * g + 4], op=ALU.add),
                        r=["pb%d" % bank, "esink"], w=["den"])
                    dve(lambda e, g=g: e.reciprocal(out=den[:, 4 * g:4 * g + 4], in_=den[:, 4 * g:4 * g + 4]),
                        r=["den"], w=["den"])
                    dve(lambda e, g=g, pv=pv: e.tensor_tensor(
                        out=mf[:, g * 256:(g + 1) * 256].rearrange("p (h d) -> p h d", d=64), in0=pv[:, :, 0:64],
                        in1=bc(den[:, 4 * g:4 * g + 4], 2, 64), op=ALU.mult),
                        r=["pb%d" % bank, "den"], w=["mf0"])
                z = zb[par][:]
                zp = zb[1 - par][:]
                dve(lambda e, z=z: e.tensor_tensor(out=z, in0=prsb[:, 256:512], in1=prsb[:, 512:768], op=ALU.mult),
                    r=["prsb1"], w=[("z", par)])
                for j, (cs, cp) in enumerate(((C_S1, C_S1P), (C_S2, C_S2P))):
                    pe(lambda e, j=j, cs=cs, z=z, st=(n == 0): e.matmul(pb[7][:, j * 256:(j + 1) * 256], lhsT=cst[:, cs:cs + 128], rhs=z,
                                                           start=True, stop=st),
                       r=["cst", ("z", par)], w=["pb7"])
                    if n > 0:
                        pe(lambda e, j=j, cp=cp, zp=zp: e.matmul(pb[7][:, j * 256:(j + 1) * 256], lhsT=cst[:, cp:cp + 128],
                                                                 rhs=zp, start=False, stop=True),
                           r=["cst", ("z", 1 - par)], w=["pb7"])
                dve(lambda e: e.tensor_tensor(out=c1[:, :], in0=pb[7][:, 256:512], in1=cwb[:, 0, :], op=ALU.mult),
                    r=["pb7", "cwb"], w=["c1"])
                dve(lambda e: e.tensor_tensor(out=c2[:, :], in0=pb[7][:, 0:256], in1=cwb[:, 1, :], op=ALU.mult),
                    r=["pb7", "cwb"], w=["c2"])
                dve(lambda e: e.tensor_tensor(out=c1[:, :], in0=c1[:, :], in1=c2[:, :], op=ALU.add), r=["c1", "c2"], w=["c1"])
                dve(lambda e, z=z: e.tensor_tensor(out=c2[:, :], in0=z, in1=cwb[:, 2, :], op=ALU.mult),
                    r=[("z", par), "cwb", "c1"], w=["c2"])
                dve(lambda e: e.tensor_tensor(out=c1[:, :], in0=c1[:, :], in1=c2[:, :], op=ALU.add), r=["c1", "c2"], w=["c1"])
                dve(lambda e: e.tensor_tensor(out=mf[:, 512:768], in0=c1[:, :], in1=prsb[:, 0:256], op=ALU.mult),
                    r=["c1", "prsb0"], w=["mf1"])
                for g in range(4):
                    dve(lambda e, g=g: e.bn_stats(out=bst[:, g, :], in_=prsb[:, 1024 + g * 64:1024 + (g + 1) * 64]),
                        r=["prsb2"], w=["bst"])
                for g in range(4):
                    dve(lambda e, g=g: e.bn_aggr(out=mv[:, g, :], in_=bst[:, g, :]), r=["bst"], w=["mv"])
                dve(lambda e: e.tensor_copy(out=vr[:, :], in_=mv[:, :, 1]), r=["mv"], w=["vr"])
                rstd_from_ss(vr[:, 0:4], vr[:, 0:4], 1.0, 4, "vr", "vr", vt[:, 0:4], "vt")
                dve(lambda e: e.tensor_tensor(out=vc[:, :, :], in0=prsb[:, 1024:1280].rearrange("p (g d) -> p g d", g=4),
                                              in1=bc(mv[:, :, 0], 2, 64), op=ALU.subtract), r=["prsb2", "mv"], w=["vc"])
                dve(lambda e: e.tensor_tensor(out=vn[:, :, :], in0=vc[:, :, :], in1=bc(vr[:, 0:4], 2, 64), op=ALU.mult),
                    r=["vc", "vr"], w=["vn"])
                for g in range(4):
                    pe(lambda e, g=g: e.matmul(pb[7][:, g * 64:(g + 1) * 64], lhsT=swT[:, g, :], rhs=vn[:, g, :],
                                               start=True, stop=True), r=["swT", "vn"], w=["pb7"])
                dve(lambda e: e.tensor_tensor(out=vc[:, :, :], in0=pb[7][:, 0:256].rearrange("p (g d) -> p g d", g=4),
                                              in1=bc(sgb[:], 2, 64), op=ALU.add), r=["pb7", "sgb"], w=["vc"])
                dve(lambda e: e.tensor_tensor(out=mf[:, 768:1024], in0=vc[:, :, :].rearrange("p g d -> p (g d)"),
                                              in1=prsb[:, 768:1024], op=ALU.mult), r=["vc", "prsb2"], w=["mf2"])
                for j, (a0, a1) in enumerate(((0, 512), (512, 768), (768, 1024))):
                    act(lambda e, j=j, a0=a0, a1=a1: e.activation(out=A["junkb"][:, a0:a1], in_=mf[:, a0:a1], func=AF.Square,
                                                                  accum_out=s3[:, j:j + 1]),
                        r=["mf%d" % j], w=["junkb", "s3"])
                dve(lambda e: e.tensor_tensor(out=s3[:, 0:3], in0=s3[:, 0:3], in1=cst[:, C_SEGL:C_SEGL + 3], op=ALU.mult),
                    r=["s3", "cst"], w=["s3"])
                rstd_from_ss(s3[:, 0:3], r3s[:, 0:3], 1.0, 3, "s3", "r3s", t3[:, 0:3], "t3")
                for j, (a0, a1) in enumerate(((0, 512), (512, 768), (768, 1024))):
                    act(lambda e, j=j, a0=a0, a1=a1: e.activation(out=mb[:, a0:a1], in_=mf[:, a0:a1], func=AF.Copy,
                                                                  scale=r3s[:, j:j + 1]),
                        r=["mf%d" % j, "r3s"], w=["mb"])
                for k in range(8):
                    pe(lambda e, k=k: e.transpose(out=pb0b[:, k * 128:(k + 1) * 128], in_=mb[:, k * 128:(k + 1) * 128],
                                                  identity=identb[:]), r=["mb", "identb"], w=["pb0"])
                act(lambda e: e.activation(out=mT[:].rearrange("p k t -> p (k t)"), in_=pb0b[:, :], func=AF.Copy),
                    r=["pb0"], w=["mT"])
                for cc in range(2):
                    bank = 1 + cc
                    for k in range(8):
                        pe(lambda e, cc=cc, k=k, bank=bank: e.matmul(pb[bank][:, :], lhsT=mT[:, k, :],
                                                                     rhs=w_out_sb[:, k, cc * 512:(cc + 1) * 512],
                                                                     start=(k == 0), stop=(k == 7)),
                           r=["mT", "w_out"], w=["pb%d" % bank])
                    dve(lambda e, cc=cc, bank=bank: e.tensor_tensor(out=og[:, :], in0=pb[bank][:, :],
                                                                    in1=gB[:, 0, cc * 512:(cc + 1) * 512], op=ALU.mult),
                        r=["pb%d" % bank, "gB"], w=["og"])
                    dve(lambda e, cc=cc, xt=xt: e.tensor_tensor(out=xt[:, cc * 512:(cc + 1) * 512],
                                                                in0=xt[:, cc * 512:(cc + 1) * 512], in1=og[:, :], op=ALU.add),
                        r=["og", xtag], w=[xtag])
                dma("sp", y_out[n * 128:(n + 1) * 128, :], xt, r=[xtag], w=[("yrow", n)], key=("xo", n % 2))

        def phaseB(l):
            ar.reset()
            A = {}
            A["junkb"] = ar.alloc([D], BF16)
            A["ss"] = ar.alloc([4], F32)
            A["rs"] = ar.alloc([4], F32)
            A["xn"] = ar.alloc([D], BF16)
            A["hT"] = ar.alloc([8, 128], BF16)
            h2 = ar.alloc([D], BF16)
            qTp = ar.alloc([16, 128], BF16)
            ssb = ar.alloc([16, 128], F32)
            s2 = ar.alloc([256], F32)
            tops = ar.alloc([16, 16], F32)
            topi = ar.alloc([16, 16], U32)
            topf = ar.alloc([16, 16], F32)
            cand = ar.alloc([8, 256], F32)
            best = ar.alloc([8, 16], F32)
            bpos = ar.alloc([8, 16], U32)
            ab_u = ar.alloc([2, 128], U32)
            ab_f = ar.alloc([2, 128], F32)
            eq = ar.alloc([8, 16, 16], F32)
            ij = ar.alloc([2, 128], F32)
            Ef = ar.alloc([128], F32)
            Eu = ar.alloc([128], U32)
            gd = ar.alloc([8, 16], F32)
            gs = ar.alloc([8], F32)
            gate = ar.alloc([8, 16], F32)
            actv = ar.alloc([128], F32)
            wgt = ar.alloc([128], F32)
            yacc = ar.alloc([D], F32)
            yacc1 = ar.alloc([D], F32)
            junk1 = ar.alloc([D], BF16)
            Dm = ar.alloc([3, 4, 128], BF16)
            og = ar.alloc([512], F32)
            for n in range(nt):
                xt, xtag = tile_front(n, y_out, 1, A)
                for k in range(8):
                    pe(lambda e, k=k: e.transpose(out=pb0b[:, k * 128:(k + 1) * 128], in_=A["hT"][:, k, :],
                                                  identity=identb[:]), r=["hT", "identb"], w=["pb0"])
                act(lambda e: e.activation(out=h2[:, :], in_=pb0b[:, :], func=AF.Copy), r=["pb0"], w=["h2"])
                for j in range(16):
                    bank = 1 + j // 4
                    for k in range(8):
                        pe(lambda e, j=j, k=k, bank=bank: e.matmul(pb[bank][:, (j % 4) * 128:(j % 4 + 1) * 128],
                                                                   lhsT=wq_sb[:, k, j * 128:(j + 1) * 128], rhs=A["hT"][:, k, :],
                                                                   start=(k == 0), stop=(k == 7)),
                           r=["wq", "hT"], w=["pb%d" % bank])
                for b4 in range(4):
                    act(lambda e, b4=b4: e.activation(out=qTp[:, b4 * 4:(b4 + 1) * 4, :].rearrange("p j t -> p (j t)"),
                                                      in_=pb[1 + b4][:, :], func=AF.Copy),
                        r=["pb%d" % (1 + b4)], w=[("qTp", b4)])
                for j in range(16):
                    bank = 1 + j // 4
                    pe(lambda e, j=j, bank=bank: e.matmul(pb[bank][:, (j % 4) * 128:(j % 4 + 1) * 128], lhsT=qTp[:, j, :],
                                                          rhs=skT_sb[:, j, :], start=True, stop=True),
                       r=[("qTp", j // 4), "skT"], w=["pb%d" % bank])
                for b4 in range(4):
                    act(lambda e, b4=b4: e.activation(out=ssb[:, b4 * 4:(b4 + 1) * 4, :].rearrange("p j t -> p (j t)"),
                                                      in_=pb[1 + b4][:, :], func=AF.Copy),
                        r=["pb%d" % (1 + b4)], w=[("ssb", b4)])

                def top16(src_ap, srctag, tmp_ap, vals, idxs, vtag, itag):
                    dve(lambda e: e.max(out=vals[:, 0:8], in_=src_ap), r=[srctag], w=[vtag])
                    dve(lambda e: e.max_index(out=idxs[:, 0:8], in_max=vals[:, 0:8], in_values=src_ap),
                        r=[srctag, vtag], w=[itag])
                    dve(lambda e: e.match_replace(out=tmp_ap, in_to_replace=vals[:, 0:8], in_values=src_ap, imm_value=-1e30),
                        r=[srctag, vtag], w=["s2"])
                    dve(lambda e: e.max(out=vals[:, 8:16], in_=tmp_ap), r=["s2"], w=[vtag + "b"])
                    dve(lambda e: e.max_index(out=idxs[:, 8:16], in_max=vals[:, 8:16], in_values=tmp_ap),
                        r=["s2", vtag + "b"], w=[itag + "b"])

                for j in range(16):
                    top16(ssb[:, j, :], ("ssb", j // 4), s2[:, 0:128], tops[:, j, :], topi[:, j, :], "tops", "topi")
                dve(lambda e: e.tensor_copy(out=topf[:, :, :], in_=topi[:, :, :]), r=["topi", "topib"], w=["topf"])
                tv = tops[:, :, :].rearrange("p (h c) k -> p h c k", c=2)
                tf = topf[:, :, :].rearrange("p (h c) k -> p h c k", c=2)
                dve(lambda e: e.tensor_tensor(out=cand[:, :, :].rearrange("p h (a b) -> p h a b", a=16),
                                              in0=bc(tv[:, :, 0, :], 3, 16), in1=bc(tv[:, :, 1, :], 2, 16), op=ALU.add),
                    r=["tops", "topsb"], w=["cand"])
                for h in range(8):
                    top16(cand[:, h, :], "cand", s2[:, 0:256], best[:, h, :], bpos[:, h, :], "best", "bpos")
                bp = bpos[:, :, :].rearrange("p h k -> p (h k)")
                dve(lambda e: e.tensor_single_scalar(out=ab_u[:, 0, :], in_=bp, scalar=4, op=ALU.logical_shift_right),
                    r=["bpos", "bposb"], w=["ab_u"])
                dve(lambda e: e.tensor_single_scalar(out=ab_u[:, 1, :], in_=bp, scalar=15, op=ALU.bitwise_and),
                    r=["bpos", "bposb"], w=["ab_u"])
                dve(lambda e: e.tensor_copy(out=ab_f[:, :, :], in_=ab_u[:, :, :]), r=["ab_u"], w=["ab_f"])
                iota = cst[:, C_IOTA:C_IOTA + 16]
                for c in range(2):
                    abv = ab_f[:, c, :].rearrange("p (h k) -> p h k", h=8)
                    dve(lambda e, abv=abv: e.tensor_tensor(out=eq[:, :, :, :], in0=bc(abv, 3, 16),
                                                           in1=bc(bc(iota, 1, 16), 1, 8), op=ALU.is_equal),
                        r=["ab_f", "cst", "ij"], w=["eq"])
                    dve(lambda e, c=c: e.tensor_tensor(out=eq[:, :, :, :], in0=eq[:, :, :, :], in1=bc(tf[:, :, c, :], 2, 16),
                                                       op=ALU.mult), r=["eq", "topf"], w=["eq"])
                    dve(lambda e, c=c: e.tensor_reduce(out=ij[:, c, :], in_=eq[:, :, :, :].rearrange("p h k a -> p (h k) a"),
                                                       axis=AX.X, op=ALU.add), r=["eq"], w=["ij"])
                dve(lambda e: e.scalar_tensor_tensor(out=Ef[:, :], in0=ij[:, 0, :], scalar=128.0, in1=ij[:, 1, :],
                                                     op0=ALU.mult, op1=ALU.add), r=["ij"], w=["Ef"])
                dve(lambda e: e.tensor_copy(out=Eu[:, :], in_=Ef[:, :]), r=["Ef"], w=["Eu"])
                dve(lambda e: e.tensor_tensor(out=gd[:, :, :], in0=best[:, :, :], in1=bc(best[:, :, 0], 2, 16), op=ALU.subtract),
                    r=["best", "bestb"], w=["gd"])
                act(lambda e: e.activation(out=gd[:, :, :], in_=gd[:, :, :], func=AF.Exp), r=["gd"], w=["gd"])
                dve(lambda e: e.tensor_reduce(out=gs[:, :], in_=gd[:, :, :], axis=AX.X, op=ALU.add), r=["gd"], w=["gs"])
                dve(lambda e: e.reciprocal(out=gs[:, :], in_=gs[:, :]), r=["gs"], w=["gs"])
                dve(lambda e: e.tensor_tensor(out=gate[:, :, :], in0=gd[:, :, :], in1=bc(gs[:, :], 2, 16), op=ALU.mult),
                    r=["gd", "gs"], w=["gate"])
                gflat = gate[:, :, :].rearrange("p h k -> p (h k)")

                def emit_accum(g):
                    sl4 = slice(4 * g, 4 * g + 4)
                    tg_a = ("actvg", g % 4)
                    tg_w = ("wgtg", g % 4)
                    act(lambda e, sl4=sl4: e.activation(out=wgt[:, sl4], in_=actv[:, sl4], func=AF.Gelu),
                        r=[("actv", (4 * g + j) % 16) for j in range(4)], w=[tg_w])
                    dve(lambda e, sl4=sl4: e.tensor_tensor(out=wgt[:, sl4], in0=wgt[:, sl4], in1=gflat[:, sl4], op=ALU.mult),
                        r=[tg_w, "gate"], w=[tg_w])
                    gi = g % 3
                    dve(lambda e, sl4=sl4, gi=gi: e.tensor_tensor(out=Dm[:, gi, :, :], in0=bc(identb[:], 1, 4),
                                                                  in1=bc(wgt[:, sl4], 2, 128), op=ALU.mult),
                        r=[tg_w, "identb"], w=[("Dm", gi)])
                    for m in range(4 * g, 4 * g + 4):
                        sl = m % NSLOT
                        for half in range(2):
                            pe(lambda e, m=m, sl=sl, gi=gi, half=half: e.matmul(
                                pb[5 + half][:, :], lhsT=Dm[:, gi, m % 4, :], rhs=gsl[:, sl, 1, half * 512:(half + 1) * 512],
                                start=(m == 0), stop=(m == 127)),
                                r=[("Dm", gi), ("gs", sl)], w=["pb%d" % (5 + half)])

                for g in range(32):
                    for m in range(4 * g, 4 * g + 4):
                        sl = m % NSLOT
                        P.op("pool", lambda e, m=m, sl=sl: e.indirect_dma_start(
                            out=gslf[:, sl, :], out_offset=None, in_=ptab,
                            in_offset=bass.IndirectOffsetOnAxis(ap=Eu[:, m:m + 1], axis=0)),
                            r=["Eu"], w=[("gs", sl)], dma=("gs", sl))
                    for m in range(4 * g, 4 * g + 4):
                        sl = m % NSLOT
                        jk = A["junkb"] if m % 2 == 0 else junk1
                        dve(lambda e, m=m, sl=sl, jk=jk: e.scalar_tensor_tensor(out=jk, in0=gsl[:, sl, 0, :], scalar=1.0, in1=h2[:, :],
                                                                                op0=ALU.mult, op1=ALU.mult, accum_out=actv[:, m:m + 1]),
                            r=[("gs", sl), "h2"], w=["junkb" if m % 2 == 0 else "junk1", ("actv", m % 16)])
                    if g >= 1:
                        emit_accum(g - 1)
                emit_accum(31)
                for cc in range(2):
                    dve(lambda e, cc=cc: e.tensor_tensor(out=og[:, :], in0=pb[5 + cc][:, :],
                                                         in1=gB[:, 1, cc * 512:(cc + 1) * 512], op=ALU.mult),
                        r=["pb%d" % (5 + cc), "gB"], w=["og"])
                    dve(lambda e, cc=cc, xt=xt: e.tensor_tensor(out=xt[:, cc * 512:(cc + 1) * 512],
                                                                in0=xt[:, cc * 512:(cc + 1) * 512], in1=og[:, :], op=ALU.add),
                        r=["og", xtag], w=[xtag])
                dma("sp", y_out[n * 128:(n + 1) * 128, :], xt, r=[xtag], w=[("yrow", n)], key=("xo", n % 2))

        for l in range(depth):
            P.barrier()
            layer_params(l)
            phaseA_weights(l)
            P.barrier()
            phaseA(l)
            P.barrier()
            if only_a:
                continue
            phaseB_weights(l)
            P.barrier()
            convert_tables(l)
            P.barrier()
            phaseB(l)
        P.barrier()

        sems = {}
        for e in Prog.ENG:
            sems[e] = es.enter_context(nc.semaphore("s_" + e))
        for i, key in enumerate(P.dcnt):
            sems[key] = es.enter_context(nc.semaphore("d%d" % i))
        block = es.enter_context(nc.Block())

        def make_body(ename):
            def body(eng):
                for o in P.ops[ename]:
                    if o[0] == "w":
                        eng.wait_ge(sems[o[1]], o[2])
                    else:
                        ins = o[1](eng)
                        k = o[2][0]
                        ins.then_inc(sems[k], 16 if isinstance(k, tuple) else 1)
            return body

        block.tensor(make_body("pe"))
        block.scalar(make_body("act"))
        block.vector(make_body("dve"))
        block.gpsimd(make_body("pool"))
        block.sync(make_body("sp"))
    return nc


def _bucket(dist):
    n_buckets, max_distance = 32, 128
    max_exact = n_buckets // 2
    d = np.maximum(dist, 0)
    lr = np.log(np.maximum(d, 1).astype(np.float32) / np.float32(max_exact)) / np.float32(math.log(max_distance / max_exact))
    large = max_exact + (lr.astype(np.float32) * np.float32(n_buckets - max_exact)).astype(np.int32)
    large = np.minimum(large, n_buckets - 1)
    return np.where(d < max_exact, d, large)


def _consts():
    c = np.zeros((128, NCST), np.float32)
    i = np.arange(128)
    c[:, C_ID:C_ID + 128] = np.eye(128, dtype=np.float32)
    tp, t = np.meshgrid(i, i, indexing="ij")
    c[:, C_S1:C_S1 + 128] = (tp == t - 1)
    c[:, C_S2:C_S2 + 128] = (tp == t - 2)
    c[:, C_S1P:C_S1P + 128] = (tp == t + 127)
    c[:, C_S2P:C_S2P + 128] = (tp == t + 126)
    tt, ss = np.meshgrid(i, i, indexing="ij")
    c[:, C_TRIL:C_TRIL + 128] = (ss <= tt)
    s, q = np.meshgrid(i, i, indexing="ij")
    c[:, C_MPREV:C_MPREV + 128] = np.where(s > q, 0.0, -1e30)
    c[:, C_MCUR:C_MCUR + 128] = np.where(s <= q, 0.0, -1e30)
    c[:, C_IOTA:C_IOTA + 16] = np.arange(16, dtype=np.float32)[None, :]
    c[:, C_SEGL:C_SEGL + 3] = np.array([1 / 512., 1 / 256., 1 / 256.], np.float32)[None, :]
    return c


def _bias_index():
    i = np.arange(128)
    s, q = np.meshgrid(i, i, indexing="ij")
    dprev = q + 128 - s
    dcur = q - s
    bp = np.where(s > q, _bucket(dprev), 0)
    bcur = np.where(s <= q, _bucket(dcur), 0)
    return np.stack([bp, bcur], axis=1)


_CACHE = {}


def _prep_shared(inp, depth):
    f = lambda a: np.ascontiguousarray(np.asarray(a, dtype=np.float32))
    rb = f(inp["rel_bias"])
    bidx = _bias_index()
    braw = rb[bidx]
    braw = np.ascontiguousarray(braw.transpose(0, 1, 3, 2)).reshape(128, 2 * 8 * 128)
    b_ada = f(inp["b_ada"])[:depth]
    sh = {
        "cst": _consts(),
        "braw": braw,
        "w_ada": f(inp["w_ada"])[:depth],
        "badaT": np.ascontiguousarray(b_ada.reshape(depth, 48, 128).transpose(0, 2, 1)),
        "badaG": np.ascontiguousarray(np.broadcast_to(
            np.concatenate([b_ada[:, 2048:3072], b_ada[:, 5120:6144]], axis=1)[:, None, :], (depth, 128, 2048))),
        "n1T": np.ascontiguousarray(f(inp["norm1_g"])[:depth].reshape(depth, 8, 128).transpose(0, 2, 1)),
        "n2T": np.ascontiguousarray(f(inp["norm2_g"])[:depth].reshape(depth, 8, 128).transpose(0, 2, 1)),
        "ongT": np.ascontiguousarray(f(inp["out_norm_g"])[:depth].reshape(depth, 8, 128).transpose(0, 2, 1)),
        "w_in": f(inp["w_in"])[:depth],
        "w_out": f(inp["w_out"])[:depth],
        "wq": f(inp["peer_wq"])[:depth],
        "qgB": np.ascontiguousarray(np.broadcast_to(f(inp["q_norm_g"])[:depth, None, :], (depth, 128, 64))),
        "kgB": np.ascontiguousarray(np.broadcast_to(f(inp["k_norm_g"])[:depth, None, :], (depth, 128, 64))),
        "sinkB": np.ascontiguousarray(np.broadcast_to(f(inp["attn_sink"])[:depth, None, :], (depth, 128, 8))),
        "convB": np.ascontiguousarray(np.broadcast_to(f(inp["conv_w"])[:depth].reshape(depth, 1, 768), (depth, 128, 768))),
        "sgu_w": f(inp["sgu_w"])[:depth],
        "sgubT": np.ascontiguousarray(f(inp["sgu_b"])[:depth].transpose(0, 2, 1)),
        "subk": f(inp["peer_sub_keys"])[:depth].reshape(depth, 16, 128, 128),
    }
    ne = 128 if DEBUG_SMALL_TABLES else 16384
    for i in range(depth):
        sh["pdown%d" % i] = np.ascontiguousarray(np.asarray(inp["peer_down"][i], dtype=np.float32)[:ne])
        sh["pup%d" % i] = np.ascontiguousarray(np.asarray(inp["peer_up"][i], dtype=np.float32)[:ne])
    return sh


def run(inp, n_cores=8, nt=SEQ // 128, depth=DEPTH, trace=False, only_a=False):
    key = (nt, depth, only_a)
    if key not in _CACHE:
        _CACHE[key] = build_program(nt, depth, only_a=only_a)
    nc = _CACHE[key]
    sh = _prep_shared(inp, depth)
    x = np.asarray(inp["x"], dtype=np.float32)
    c = np.asarray(inp["c"], dtype=np.float32)
    in_maps = []
    for b in range(n_cores):
        m = dict(sh)
        m["x"] = np.ascontiguousarray(x[b, :nt * 128])
        m["cT"] = np.ascontiguousarray(c[b].reshape(8, 128).T)
        in_maps.append(m)
    res = run_bass_kernel_spmd(nc, in_maps, core_ids=list(range(n_cores)), **({"trace": True} if trace else {}))
    return np.stack([r["y"] for r in res.results], axis=0), res


def kernel(**inputs):
    out, _ = run(inputs)
    return out.astype(np.float32)
```

```python
import math
from contextlib import ExitStack
import numpy as np
import concourse.bass as bass
import concourse.mybir as mybir
from concourse.bass_utils import run_bass_kernel_spmd

F32 = mybir.dt.float32
BF16 = mybir.dt.bfloat16
U32 = mybir.dt.uint32
ALU = mybir.AluOpType
AF = mybir.ActivationFunctionType
AX = mybir.AxisListType

D = 1024
DEPTH = 4
SEQ = 4096
EPS = 1e-6
NSLOT = 12
SAME_ENGINE_SYNC = True
DEBUG_MAXOPS = None
DEBUG_SKIP = set()
DEBUG_SMALL_TABLES = False

C_ID = 0
C_S1 = 128
C_S2 = 256
C_S1P = 384
C_S2P = 512
C_TRIL = 640
C_MPREV = 768
C_MCUR = 896
C_IOTA = 1024
C_SEGL = 1040
NCST = 1048


class Prog:
    ENG = ("pe", "act", "dve", "pool", "sp")

    def __init__(self):
        self.ops = {e: [] for e in self.ENG}
        self.cnt = {e: 0 for e in self.ENG}
        self.known = {e: {} for e in self.ENG}
        self.tags = {}
        self.dcnt = {}

    def _dep(self, eng, tok, raw):
        k, v = tok
        if k == eng:
            if eng == "pe" or not SAME_ENGINE_SYNC:
                return
        if self.known[eng].get(k, 0) >= v:
            return
        self.known[eng][k] = v
        self.ops[eng].append(("w", k, v))

    def op(self, eng, fn, r=(), w=(), dma=None):
        self.nops = getattr(self, 'nops', 0) + 1
        if DEBUG_MAXOPS is not None and self.nops > DEBUG_MAXOPS:
            return
        if self.nops in DEBUG_SKIP:
            return
        for t in r:
            st = self.tags.get(t)
            if st and st["w"]:
                self._dep(eng, st["w"], True)
        for t in w:
            st = self.tags.get(t)
            if st:
                if st["w"]:
                    self._dep(eng, st["w"], False)
                for k, v in st["r"].items():
                    self._dep(eng, (k, v), False)
        if dma is None:
            self.cnt[eng] += 1
            tok = (eng, self.cnt[eng])
        else:
            key = ("dma", dma)
            self.dcnt[key] = self.dcnt.get(key, 0) + 16
            tok = (key, self.dcnt[key])
        self.ops[eng].append(("i", fn, tok))
        for t in w:
            self.tags[t] = {"w": tok, "r": {}}
        for t in r:
            st = self.tags.setdefault(t, {"w": None, "r": {}})
            st["r"][tok[0]] = max(st["r"].get(tok[0], 0), tok[1])

    def barrier(self):
        for e in self.ENG:
            for f in self.ENG:
                if f != e and self.cnt[f] > 0:
                    self._dep(e, (f, self.cnt[f]), True)
            for key, v in self.dcnt.items():
                self._dep(e, (key, v), True)


ARENA_LOG = []


class Arena:
    def __init__(self, ap_f32, nwords):
        self.ap = ap_f32
        self.n = nwords
        self.off = 0

    def reset(self):
        self.off = 0

    def alloc(self, free_shape, dtype):
        nel = int(np.prod(free_shape))
        esz = 2 if dtype == BF16 else 4
        nw = (nel * esz + 3) // 4
        nw = (nw + 7) // 8 * 8
        assert self.off + nw <= self.n, ("arena overflow", self.off, nw, self.n)
        v = self.ap[:, self.off:self.off + nw]
        ARENA_LOG.append((self.off, nel, dtype, tuple(free_shape)))
        self.off += nw
        if dtype != F32:
            v = v.bitcast(dtype)
        v = v[:, 0:nel]
        if len(free_shape) == 2:
            v = v.rearrange("p (a b) -> p a b", a=free_shape[0])
        elif len(free_shape) == 3:
            v = v.rearrange("p (a b c) -> p a b c", a=free_shape[0], b=free_shape[1])
        return v


def bc(ap, pos, n):
    u = ap.unsqueeze(pos)
    shp = list(u.shape)
    shp[pos] = n
    return u.broadcast_to(shp)


def build_program(nt, depth, only_a=False):
    S = nt * 128
    nc = bass.Bass("TRN2", target_bir_lowering=False)

    def din(name, shape, dt=F32):
        return nc.dram_tensor(name, list(shape), dt, kind="ExternalInput").ap()

    x_in = din("x", [S, D])
    cT_in = din("cT", [128, 8])
    cst_in = din("cst", [128, NCST])
    braw_in = din("braw", [128, 2 * 8 * 128])
    w_ada = din("w_ada", [depth, D, 6 * D])
    badaT = din("badaT", [depth, 128, 48])
    badaG = din("badaG", [depth, 128, 2 * D])
    n1T = din("n1T", [depth, 128, 8])
    n2T = din("n2T", [depth, 128, 8])
    ongT = din("ongT", [depth, 128, 8])
    w_in = din("w_in", [depth, D, 2048])
    w_out = din("w_out", [depth, D, D])
    wq = din("wq", [depth, D, 2048])
    qgB = din("qgB", [depth, 128, 64])
    kgB = din("kgB", [depth, 128, 64])
    sinkB = din("sinkB", [depth, 128, 8])
    convB = din("convB", [depth, 128, 768])
    sgu_w = din("sgu_w", [depth, 4, 128, 128])
    sgubT = din("sgubT", [depth, 128, 4])
    subk = din("subk", [depth, 16, 128, 128])
    NE = 128 if DEBUG_SMALL_TABLES else 16384
    pdown = [din("pdown%d" % i, [NE, D]) for i in range(depth)]
    pup = [din("pup%d" % i, [NE, D]) for i in range(depth)]
    y_out = nc.dram_tensor("y", [S, D], F32, kind="ExternalOutput").ap()
    ptab = nc.dram_tensor("ptab_bf", [NE, 2 * D], BF16).ap()

    P = Prog()
    es = ExitStack()
    with es:
        def sb(name, shape, dt=F32):
            return es.enter_context(nc.sbuf_tensor(name, list(shape), dt))

        def ps(name):
            return es.enter_context(nc.psum_tensor(name, [128, 512], F32))

        R1W = 12288
        r1 = sb("r1", [128, R1W])
        GW = NSLOT * 512
        gsl2 = sb("gsl", [128, NSLOT * 2 * D], BF16)
        gsl = gsl2[:].rearrange("p (s c d) -> p s c d", s=NSLOT, c=2)
        gslf = gsl2[:].rearrange("p (s d) -> p s d", s=NSLOT)
        stg = [gsl2[:].bitcast(F32)[:, 0:2048], gsl2[:].bitcast(F32)[:, 2048:4096]]
        cst = sb("cst_sb", [128, NCST])
        identb = sb("identb", [128, 128], BF16)
        biasT = sb("biasT", [128, 2, 8, 128])
        R3W = 16 * 1024
        r3 = sb("r3", [128, R3W])
        cact = sb("cact", [128, 8])
        cbc = sb("cbc", [128, 8, 128])
        modT = sb("modT", [128, 4, 8])
        bT = sb("bT", [128, 48])
        nT = sb("nT", [128, 3, 8])
        sT = sb("sT", [128, 2, 8])
        gB = sb("gB", [128, 2, D])
        qg8 = sb("qg8", [128, 64])
        kgb = sb("kgb", [128, 64])
        esink = sb("esink", [128, 8])
        cwb = sb("cwb", [128, 3, 256])
        swT = sb("swT", [128, 4, 128], BF16)
        sgb = sb("sgb", [128, 4])
        xbuf = [sb("xbuf0", [128, D]), sb("xbuf1", [128, D])]
        kTb = [sb("kT0", [64, 2, 128], BF16), sb("kT1", [64, 2, 128], BF16)]
        vaug = [sb("va0", [128, 2, 80], BF16), sb("va1", [128, 2, 80], BF16)]
        zb = [sb("z0", [128, 256]), sb("z1", [128, 256])]
        pb = [ps("pb%d" % i) for i in range(8)]
        pb0b = pb[0][:].bitcast(BF16)

        w_in_sb = r1[:, 0:8192].bitcast(BF16).rearrange("p (k n) -> p k n", k=8)
        w_out_sb = r1[:, 8192:12288].bitcast(BF16).rearrange("p (k n) -> p k n", k=8)
        wq_sb = w_in_sb
        skT_sb = r1[:, 8192:9216].bitcast(BF16).rearrange("p (j n) -> p j n", j=16)

        ar = Arena(r3[:], R3W)

        def dve(fn, r=(), w=()):
            P.op("dve", fn, r, w)

        def act(fn, r=(), w=()):
            P.op("act", fn, r, w)

        def pe(fn, r=(), w=()):
            P.op("pe", fn, r, w)

        def dma(eng, out, in_, r=(), w=(), key=None):
            P.op(eng, lambda e, out=out, in_=in_: e.dma_start(out=out, in_=in_), r, w, dma=key)

        def rstd_from_ss(ss, out, scale, n, tagss, tagout, tmp, tagtmp):
            dve(lambda e: e.tensor_scalar(out=tmp, in0=ss, scalar1=scale, scalar2=EPS, op0=ALU.mult, op1=ALU.add),
                r=[tagss], w=[tagtmp])
            act(lambda e: e.activation(out=tmp, in_=tmp, func=AF.Sqrt), r=[tagtmp], w=[tagtmp])
            dve(lambda e: e.reciprocal(out=out, in_=tmp), r=[tagtmp], w=[tagout])

        dma("sp", cst[:], cst_in, w=["cst"], key="c0")
        dma("sp", biasT[:].rearrange("p a h q -> p (a h q)"), braw_in, w=["biasT"], key="c1")
        dma("sp", cact[:], cT_in, w=["cact"], key="c2")
        dve(lambda e: e.tensor_copy(out=identb[:], in_=cst[:, C_ID:C_ID + 128]), r=["cst"], w=["identb"])
        for a, cm in ((0, C_MPREV), (1, C_MCUR)):
            dve(lambda e, a=a, cm=cm: e.tensor_tensor(out=biasT[:, a], in0=biasT[:, a],
                                                       in1=bc(cst[:, cm:cm + 128], 1, 8), op=ALU.add),
                r=["cst", "biasT"], w=["biasT"])
        act(lambda e: e.activation(out=cact[:], in_=cact[:], func=AF.Silu), r=["cact"], w=["cact"])
        dve(lambda e: e.tensor_copy(out=cbc[:], in_=bc(cact[:], 2, 128)), r=["cact"], w=["cbc"])
        for par in range(2):
            dve(lambda e, par=par: e.memset(vaug[par][:], 1.0), w=[("va", par)])

        ident32 = cst[:, C_ID:C_ID + 128]

        def load_convert(dst_fn, src_rows_fn, nrows_chunks, ncols, scale_fn=None, wtag=None):
            i = 0
            for k in range(nrows_chunks):
                for c0 in range(0, ncols, 2048):
                    c1 = min(ncols, c0 + 2048)
                    s = stg[i % 2]
                    tg = ("stg", i % 2)
                    dma("sp", s[:, 0:c1 - c0], src_rows_fn(k)[:, c0:c1], w=[tg], key=("stg", i % 2))
                    eng = "dve" if i % 2 == 0 else "act"
                    if scale_fn is None:
                        if eng == "dve":
                            dve(lambda e, s=s, k=k, c0=c0, c1=c1: e.tensor_copy(out=dst_fn(k, c0, c1), in_=s[:, 0:c1 - c0]),
                                r=[tg], w=[wtag])
                        else:
                            act(lambda e, s=s, k=k, c0=c0, c1=c1: e.activation(out=dst_fn(k, c0, c1), in_=s[:, 0:c1 - c0], func=AF.Copy),
                                r=[tg], w=[wtag])
                    else:
                        dve(lambda e, s=s, k=k, c0=c0, c1=c1: e.tensor_scalar(out=dst_fn(k, c0, c1), in0=s[:, 0:c1 - c0],
                                                                              scalar1=scale_fn(k), scalar2=None, op0=ALU.mult),
                            r=[tg, "nT"], w=[wtag])
                    i += 1

        def layer_params(l):
            dma("sp", bT[:], badaT[l], w=["bT"], key="p0")
            dma("sp", nT[:, 0, :], n1T[l], w=["nT"], key="p1")
            dma("sp", nT[:, 1, :], n2T[l], w=["nT"], key="p1")
            dma("sp", nT[:, 2, :], ongT[l], w=["nT"], key="p1")
            dma("sp", gB[:].rearrange("p a d -> p (a d)"), badaG[l], w=["gB"], key="p2")
            dma("sp", qg8[:], qgB[l], w=["qg8"], key="p3")
            dma("sp", kgb[:], kgB[l], w=["kgb"], key="p4")
            dma("sp", esink[:], sinkB[l], w=["esink"], key="p5")
            dma("sp", cwb[:].rearrange("p a c -> p (a c)"), convB[l], w=["cwb"], key="p6")
            dma("sp", sgb[:], sgubT[l], w=["sgb"], key="p7")
            dve(lambda e: e.tensor_scalar(out=qg8[:], in0=qg8[:], scalar1=0.125, scalar2=None, op0=ALU.mult),
                r=["qg8"], w=["qg8"])
            act(lambda e: e.activation(out=esink[:], in_=esink[:], func=AF.Exp), r=["esink"], w=["esink"])
            fm_slot = {0: 0, 1: 1, 3: 2, 4: 3}
            for ci in range(24):
                seg = ci // 4
                s = stg[ci % 2]
                tg = ("stg", ci % 2)
                sv = s.rearrange("p (k n) -> p k n", k=8)
                dma("sp", sv, w_ada[l, :, ci * 256:(ci + 1) * 256].rearrange("(k p) n -> p k n", p=128),
                    w=[tg], key=("stg", ci % 2))
                if seg in fm_slot:
                    for half in range(2):
                        col = (ci % 4) * 2 + half
                        for k in range(8):
                            pe(lambda e, sv=sv, k=k, half=half: e.matmul(
                                pb[7][:, 0:1], lhsT=sv[:, k, half * 128:(half + 1) * 128], rhs=cact[:, k:k + 1],
                                start=(k == 0), stop=(k == 7)), r=[tg, "cact"], w=["pb7"])
                        dve(lambda e, seg=seg, col=col: e.tensor_tensor(
                            out=modT[:, fm_slot[seg], col:col + 1], in0=pb[7][:, 0:1],
                            in1=bT[:, seg * 8 + col:seg * 8 + col + 1], op=ALU.add),
                            r=["pb7", "bT"], w=["modT"])
                else:
                    gi = 0 if seg == 2 else 1
                    cc0 = (ci % 4) * 256
                    for k in range(8):
                        pe(lambda e, sv=sv, k=k: e.matmul(pb[6][:, 0:256], lhsT=cbc[:, k, :], rhs=sv[:, k, :],
                                                          start=(k == 0), stop=(k == 7)), r=[tg, "cbc"], w=["pb6"])
                    dve(lambda e, gi=gi, cc0=cc0: e.tensor_tensor(out=gB[:, gi, cc0:cc0 + 256], in0=pb[6][:, 0:256],
                                                                   in1=gB[:, gi, cc0:cc0 + 256], op=ALU.add),
                        r=["pb6", "gB"], w=["gB"])
            for j, slot in ((0, 1), (1, 3)):
                dve(lambda e, j=j, slot=slot: e.scalar_tensor_tensor(out=sT[:, j, :], in0=modT[:, slot, :], scalar=1.0,
                                                                     in1=nT[:, j, :], op0=ALU.add, op1=ALU.mult),
                    r=["modT", "nT"], w=["sT"])

        def phaseA_weights(l):
            load_convert(lambda k, c0, c1: w_in_sb[:, k, c0:c1],
                         lambda k: w_in[l, k * 128:(k + 1) * 128, :], 8, 2048, wtag="w_in")
            load_convert(lambda k, c0, c1: w_out_sb[:, k, c0:c1],
                         lambda k: w_out[l, k * 128:(k + 1) * 128, :], 8, 1024,
                         scale_fn=lambda k: nT[:, 2, k:k + 1], wtag="w_out")
            s = stg[0]
            sv = s[:, 0:512].rearrange("p (g s) -> p g s", g=4)
            dma("sp", sv, sgu_w[l].rearrange("g t s -> t g s"), w=[("stg", 0)], key=("stg", 0))
            dve(lambda e: e.tensor_tensor(out=sv, in0=sv, in1=bc(cst[:, C_TRIL:C_TRIL + 128], 1, 4), op=ALU.mult),
                r=[("stg", 0), "cst"], w=[("stg", 0)])
            for g in range(4):
                pe(lambda e, g=g: e.transpose(out=pb[7][:, g * 128:(g + 1) * 128], in_=sv[:, g, :], identity=ident32),
                   r=[("stg", 0), "cst"], w=["pb7"])
            dve(lambda e: e.tensor_copy(out=swT[:].rearrange("p g t -> p (g t)"), in_=pb[7][:, 0:512]),
                r=["pb7"], w=["swT"])

        def phaseB_weights(l):
            load_convert(lambda k, c0, c1: wq_sb[:, k, c0:c1],
                         lambda k: wq[l, k * 128:(k + 1) * 128, :], 8, 2048, wtag="wq")
            for jb in range(4):
                s = stg[jb % 2]
                tg = ("stg", jb % 2)
                sv = s[:, 0:512].rearrange("p (j k) -> p j k", j=4)
                dma("sp", sv, subk[l, jb * 4:(jb + 1) * 4].rearrange("j n k -> n j k"), w=[tg], key=("stg", jb % 2))
                for jj in range(4):
                    pe(lambda e, sv=sv, jj=jj: e.transpose(out=pb[7][:, jj * 128:(jj + 1) * 128], in_=sv[:, jj, :],
                                                           identity=ident32), r=[tg, "cst"], w=["pb7"])
                dve(lambda e, jb=jb: e.tensor_copy(out=skT_sb[:, jb * 4:(jb + 1) * 4, :].rearrange("p j n -> p (j n)"),
                                                   in_=pb[7][:, 0:512]), r=["pb7"], w=["skT"])

        def convert_tables(l):
            cbuf = [gsl2[:, 8192:10240], gsl2[:, 10240:12288]]
            jobs = []
            for src_t, c0 in ((pdown[l], 0), (pup[l], D)):
                for c in range(NE // 256):
                    jobs.append((src_t[c * 256:(c + 1) * 256, :].rearrange("(p r) d -> p (r d)", r=2),
                                 ptab[c * 256:(c + 1) * 256, c0:c0 + D].rearrange("(p r) d -> p r d", r=2)))

            def load(i):
                dma("sp", stg[i % 2], jobs[i][0], w=[("stg", i % 2)], key=("stg", i % 2))

            load(0)
            if len(jobs) > 1:
                load(1)
            for i in range(len(jobs)):
                b = i % 2
                if i % 2 == 0:
                    act(lambda e, b=b: e.activation(out=cbuf[b], in_=stg[b], func=AF.Copy), r=[("stg", b)], w=[("cb", b)])
                else:
                    dve(lambda e, b=b: e.tensor_copy(out=cbuf[b], in_=stg[b]), r=[("stg", b)], w=[("cb", b)])
                dma("sp", jobs[i][1], cbuf[b].rearrange("p (r d) -> p r d", r=2), r=[("cb", b)], w=[("tabrow", i)], key=("cbo", b))
                if i + 2 < len(jobs):
                    load(i + 2)

        def tile_front(n, src, which, A):
            xt = xbuf[n % 2][:]
            xtag = ("x", n % 2)
            if n == 0:
                dma("sp", xt, src[0:128, :], w=[xtag], key=("x", 0))
            if n + 1 < nt:
                dma("sp", xbuf[(n + 1) % 2][:], src[(n + 1) * 128:(n + 2) * 128, :], w=[("x", (n + 1) % 2)],
                    key=("x", (n + 1) % 2))
            act(lambda e: e.activation(out=A["junkb"], in_=xt, func=AF.Square, accum_out=A["ss"][:, 0:1]),
                r=[xtag], w=["junkb", "ss"])
            rstd_from_ss(A["ss"][:, 0:1], A["rs"][:, 0:1], 1.0 / D, 1, "ss", "rs", A["ss"][:, 1:2], "ss1")
            act(lambda e: e.activation(out=A["xn"], in_=xt, func=AF.Copy, scale=A["rs"][:, 0:1]),
                r=[xtag, "rs"], w=["xn"])
            for k in range(8):
                pe(lambda e, k=k: e.transpose(out=pb0b[:, k * 128:(k + 1) * 128], in_=A["xn"][:, k * 128:(k + 1) * 128],
                                              identity=identb[:]), r=["xn", "identb"], w=["pb0"])
            sh_slot = 0 if which == 0 else 2
            for k in range(8):
                dve(lambda e, k=k: e.tensor_scalar(out=A["hT"][:, k, :], in0=pb0b[:, k * 128:(k + 1) * 128],
                                                   scalar1=sT[:, which, k:k + 1], scalar2=modT[:, sh_slot, k:k + 1],
                                                   op0=ALU.mult, op1=ALU.add),
                    r=["pb0", "sT", "modT"], w=["hT"])
            return xt, xtag

        def phaseA(l):
            ar.reset()
            A = {}
            A["junkb"] = ar.alloc([D], BF16)
            A["ss"] = ar.alloc([4], F32)
            A["rs"] = ar.alloc([4], F32)
            A["xn"] = ar.alloc([D], BF16)
            A["hT"] = ar.alloc([8, 128], BF16)
            prsb = ar.alloc([1280], F32)
            sq = ar.alloc([640], F32)
            ssq = ar.alloc([16], F32)
            rq = ar.alloc([16], F32)
            tq = ar.alloc([16], F32)
            qn = ar.alloc([640], BF16)
            qtmp = ar.alloc([640], F32)
            qT = ar.alloc([8, 128], BF16)
            lg = ar.alloc([512], F32)
            PT = ar.alloc([2, 8, 128], BF16)
            mf = ar.alloc([D], F32)
            mb = ar.alloc([D], BF16)
            mT = ar.alloc([8, 128], BF16)
            den = ar.alloc([8], F32)
            c1 = ar.alloc([256], F32)
            c2 = ar.alloc([256], F32)
            bst = ar.alloc([4, 6], F32)
            mv = ar.alloc([4, 2], F32)
            vr = ar.alloc([4], F32)
            vt = ar.alloc([4], F32)
            vc = ar.alloc([4, 64], F32)
            vn = ar.alloc([4, 64], BF16)
            s3 = ar.alloc([4], F32)
            r3s = ar.alloc([4], F32)
            t3 = ar.alloc([4], F32)
            og = ar.alloc([512], F32)
            src = x_in if l == 0 else y_out
            for n in range(nt):
                par = n % 2
                xt, xtag = tile_front(n, src, 0, A)
                for cc in range(4):
                    for k in range(8):
                        pe(lambda e, cc=cc, k=k: e.matmul(pb[1 + cc][:, :], lhsT=A["hT"][:, k, :],
                                                          rhs=w_in_sb[:, k, cc * 512:(cc + 1) * 512],
                                                          start=(k == 0), stop=(k == 7)),
                           r=["hT", "w_in"], w=["pb%d" % (1 + cc)])
                act(lambda e: e.activation(out=prsb[:, 0:256], in_=pb[2][:, 256:512], func=AF.Copy), r=["pb2"], w=["prsb0"])
                act(lambda e: e.activation(out=prsb[:, 256:768], in_=pb[3][:, :], func=AF.Copy), r=["pb3"], w=["prsb1"])
                act(lambda e: e.activation(out=prsb[:, 768:1280], in_=pb[4][:, :], func=AF.Copy), r=["pb4"], w=["prsb2"])
                for g in range(2):
                    act(lambda e, par=par, g=g: e.activation(out=vaug[par][:, g, 0:64],
                                                             in_=pb[2][:, 128 + g * 64:128 + (g + 1) * 64], func=AF.Copy),
                        r=["pb2"], w=[("va", par)])
                act(lambda e: e.activation(out=sq[:, 0:512], in_=pb[1][:, :], func=AF.Square), r=["pb1"], w=["sq"])
                act(lambda e: e.activation(out=sq[:, 512:640], in_=pb[2][:, 0:128], func=AF.Square), r=["pb2"], w=["sq"])
                dve(lambda e: e.tensor_reduce(out=ssq[:, 0:10], in_=sq[:, 0:640].rearrange("p (h d) -> p h d", d=64),
                                              axis=AX.X, op=ALU.add), r=["sq"], w=["ssq"])
                rstd_from_ss(ssq[:, 0:10], rq[:, 0:10], 1.0 / 64, 10, "ssq", "rq", tq[:, 0:10], "tq")
                dve(lambda e: e.tensor_tensor(out=qtmp[:, 0:512].rearrange("p (h d) -> p h d", d=64),
                                              in0=pb[1][:, :].rearrange("p (h d) -> p h d", d=64),
                                              in1=bc(rq[:, 0:8], 2, 64), op=ALU.mult), r=["pb1", "rq"], w=["qtmp"])
                dve(lambda e: e.tensor_tensor(out=qtmp[:, 512:640].rearrange("p (h d) -> p h d", d=64),
                                              in0=pb[2][:, 0:128].rearrange("p (h d) -> p h d", d=64),
                                              in1=bc(rq[:, 8:10], 2, 64), op=ALU.mult), r=["pb2", "rq"], w=["qtmp"])
                dve(lambda e: e.tensor_tensor(out=qn[:, 0:512].rearrange("p (h d) -> p h d", d=64),
                                              in0=qtmp[:, 0:512].rearrange("p (h d) -> p h d", d=64),
                                              in1=bc(qg8[:], 1, 8), op=ALU.mult), r=["qtmp", "qg8"], w=["qn"])
                dve(lambda e: e.tensor_tensor(out=qn[:, 512:640].rearrange("p (h d) -> p h d", d=64),
                                              in0=qtmp[:, 512:640].rearrange("p (h d) -> p h d", d=64),
                                              in1=bc(kgb[:], 1, 2), op=ALU.mult), r=["qtmp", "kgb"], w=["qn"])
                for h in range(8):
                    pe(lambda e, h=h: e.transpose(out=pb0b[0:64, h * 128:(h + 1) * 128], in_=qn[:, h * 64:(h + 1) * 64],
                                                  identity=identb[:]), r=["qn", "identb"], w=["pb0"])
                act(lambda e: e.activation(out=qT[0:64].rearrange("p h t -> p (h t)"), in_=pb0b[0:64, :], func=AF.Copy),
                    r=["pb0"], w=["qT"])
                for g in range(2):
                    pe(lambda e, g=g: e.transpose(out=pb0b[0:64, g * 128:(g + 1) * 128],
                                                  in_=qn[:, 512 + g * 64:512 + (g + 1) * 64], identity=identb[:]),
                       r=["qn", "identb"], w=["pb0"])
                act(lambda e, par=par: e.activation(out=kTb[par][:].rearrange("p g t -> p (g t)"), in_=pb0b[0:64, 0:256],
                                                    func=AF.Copy), r=["pb0"], w=[("kT", par)])
                whichs = (0, 1) if n > 0 else (1,)
                for g in range(2):
                    for a in whichs:
                        kp = par if a == 1 else 1 - par
                        bank = 1 + g * 2 + a
                        pe(lambda e, g=g, kp=kp, bank=bank: e.matmul(
                            pb[bank][:, :], lhsT=kTb[kp][:, g, :],
                            rhs=qT[0:64, 4 * g:4 * g + 4, :].rearrange("p h t -> p (h t)"), start=True, stop=True),
                            r=[("kT", kp), "qT"], w=["pb%d" % bank])
                        dve(lambda e, g=g, a=a, bank=bank: e.tensor_tensor(
                            out=lg[:, :], in0=pb[bank][:, :],
                            in1=biasT[:, a, 4 * g:4 * g + 4, :].rearrange("p h q -> p (h q)"), op=ALU.add),
                            r=["pb%d" % bank, "biasT"], w=["lg"])
                        act(lambda e, g=g, a=a: e.activation(
                            out=PT[:, a, 4 * g:4 * g + 4, :].rearrange("p h q -> p (h q)"), in_=lg[:, :], func=AF.Exp),
                            r=["lg"], w=[("PT", a, g)])
                for h in range(8):
                    g = h // 4
                    bank = 5 + g
                    o0 = (h % 4) * 65
                    for a in whichs:
                        kp = par if a == 1 else 1 - par
                        pe(lambda e, h=h, g=g, a=a, kp=kp, bank=bank, o0=o0, st=(a == whichs[0]): e.matmul(
                            pb[bank][:, o0:o0 + 65], lhsT=PT[:, a, h, :], rhs=vaug[kp][:, g, 0:65],
                            start=st, stop=(a == 1)),
                            r=[("PT", a, g), ("va", kp)], w=["pb%d" % bank])
                for g in range(2):
                    bank = 5 + g
                    pv = pb[bank][:, 0:260].rearrange("p (h c) -> p h c", c=65)
                    dve(lambda e, g=g, pv=pv: e.tensor_tensor(out=den[:, 4 * g:4 * g + 4], in0=pv[:, :, 64],
                                                              in1=esink[:, 4 * g:4 * g + 4], op=ALU.add),
                        r=["pb%d" % bank, "esink"], w=["den"])
                    dve(lambda e, g=g: e.reciprocal(out=den[:, 4 * g:4 * g + 4], in_=den[:, 4 * g:4 * g + 4]),
                        r=["den"], w=["den"])
                    dve(lambda e, g=g, pv=pv: e.tensor_tensor(
                        out=mf[:, g * 256:(g + 1) * 256].rearrange("p (h d) -> p h d", d=64), in0=pv[:, :, 0:64],
                        in1=bc(den[:, 4 * g:4 * g + 4], 2, 64), op=ALU.mult),
                        r=["pb%d" % bank, "den"], w=["mf0"])
                z = zb[par][:]
                zp = zb[1 - par][:]
                dve(lambda e, z=z: e.tensor_tensor(out=z, in0=prsb[:, 256:512], in1=prsb[:, 512:768], op=ALU.mult),
                    r=["prsb1"], w=[("z", par)])
                for j, (cs, cp) in enumerate(((C_S1, C_S1P), (C_S2, C_S2P))):
                    pe(lambda e, j=j, cs=cs, z=z, st=(n == 0): e.matmul(pb[7][:, j * 256:(j + 1) * 256], lhsT=cst[:, cs:cs + 128], rhs=z,
                                                           start=True, stop=st),
                       r=["cst", ("z", par)], w=["pb7"])
                    if n > 0:
                        pe(lambda e, j=j, cp=cp, zp=zp: e.matmul(pb[7][:, j * 256:(j + 1) * 256], lhsT=cst[:, cp:cp + 128],
                                                                 rhs=zp, start=False, stop=True),
                           r=["cst", ("z", 1 - par)], w=["pb7"])
                dve(lambda e: e.tensor_tensor(out=c1[:, :], in0=pb[7][:, 256:512], in1=cwb[:, 0, :], op=ALU.mult),
                    r=["pb7", "cwb"], w=["c1"])
                dve(lambda e: e.tensor_tensor(out=c2[:, :], in0=pb[7][:, 0:256], in1=cwb[:, 1, :], op=ALU.mult),
                    r=["pb7", "cwb"], w=["c2"])
                dve(lambda e: e.tensor_tensor(out=c1[:, :], in0=c1[:, :], in1=c2[:, :], op=ALU.add), r=["c1", "c2"], w=["c1"])
                dve(lambda e, z=z: e.tensor_tensor(out=c2[:, :], in0=z, in1=cwb[:, 2, :], op=ALU.mult),
                    r=[("z", par), "cwb", "c1"], w=["c2"])
                dve(lambda e: e.tensor_tensor(out=c1[:, :], in0=c1[:, :], in1=c2[:, :], op=ALU.add), r=["c1", "c2"], w=["c1"])
                dve(lambda e: e.tensor_tensor(out=mf[:, 512:768], in0=c1[:, :], in1=prsb[:, 0:256], op=ALU.mult),
                    r=["c1", "prsb0"], w=["mf1"])
                for g in range(4):
                    dve(lambda e, g=g: e.bn_stats(out=bst[:, g, :], in_=prsb[:, 1024 + g * 64:1024 + (g + 1) * 64]),
                        r=["prsb2"], w=["bst"])
                for g in range(4):
                    dve(lambda e, g=g: e.bn_aggr(out=mv[:, g, :], in_=bst[:, g, :]), r=["bst"], w=["mv"])
                dve(lambda e: e.tensor_copy(out=vr[:, :], in_=mv[:, :, 1]), r=["mv"], w=["vr"])
                rstd_from_ss(vr[:, 0:4], vr[:, 0:4], 1.0, 4, "vr", "vr", vt[:, 0:4], "vt")
                dve(lambda e: e.tensor_tensor(out=vc[:, :, :], in0=prsb[:, 1024:1280].rearrange("p (g d) -> p g d", g=4),
                                              in1=bc(mv[:, :, 0], 2, 64), op=ALU.subtract), r=["prsb2", "mv"], w=["vc"])
                dve(lambda e: e.tensor_tensor(out=vn[:, :, :], in0=vc[:, :, :], in1=bc(vr[:, 0:4], 2, 64), op=ALU.mult),
                    r=["vc", "vr"], w=["vn"])
                for g in range(4):
                    pe(lambda e, g=g: e.matmul(pb[7][:, g * 64:(g + 1) * 64], lhsT=swT[:, g, :], rhs=vn[:, g, :],
                                               start=True, stop=True), r=["swT", "vn"], w=["pb7"])
                dve(lambda e: e.tensor_tensor(out=vc[:, :, :], in0=pb[7][:, 0:256].rearrange("p (g d) -> p g d", g=4),
                                              in1=bc(sgb[:], 2, 64), op=ALU.add), r=["pb7", "sgb"], w=["vc"])
                dve(lambda e: e.tensor_tensor(out=mf[:, 768:1024], in0=vc[:, :, :].rearrange("p g d -> p (g d)"),
                                              in1=prsb[:, 768:1024], op=ALU.mult), r=["vc", "prsb2"], w=["mf2"])
                for j, (a0, a1) in enumerate(((0, 512), (512, 768), (768, 1024))):
                    act(lambda e, j=j, a0=a0, a1=a1: e.activation(out=A["junkb"][:, a0:a1], in_=mf[:, a0:a1], func=AF.Square,
                                                                  accum_out=s3[:, j:j + 1]),
                        r=["mf%d" % j], w=["junkb", "s3"])
                dve(lambda e: e.tensor_tensor(out=s3[:, 0:3], in0=s3[:, 0:3], in1=cst[:, C_SEGL:C_SEGL + 3], op=ALU.mult),
                    r=["s3", "cst"], w=["s3"])
                rstd_from_ss(s3[:, 0:3], r3s[:, 0:3], 1.0, 3, "s3", "r3s", t3[:, 0:3], "t3")
                for j, (a0, a1) in enumerate(((0, 512), (512, 768), (768, 1024))):
                    act(lambda e, j=j, a0=a0, a1=a1: e.activation(out=mb[:, a0:a1], in_=mf[:, a0:a1], func=AF.Copy,
                                                                  scale=r3s[:, j:j + 1]),
                        r=["mf%d" % j, "r3s"], w=["mb"])
                for k in range(8):
                    pe(lambda e, k=k: e.transpose(out=pb0b[:, k * 128:(k + 1) * 128], in_=mb[:, k * 128:(k + 1) * 128],
                                                  identity=identb[:]), r=["mb", "identb"], w=["pb0"])
                act(lambda e: e.activation(out=mT[:].rearrange("p k t -> p (k t)"), in_=pb0b[:, :], func=AF.Copy),
                    r=["pb0"], w=["mT"])
                for cc in range(2):
                    bank = 1 + cc
                    for k in range(8):
                        pe(lambda e, cc=cc, k=k, bank=bank: e.matmul(pb[bank][:, :], lhsT=mT[:, k, :],
                                                                     rhs=w_out_sb[:, k, cc * 512:(cc + 1) * 512],
                                                                     start=(k == 0), stop=(k == 7)),
                           r=["mT", "w_out"], w=["pb%d" % bank])
                    dve(lambda e, cc=cc, bank=bank: e.tensor_tensor(out=og[:, :], in0=pb[bank][:, :],
                                                                    in1=gB[:, 0, cc * 512:(cc + 1) * 512], op=ALU.mult),
                        r=["pb%d" % bank, "gB"], w=["og"])
                    dve(lambda e, cc=cc, xt=xt: e.tensor_tensor(out=xt[:, cc * 512:(cc + 1) * 512],
                                                                in0=xt[:, cc * 512:(cc + 1) * 512], in1=og[:, :], op=ALU.add),
                        r=["og", xtag], w=[xtag])
                dma("sp", y_out[n * 128:(n + 1) * 128, :], xt, r=[xtag], w=[("yrow", n)], key=("xo", n % 2))

        def phaseB(l):
            ar.reset()
            A = {}
            A["junkb"] = ar.alloc([D], BF16)
            A["ss"] = ar.alloc([4], F32)
            A["rs"] = ar.alloc([4], F32)
            A["xn"] = ar.alloc([D], BF16)
            A["hT"] = ar.alloc([8, 128], BF16)
            h2 = ar.alloc([D], BF16)
            qTp = ar.alloc([16, 128], BF16)
            ssb = ar.alloc([16, 128], F32)
            s2 = ar.alloc([256], F32)
            tops = ar.alloc([16, 16], F32)
            topi = ar.alloc([16, 16], U32)
            topf = ar.alloc([16, 16], F32)
            cand = ar.alloc([8, 256], F32)
            best = ar.alloc([8, 16], F32)
            bpos = ar.alloc([8, 16], U32)
            ab_u = ar.alloc([2, 128], U32)
            ab_f = ar.alloc([2, 128], F32)
            eq = ar.alloc([8, 16, 16], F32)
            ij = ar.alloc([2, 128], F32)
            Ef = ar.alloc([128], F32)
            Eu = ar.alloc([128], U32)
            gd = ar.alloc([8, 16], F32)
            gs = ar.alloc([8], F32)
            gate = ar.alloc([8, 16], F32)
            actv = ar.alloc([128], F32)
            wgt = ar.alloc([128], F32)
            yacc = ar.alloc([D], F32)
            yacc1 = ar.alloc([D], F32)
            junk1 = ar.alloc([D], BF16)
            Dm = ar.alloc([3, 4, 128], BF16)
            og = ar.alloc([512], F32)
            for n in range(nt):
                xt, xtag = tile_front(n, y_out, 1, A)
                for k in range(8):
                    pe(lambda e, k=k: e.transpose(out=pb0b[:, k * 128:(k + 1) * 128], in_=A["hT"][:, k, :],
                                                  identity=identb[:]), r=["hT", "identb"], w=["pb0"])
                act(lambda e: e.activation(out=h2[:, :], in_=pb0b[:, :], func=AF.Copy), r=["pb0"], w=["h2"])
                for j in range(16):
                    bank = 1 + j // 4
                    for k in range(8):
                        pe(lambda e, j=j, k=k, bank=bank: e.matmul(pb[bank][:, (j % 4) * 128:(j % 4 + 1) * 128],
                                                                   lhsT=wq_sb[:, k, j * 128:(j + 1) * 128], rhs=A["hT"][:, k, :],
                                                                   start=(k == 0), stop=(k == 7)),
                           r=["wq", "hT"], w=["pb%d" % bank])
                for b4 in range(4):
                    act(lambda e, b4=b4: e.activation(out=qTp[:, b4 * 4:(b4 + 1) * 4, :].rearrange("p j t -> p (j t)"),
                                                      in_=pb[1 + b4][:, :], func=AF.Copy),
                        r=["pb%d" % (1 + b4)], w=[("qTp", b4)])
                for j in range(16):
                    bank = 1 + j // 4
                    pe(lambda e, j=j, bank=bank: e.matmul(pb[bank][:, (j % 4) * 128:(j % 4 + 1) * 128], lhsT=qTp[:, j, :],
                                                          rhs=skT_sb[:, j, :], start=True, stop=True),
                       r=[("qTp", j // 4), "skT"], w=["pb%d" % bank])
                for b4 in range(4):
                    act(lambda e, b4=b4: e.activation(out=ssb[:, b4 * 4:(b4 + 1) * 4, :].rearrange("p j t -> p (j t)"),
                                                      in_=pb[1 + b4][:, :], func=AF.Copy),
                        r=["pb%d" % (1 + b4)], w=[("ssb", b4)])

                def top16(src_ap, srctag, tmp_ap, vals, idxs, vtag, itag):
                    dve(lambda e: e.max(out=vals[:, 0:8], in_=src_ap), r=[srctag], w=[vtag])
                    dve(lambda e: e.max_index(out=idxs[:, 0:8], in_max=vals[:, 0:8], in_values=src_ap),
                        r=[srctag, vtag], w=[itag])
                    dve(lambda e: e.match_replace(out=tmp_ap, in_to_replace=vals[:, 0:8], in_values=src_ap, imm_value=-1e30),
                        r=[srctag, vtag], w=["s2"])
                    dve(lambda e: e.max(out=vals[:, 8:16], in_=tmp_ap), r=["s2"], w=[vtag + "b"])
                    dve(lambda e: e.max_index(out=idxs[:, 8:16], in_max=vals[:, 8:16], in_values=tmp_ap),
                        r=["s2", vtag + "b"], w=[itag + "b"])

                for j in range(16):
                    top16(ssb[:, j, :], ("ssb", j // 4), s2[:, 0:128], tops[:, j, :], topi[:, j, :], "tops", "topi")
                dve(lambda e: e.tensor_copy(out=topf[:, :, :], in_=topi[:, :, :]), r=["topi", "topib"], w=["topf"])
                tv = tops[:, :, :].rearrange("p (h c) k -> p h c k", c=2)
                tf = topf[:, :, :].rearrange("p (h c) k -> p h c k", c=2)
                dve(lambda e: e.tensor_tensor(out=cand[:, :, :].rearrange("p h (a b) -> p h a b", a=16),
                                              in0=bc(tv[:, :, 0, :], 3, 16), in1=bc(tv[:, :, 1, :], 2, 16), op=ALU.add),
                    r=["tops", "topsb"], w=["cand"])
                for h in range(8):
                    top16(cand[:, h, :], "cand", s2[:, 0:256], best[:, h, :], bpos[:, h, :], "best", "bpos")
                bp = bpos[:, :, :].rearrange("p h k -> p (h k)")
                dve(lambda e: e.tensor_single_scalar(out=ab_u[:, 0, :], in_=bp, scalar=4, op=ALU.logical_shift_right),
                    r=["bpos", "bposb"], w=["ab_u"])
                dve(lambda e: e.tensor_single_scalar(out=ab_u[:, 1, :], in_=bp, scalar=15, op=ALU.bitwise_and),
                    r=["bpos", "bposb"], w=["ab_u"])
                dve(lambda e: e.tensor_copy(out=ab_f[:, :, :], in_=ab_u[:, :, :]), r=["ab_u"], w=["ab_f"])
                iota = cst[:, C_IOTA:C_IOTA + 16]
                for c in range(2):
                    abv = ab_f[:, c, :].rearrange("p (h k) -> p h k", h=8)
                    dve(lambda e, abv=abv: e.tensor_tensor(out=eq[:, :, :, :], in0=bc(abv, 3, 16),
                                                           in1=bc(bc(iota, 1, 16), 1, 8), op=ALU.is_equal),
                        r=["ab_f", "cst", "ij"], w=["eq"])
                    dve(lambda e, c=c: e.tensor_tensor(out=eq[:, :, :, :], in0=eq[:, :, :, :], in1=bc(tf[:, :, c, :], 2, 16),
                                                       op=ALU.mult), r=["eq", "topf"], w=["eq"])
                    dve(lambda e, c=c: e.tensor_reduce(out=ij[:, c, :], in_=eq[:, :, :, :].rearrange("p h k a -> p (h k) a"),
                                                       axis=AX.X, op=ALU.add), r=["eq"], w=["ij"])
                dve(lambda e: e.scalar_tensor_tensor(out=Ef[:, :], in0=ij[:, 0, :], scalar=128.0, in1=ij[:, 1, :],
                                                     op0=ALU.mult, op1=ALU.add), r=["ij"], w=["Ef"])
                dve(lambda e: e.tensor_copy(out=Eu[:, :], in_=Ef[:, :]), r=["Ef"], w=["Eu"])
                dve(lambda e: e.tensor_tensor(out=gd[:, :, :], in0=best[:, :, :], in1=bc(best[:, :, 0], 2, 16), op=ALU.subtract),
                    r=["best", "bestb"], w=["gd"])
                act(lambda e: e.activation(out=gd[:, :, :], in_=gd[:, :, :], func=AF.Exp), r=["gd"], w=["gd"])
                dve(lambda e: e.tensor_reduce(out=gs[:, :], in_=gd[:, :, :], axis=AX.X, op=ALU.add), r=["gd"], w=["gs"])
                dve(lambda e: e.reciprocal(out=gs[:, :], in_=gs[:, :]), r=["gs"], w=["gs"])
                dve(lambda e: e.tensor_tensor(out=gate[:, :, :], in0=gd[:, :, :], in1=bc(gs[:, :], 2, 16), op=ALU.mult),
                    r=["gd", "gs"], w=["gate"])
                gflat = gate[:, :, :].rearrange("p h k -> p (h k)")

                def emit_accum(g):
                    sl4 = slice(4 * g, 4 * g + 4)
                    tg_a = ("actvg", g % 4)
                    tg_w = ("wgtg", g % 4)
                    act(lambda e, sl4=sl4: e.activation(out=wgt[:, sl4], in_=actv[:, sl4], func=AF.Gelu),
                        r=[("actv", (4 * g + j) % 16) for j in range(4)], w=[tg_w])
                    dve(lambda e, sl4=sl4: e.tensor_tensor(out=wgt[:, sl4], in0=wgt[:, sl4], in1=gflat[:, sl4], op=ALU.mult),
                        r=[tg_w, "gate"], w=[tg_w])
                    gi = g % 3
                    dve(lambda e, sl4=sl4, gi=gi: e.tensor_tensor(out=Dm[:, gi, :, :], in0=bc(identb[:], 1, 4),
                                                                  in1=bc(wgt[:, sl4], 2, 128), op=ALU.mult),
                        r=[tg_w, "identb"], w=[("Dm", gi)])
                    for m in range(4 * g, 4 * g + 4):
                        sl = m % NSLOT
                        for half in range(2):
                            pe(lambda e, m=m, sl=sl, gi=gi, half=half: e.matmul(
                                pb[5 + half][:, :], lhsT=Dm[:, gi, m % 4, :], rhs=gsl[:, sl, 1, half * 512:(half + 1) * 512],
                                start=(m == 0), stop=(m == 127)),
                                r=[("Dm", gi), ("gs", sl)], w=["pb%d" % (5 + half)])

                for g in range(32):
                    for m in range(4 * g, 4 * g + 4):
                        sl = m % NSLOT
                        P.op("pool", lambda e, m=m, sl=sl: e.indirect_dma_start(
                            out=gslf[:, sl, :], out_offset=None, in_=ptab,
                            in_offset=bass.IndirectOffsetOnAxis(ap=Eu[:, m:m + 1], axis=0)),
                            r=["Eu"], w=[("gs", sl)], dma=("gs", sl))
                    for m in range(4 * g, 4 * g + 4):
                        sl = m % NSLOT
                        jk = A["junkb"] if m % 2 == 0 else junk1
                        dve(lambda e, m=m, sl=sl, jk=jk: e.scalar_tensor_tensor(out=jk, in0=gsl[:, sl, 0, :], scalar=1.0, in1=h2[:, :],
                                                                                op0=ALU.mult, op1=ALU.mult, accum_out=actv[:, m:m + 1]),
                            r=[("gs", sl), "h2"], w=["junkb" if m % 2 == 0 else "junk1", ("actv", m % 16)])
                    if g >= 1:
                        emit_accum(g - 1)
                emit_accum(31)
                for cc in range(2):
                    dve(lambda e, cc=cc: e.tensor_tensor(out=og[:, :], in0=pb[5 + cc][:, :],
                                                         in1=gB[:, 1, cc * 512:(cc + 1) * 512], op=ALU.mult),
                        r=["pb%d" % (5 + cc), "gB"], w=["og"])
                    dve(lambda e, cc=cc, xt=xt: e.tensor_tensor(out=xt[:, cc * 512:(cc + 1) * 512],
                                                                in0=xt[:, cc * 512:(cc + 1) * 512], in1=og[:, :], op=ALU.add),
                        r=["og", xtag], w=[xtag])
                dma("sp", y_out[n * 128:(n + 1) * 128, :], xt, r=[xtag], w=[("yrow", n)], key=("xo", n % 2))

        for l in range(depth):
            P.barrier()
            layer_params(l)
            phaseA_weights(l)
            P.barrier()
            phaseA(l)
            P.barrier()
            if only_a:
                continue
            phaseB_weights(l)
            P.barrier()
            convert_tables(l)
            P.barrier()
            phaseB(l)
        P.barrier()

        sems = {}
        for e in Prog.ENG:
            sems[e] = es.enter_context(nc.semaphore("s_" + e))
        for i, key in enumerate(P.dcnt):
            sems[key] = es.enter_context(nc.semaphore("d%d" % i))
        block = es.enter_context(nc.Block())

        def make_body(ename):
            def body(eng):
                for o in P.ops[ename]:
                    if o[0] == "w":
                        eng.wait_ge(sems[o[1]], o[2])
                    else:
                        ins = o[1](eng)
                        k = o[2][0]
                        ins.then_inc(sems[k], 16 if isinstance(k, tuple) else 1)
            return body

        block.tensor(make_body("pe"))
        block.scalar(make_body("act"))
        block.vector(make_body("dve"))
        block.gpsimd(make_body("pool"))
        block.sync(make_body("sp"))
    return nc


def _bucket(dist):
    n_buckets, max_distance = 32, 128
    max_exact = n_buckets // 2
    d = np.maximum(dist, 0)
    lr = np.log(np.maximum(d, 1).astype(np.float32) / np.float32(max_exact)) / np.float32(math.log(max_distance / max_exact))
    large = max_exact + (lr.astype(np.float32) * np.float32(n_buckets - max_exact)).astype(np.int32)
    large = np.minimum(large, n_buckets - 1)
    return np.where(d < max_exact, d, large)


def _consts():
    c = np.zeros((128, NCST), np.float32)
    i = np.arange(128)
    c[:, C_ID:C_ID + 128] = np.eye(128, dtype=np.float32)
    tp, t = np.meshgrid(i, i, indexing="ij")
    c[:, C_S1:C_S1 + 128] = (tp == t - 1)
    c[:, C_S2:C_S2 + 128] = (tp == t - 2)
    c[:, C_S1P:C_S1P + 128] = (tp == t + 127)
    c[:, C_S2P:C_S2P + 128] = (tp == t + 126)
    tt, ss = np.meshgrid(i, i, indexing="ij")
    c[:, C_TRIL:C_TRIL + 128] = (ss <= tt)
    s, q = np.meshgrid(i, i, indexing="ij")
    c[:, C_MPREV:C_MPREV + 128] = np.where(s > q, 0.0, -1e30)
    c[:, C_MCUR:C_MCUR + 128] = np.where(s <= q, 0.0, -1e30)
    c[:, C_IOTA:C_IOTA + 16] = np.arange(16, dtype=np.float32)[None, :]
    c[:, C_SEGL:C_SEGL + 3] = np.array([1 / 512., 1 / 256., 1 / 256.], np.float32)[None, :]
    return c


def _bias_index():
    i = np.arange(128)
    s, q = np.meshgrid(i, i, indexing="ij")
    dprev = q + 128 - s
    dcur = q - s
    bp = np.where(s > q, _bucket(dprev), 0)
    bcur = np.where(s <= q, _bucket(dcur), 0)
    return np.stack([bp, bcur], axis=1)


_CACHE = {}


def _prep_shared(inp, depth):
    f = lambda a: np.ascontiguousarray(np.asarray(a, dtype=np.float32))
    rb = f(inp["rel_bias"])
    bidx = _bias_index()
    braw = rb[bidx]
    braw = np.ascontiguousarray(braw.transpose(0, 1, 3, 2)).reshape(128, 2 * 8 * 128)
    b_ada = f(inp["b_ada"])[:depth]
    sh = {
        "cst": _consts(),
        "braw": braw,
        "w_ada": f(inp["w_ada"])[:depth],
        "badaT": np.ascontiguousarray(b_ada.reshape(depth, 48, 128).transpose(0, 2, 1)),
        "badaG": np.ascontiguousarray(np.broadcast_to(
            np.concatenate([b_ada[:, 2048:3072], b_ada[:, 5120:6144]], axis=1)[:, None, :], (depth, 128, 2048))),
        "n1T": np.ascontiguousarray(f(inp["norm1_g"])[:depth].reshape(depth, 8, 128).transpose(0, 2, 1)),
        "n2T": np.ascontiguousarray(f(inp["norm2_g"])[:depth].reshape(depth, 8, 128).transpose(0, 2, 1)),
        "ongT": np.ascontiguousarray(f(inp["out_norm_g"])[:depth].reshape(depth, 8, 128).transpose(0, 2, 1)),
        "w_in": f(inp["w_in"])[:depth],
        "w_out": f(inp["w_out"])[:depth],
        "wq": f(inp["peer_wq"])[:depth],
        "qgB": np.ascontiguousarray(np.broadcast_to(f(inp["q_norm_g"])[:depth, None, :], (depth, 128, 64))),
        "kgB": np.ascontiguousarray(np.broadcast_to(f(inp["k_norm_g"])[:depth, None, :], (depth, 128, 64))),
        "sinkB": np.ascontiguousarray(np.broadcast_to(f(inp["attn_sink"])[:depth, None, :], (depth, 128, 8))),
        "convB": np.ascontiguousarray(np.broadcast_to(f(inp["conv_w"])[:depth].reshape(depth, 1, 768), (depth, 128, 768))),
        "sgu_w": f(inp["sgu_w"])[:depth],
        "sgubT": np.ascontiguousarray(f(inp["sgu_b"])[:depth].transpose(0, 2, 1)),
        "subk": f(inp["peer_sub_keys"])[:depth].reshape(depth, 16, 128, 128),
    }
    ne = 128 if DEBUG_SMALL_TABLES else 16384
    for i in range(depth):
        sh["pdown%d" % i] = np.ascontiguousarray(np.asarray(inp["peer_down"][i], dtype=np.float32)[:ne])
        sh["pup%d" % i] = np.ascontiguousarray(np.asarray(inp["peer_up"][i], dtype=np.float32)[:ne])
    return sh


def run(inp, n_cores=8, nt=SEQ // 128, depth=DEPTH, trace=False, only_a=False):
    key = (nt, depth, only_a)
    if key not in _CACHE:
        _CACHE[key] = build_program(nt, depth, only_a=only_a)
    nc = _CACHE[key]
    sh = _prep_shared(inp, depth)
    x = np.asarray(inp["x"], dtype=np.float32)
    c = np.asarray(inp["c"], dtype=np.float32)
    in_maps = []
    for b in range(n_cores):
        m = dict(sh)
        m["x"] = np.ascontiguousarray(x[b, :nt * 128])
        m["cT"] = np.ascontiguousarray(c[b].reshape(8, 128).T)
        in_maps.append(m)
    res = run_bass_kernel_spmd(nc, in_maps, core_ids=list(range(n_cores)), **({"trace": True} if trace else {}))
    return np.stack([r["y"] for r in res.results], axis=0), res


def kernel(**inputs):
    out, _ = run(inputs)
    return out.astype(np.float32)
```
